# Optimizing a Trainium2 kernel written in Bass

```python
import jax, jax.numpy as jnp
from jax import lax
import numpy as np

D_MODEL = 1024
BATCH = 8
SEQ = 4096
DEPTH = 1

CHUNK = 64
MIX_WIDTH = D_MODEL
POOL_WIDTH = MIX_WIDTH // 2
POOL_WINDOWS = (2, 4, 8, 16)
POOL_GROUPS = len(POOL_WINDOWS)
POOL_GROUP_DIM = POOL_WIDTH // POOL_GROUPS
HGRN_WIDTH = MIX_WIDTH - POOL_WIDTH
HGRN_EXPAND = 128
HGRN_HEADS = HGRN_WIDTH // HGRN_EXPAND
HGRN_DK = HGRN_EXPAND
HGRN_DV = HGRN_WIDTH // HGRN_HEADS
IN_COLS = POOL_WIDTH + 4 * HGRN_WIDTH
D_FF = 2816
CONV_WIDTH = 3
LN_EPS = 1e-5
RMS_EPS = 1e-6
ALPHA = (2.0 * DEPTH) ** 0.25
BETA = (8.0 * DEPTH) ** -0.25

kernel_name = "hymba_pool_hgrn2_deepnorm_block"


def layer_norm(x, g, b):
    xf = x.astype(jnp.float32)
    mu = jnp.mean(xf, axis=-1, keepdims=True)
    var = jnp.mean(jnp.square(xf - mu), axis=-1, keepdims=True)
    y = (xf - mu) * lax.rsqrt(var + LN_EPS)
    return (y * g + b).astype(x.dtype)


def pool_mixer(xp, pool_w, pool_scale):
    B, T, _ = xp.shape
    xf = xp.astype(jnp.float32)
    cs0 = jnp.pad(jnp.cumsum(xf, axis=1), ((0, 0), (1, 0), (0, 0)))
    pos = jnp.arange(1, T + 1, dtype=jnp.float32)
    outs = []
    for gi, w in enumerate(POOL_WINDOWS):
        sl = slice(gi * POOL_GROUP_DIM, (gi + 1) * POOL_GROUP_DIM)
        c = cs0[..., sl]
        lagged = jnp.pad(c, ((0, 0), (w - 1, 0), (0, 0)))[:, :T]
        mean = (c[:, 1:] - lagged) / jnp.minimum(pos, float(w))[None, :, None]
        d = (mean - xf[..., sl]).astype(xp.dtype)
        outs.append(jnp.einsum('btc,cd->btd', d, pool_w[gi]))
    return jnp.concatenate(outs, axis=-1) * pool_scale


def hgrn2_mixer(q, f_pre, i, g, lb, gnorm_w):
    B, T, _ = q.shape
    nC = T // CHUNK
    H = HGRN_HEADS
    lbf = lb.astype(jnp.float32)
    forget = lbf + (1.0 - lbf) * jax.nn.sigmoid(f_pre.astype(jnp.float32))
    log_f = jnp.log(forget)
    k = 1.0 - forget
    qf = jax.nn.silu(q.astype(jnp.float32))

    def to_chunks(t, d):
        return t.astype(jnp.float32).reshape(B, nC, CHUNK, H, d).transpose(1, 0, 3, 2, 4)

    qh, lfh, kh = to_chunks(qf, HGRN_DK), to_chunks(log_f, HGRN_DK), to_chunks(k, HGRN_DK)
    vh = to_chunks(i, HGRN_DV)
    causal = jnp.tril(jnp.ones((CHUNK, CHUNK), dtype=bool))

    def step(S, inp):
        qc, lfc, kc, vc = inp
        b = jnp.cumsum(lfc, axis=2)
        diff = b[:, :, :, None, :] - b[:, :, None, :, :]
        decay = jnp.exp(jnp.where(causal[:, :, None], diff, -jnp.inf))
        scores = jnp.einsum('bhtd,bhsd,bhtsd->bhts', qc, kc, decay)
        o = jnp.einsum('bhts,bhse->bhte', scores, vc) \
            + jnp.einsum('bhtd,bhde->bhte', qc * jnp.exp(b), S)
        b_last = b[:, :, -1:, :]
        S = jnp.exp(b_last)[:, :, 0, :, None] * S \
            + jnp.einsum('bhsd,bhse->bhde', kc * jnp.exp(b_last - b), vc)
        return S, o

    S0 = jnp.zeros((B, H, HGRN_DK, HGRN_DV), jnp.float32)
    _, o = lax.scan(step, S0, (qh, lfh, kh, vh))
    o = o.transpose(1, 0, 3, 2, 4).reshape(B, T, H, HGRN_DV)
    o = o * lax.rsqrt(jnp.mean(jnp.square(o), axis=-1, keepdims=True) + RMS_EPS) * gnorm_w
    o = o * jax.nn.silu(g.astype(jnp.float32).reshape(B, T, H, HGRN_DV))
    return o.reshape(B, T, HGRN_WIDTH).astype(q.dtype)


def conv_gated_mlp(h, w_up, conv_w, conv_b, w_down):
    u = jnp.einsum('btd,df->btf', h, w_up)
    T = u.shape[1]
    up = jnp.pad(u, ((0, 0), (CONV_WIDTH - 1, 0), (0, 0)))
    uc = conv_b + sum(conv_w[j] * up[:, j:j + T] for j in range(CONV_WIDTH))
    a, v = jnp.split(uc, 2, axis=-1)
    return jnp.einsum('btf,fd->btd', jax.nn.gelu(a) * v, w_down)


def setup_inputs(seed: int = 0) -> dict:
    key = jax.random.key(seed)
    ks = jax.random.split(key, 16)
    n = jax.random.normal
    f32 = jnp.float32
    return {
        "x": n(ks[0], (BATCH, SEQ, D_MODEL), f32),
        "w_in": n(ks[1], (DEPTH, D_MODEL, IN_COLS), f32) * D_MODEL ** -0.5,
        "pool_w": n(ks[2], (DEPTH, POOL_GROUPS, POOL_GROUP_DIM, POOL_GROUP_DIM), f32) * POOL_GROUP_DIM ** -0.5,
        "pool_scale": 1.0 + 0.02 * n(ks[3], (DEPTH, POOL_WIDTH), f32),
        "hgrn_lb": n(ks[4], (DEPTH + 1, HGRN_WIDTH), f32),
        "hgrn_gnorm": 1.0 + 0.02 * n(ks[5], (DEPTH, HGRN_DV), f32),
        "w_out": n(ks[6], (DEPTH, MIX_WIDTH, D_MODEL), f32) * (MIX_WIDTH ** -0.5 * BETA),
        "ln1_g": 1.0 + 0.02 * n(ks[7], (DEPTH, D_MODEL), f32),
        "ln1_b": 0.02 * n(ks[8], (DEPTH, D_MODEL), f32),
        "w_up": n(ks[9], (DEPTH, D_MODEL, 2 * D_FF), f32) * D_MODEL ** -0.5,
        "conv_w": n(ks[10], (DEPTH, CONV_WIDTH, 2 * D_FF), f32) * CONV_WIDTH ** -0.5,
        "conv_b": 0.02 * n(ks[11], (DEPTH, 2 * D_FF), f32),
        "w_down": n(ks[12], (DEPTH, D_FF, D_MODEL), f32) * (D_FF ** -0.5 * BETA),
        "ln2_g": 1.0 + 0.02 * n(ks[13], (DEPTH, D_MODEL), f32),
        "ln2_b": 0.02 * n(ks[14], (DEPTH, D_MODEL), f32),
    }


def reference(x, w_in, pool_w, pool_scale, hgrn_lb, hgrn_gnorm, w_out, ln1_g, ln1_b,
              w_up, conv_w, conv_b, w_down, ln2_g, ln2_b):
    lb_all = jnp.cumsum(jax.nn.softmax(hgrn_lb.astype(jnp.float32), axis=0), axis=0)
    splits = [POOL_WIDTH, POOL_WIDTH + HGRN_WIDTH, POOL_WIDTH + 2 * HGRN_WIDTH,
              POOL_WIDTH + 3 * HGRN_WIDTH]
    h = x
    for l in range(DEPTH):
        proj = jnp.einsum('btd,dc->btc', h, w_in[l])
        xp, q, f_pre, i, g = jnp.split(proj, splits, axis=-1)
        y_a = pool_mixer(xp, pool_w[l], pool_scale[l])
        y_b = hgrn2_mixer(q, f_pre, i, g, lb_all[l], hgrn_gnorm[l])
        mix = jnp.einsum('btc,cd->btd', jnp.concatenate([y_a, y_b], axis=-1), w_out[l])
        h = layer_norm(ALPHA * h + mix, ln1_g[l], ln1_b[l])
        ffn = conv_gated_mlp(h, w_up[l], conv_w[l], conv_b[l], w_down[l])
        h = layer_norm(ALPHA * h + ffn, ln2_g[l], ln2_b[l])
    return h
```

```python
import numpy as np
import concourse.bass as bass
import concourse.mybir as mybir
from concourse.bass_utils import run_bass_kernel_spmd

F32 = mybir.dt.float32
BF16 = mybir.dt.bfloat16
AF = mybir.ActivationFunctionType
ALU = mybir.AluOpType

T = 4096
D = 1024
TT = 512
NT = T // TT
DFF = 2816
NFC = DFF // 128
ALPHA = 2.0 ** 0.25
LN_EPS = 1e-5
RMS_EPS = 1e-6

C_PS = 0
C_LB0 = 4
C_LB1 = 8
C_GN = 12
C_L1G = 13
C_L1B = 21
C_L2G = 29
C_L2B = 37
C_CW0 = 45
C_CW1 = 89
C_CW2 = 133
C_CB = 177
NV = 221

ENG_ATTR = {"pe": "tensor", "act": "scalar", "dve": "vector", "pool": "gpsimd", "sp": "sync"}


class Sem:
    def __init__(self, handle):
        self.handle = handle
        self.count = 0


class Buf:
    def __init__(self, ap):
        self.ap = ap
        self.w = None
        self.r = {}

    def __getitem__(self, k):
        return self.ap[k]


class Prog:
    def __init__(self, nc):
        self.nc = nc
        self.ops = {e: [] for e in ENG_ATTR}
        self.sems = {e: Sem(nc.alloc_semaphore("sem_" + e)) for e in ENG_ATTR}
        self.waited = {e: {} for e in ENG_ATTR}

    def new_sem(self, name):
        return Sem(self.nc.alloc_semaphore(name))

    def emit(self, eng, fn, reads=(), writes=(), extra=(), sem=None, inc=1):
        deps = list(extra)
        for b in reads:
            if b.w is not None:
                deps.append(b.w)
        for b in writes:
            if b.w is not None:
                deps.append(b.w)
            deps.extend(b.r.values())
        waits = {}
        wd = self.waited[eng]
        for (s, v) in deps:
            if wd.get(s, 0) < v and waits.get(s, 0) < v:
                waits[s] = v
        for s, v in waits.items():
            wd[s] = v
        if sem is None:
            sem = self.sems[eng]
        sem.count += inc
        tok = (sem, sem.count)
        wl = [(s.handle, v) for s, v in waits.items()]
        h = sem.handle

        def run(E, wl=wl, fn=fn, h=h, inc=inc):
            for (sh, v) in wl:
                E.wait_ge(sh, v)
            fn(E).then_inc(h, inc)

        self.ops[eng].append(run)
        for b in reads:
            old = b.r.get(sem)
            if old is None or old[1] < tok[1]:
                b.r[sem] = tok
        for b in writes:
            b.w = tok
            b.r = {}
        return tok

    def wait_only(self, eng, toks):
        wl = [(s.handle, v) for (s, v) in toks]

        def run(E, wl=wl):
            for (sh, v) in wl:
                E.wait_ge(sh, v)

        self.ops[eng].append(run)


def build():
    nc = bass.Bass("TRN2", target_bir_lowering=False)
    P = Prog(nc)

    xT = nc.dram_tensor("xT", [D, T], F32, kind="ExternalInput").ap()
    w_in = nc.dram_tensor("w_in", [D, 2560], F32, kind="ExternalInput").ap()
    pool_w = nc.dram_tensor("pool_w", [4, 128, 128], F32, kind="ExternalInput").ap()
    w_out = nc.dram_tensor("w_out", [D, D], F32, kind="ExternalInput").ap()
    w_up = nc.dram_tensor("w_up", [D, 2 * DFF], F32, kind="ExternalInput").ap()
    w_down = nc.dram_tensor("w_down", [DFF, D], F32, kind="ExternalInput").ap()
    vecs = nc.dram_tensor("vecs", [128, NV], F32, kind="ExternalInput").ap()
    outT = nc.dram_tensor("outT", [D, T], F32, kind="ExternalOutput").ap()

    Win = nc.dram_tensor("Win_s", [5, 128, 8, 512], BF16).ap()
    Wout = nc.dram_tensor("Wout_s", [2, 128, 8, 512], BF16).ap()
    Wup = nc.dram_tensor("Wup_s", [11, 128, 8, 512], BF16).ap()
    Wdn = nc.dram_tensor("Wdn_s", [4, 128, 22, 256], BF16).ap()

    def sb(name, shape, dt=F32):
        return Buf(nc.alloc_sbuf_tensor(name, list(shape), dt).ap())

    V = sb("V", [128, NV])
    LBV = sb("LBV", [128, 4])
    OML = sb("OML", [128, 4])
    LBD = sb("LBD", [128, 4])
    PSC = sb("PSC", [128, 4])
    POOLW = sb("POOLW", [128, 4, 128], BF16)
    IDENT = sb("IDENT", [128, 128], BF16)
    IDF = sb("IDF", [128, 128])
    CMASK = sb("CMASK", [128, 128])
    MSK = sb("MSK", [128, TT])
    ONES_R = sb("ONES_R", [128, 128], BF16)
    ONES_L = sb("ONES_L", [128, 128], BF16)
    WC = sb("WC", [128, 4, 16])

    XF = [sb("XF%d" % i, [128, 8, TT]) for i in range(2)]
    ABH = [sb("AB%d" % i, [128, 4, TT], BF16) for i in range(2)]

    def ab(k, lo=0, hi=TT):
        return ABH[k // 4].ap[:, k % 4, lo:hi]
    RING_E = 5632
    NR = 3
    RING = [sb("RING%d" % i, [128, RING_E], BF16) for i in range(NR)]
    RSEM = [P.new_sem("rsem%d" % i) for i in range(NR)]
    XSEM = [P.new_sem("xsem%d" % i) for i in range(2)]
    OSEM = [P.new_sem("osem%d" % i) for i in range(2)]

    XPG = sb("XPG", [128, 16 + TT])
    XW = sb("XW", [128, TT])
    PA = sb("PA", [128, 16 + TT])
    PB = sb("PB", [128, 16 + TT])
    PTMP = sb("PTMP", [128, 16])
    PHALO = [sb("PHALO%d" % g, [128, 16]) for g in range(4)]
    DT = sb("DT", [128, TT], BF16)
    YT = [sb("YT%d" % k, [128, TT], BF16) for k in range(8)]

    QS = sb("QS", [128, TT])
    SG = sb("SG", [128, TT])
    GS = sb("GS", [128, TT])
    KK = sb("KK", [128, TT])
    LF = sb("LF", [128, TT])
    BB = sb("BB", [128, TT])
    EE = sb("EE", [128, TT])
    EN = sb("EN", [128, TT])
    EM = sb("EM", [128, 8])
    EL = sb("EL", [128, 8])
    QT = sb("QT", [128, TT], BF16)
    KTB = sb("KTB", [128, TT], BF16)
    KH = sb("KH", [128, TT], BF16)
    VT = sb("VT", [128, 4, 512], BF16)
    AT = sb("AT", [128, 4, 128], BF16)
    KHE = sb("KHE", [128, 4, 128], BF16)
    KHO = sb("KHO", [128, 4, 128], BF16)
    S32 = [sb("S32_%d" % h, [128, 128]) for h in range(4)]
    ST = sb("ST", [128, 8, 128], BF16)
    O32 = sb("O32", [128, TT])
    OSQ = sb("OSQ", [128, TT], BF16)
    RS = sb("RS", [128, TT])
    T2 = sb("T2", [128, TT])

    ZB = [sb("ZB%d" % i, [128, TT], BF16) for i in range(2)]
    ZSQ = [sb("ZSQ%d" % i, [128, TT], BF16) for i in range(2)]
    M2 = sb("M2", [128, TT])
    VAR = sb("VAR", [128, TT])
    LT = [sb("LT%d" % i, [128, TT]) for i in range(2)]
    H1F = sb("H1F", [128, 8, TT])

    ACA = [sb("ACA%d" % i, [128, TT]) for i in range(2)]
    ACV = [sb("ACV%d" % i, [128, TT]) for i in range(2)]
    GA = [sb("GA%d" % i, [128, TT]) for i in range(2)]
    GT = [sb("GT%d" % k, [128, TT], BF16) for k in range(NFC)]
    CHALO = sb("CHALO", [128, 2 * NFC, 2])

    PSG = [Buf(nc.alloc_psum_tensor("psg%d" % i, [128, 512], F32).ap()) for i in range(5)]
    PSL = [Buf(nc.alloc_psum_tensor("psl%d" % i, [128, 512], F32).ap()) for i in range(2)]
    PST = Buf(nc.alloc_psum_tensor("pst", [128, 1024], BF16).ap())
    ps_rr = [0]

    def next_ps():
        b = PSG[ps_rr[0] % 5]
        ps_rr[0] += 1
        return b

    WIN_B = Buf(Win)
    WOUT_B = Buf(Wout)
    WUP_B = Buf(Wup)
    WDN_B = Buf(Wdn)

    cast_sems = {k: P.new_sem("cs_" + k) for k in ("in", "out", "up", "dn", "misc")}

    cast_jobs = []

    def cast(dst, src_, key):
        cast_jobs.append((dst, src_, cast_sems[key]))
        cast_sems[key].count += 16

    cast(V.ap, vecs, "misc")
    cast(POOLW.ap, pool_w.rearrange("g c d -> c g d"), "misc")
    win_r = w_in.rearrange("(k p) n -> p k n", p=128)
    cast(Win[0], win_r[:, :, 1536:2048], "in")
    for h in range(4):
        for j, base in enumerate((0, 512, 1024, 2048)):
            cast(Win[1 + h][:, :, j * 128:(j + 1) * 128],
                 win_r[:, :, base + h * 128: base + (h + 1) * 128], "in")
    wout_r = w_out.rearrange("(k p) n -> p k n", p=128)
    for b in range(2):
        cast(Wout[b], wout_r[:, :, b * 512:(b + 1) * 512], "out")
    wup_r = w_up.rearrange("(k p) n -> p k n", p=128)
    for b in range(11):
        cast(Wup[b][:, :, 0:256], wup_r[:, :, b * 256:(b + 1) * 256], "up")
        cast(Wup[b][:, :, 256:512], wup_r[:, :, DFF + b * 256: DFF + (b + 1) * 256], "up")
    wdn_r = w_down.rearrange("(k p) n -> p k n", p=128)
    for b in range(4):
        cast(Wdn[b], wdn_r[:, :, b * 256:(b + 1) * 256], "dn")

    def pro_pool(E):
        for (dst, src_, s) in cast_jobs:
            E.dma_start(out=dst, in_=src_).then_inc(s.handle, 16)
        E.memset(IDF.ap, 0.0)
        E.affine_select(out=IDF.ap, in_=IDF.ap, pattern=[[-1, 128]], compare_op=ALU.not_equal,
                        fill=1.0, base=0, channel_multiplier=1)
        E.memset(CMASK.ap, 1.0)
        E.affine_select(out=CMASK.ap, in_=CMASK.ap, pattern=[[1, 128]], compare_op=ALU.is_ge,
                        fill=0.0, base=0, channel_multiplier=-1)
        E.memset(CMASK.ap[0:64, 64:128], 0.0)
        E.memset(MSK.ap, 1.0)
        E.memset(MSK.ap.rearrange("p (c t) -> p c t", t=64)[:, :, 0:1], 0.0)
        E.memset(ONES_R.ap, 1.0 / 128.0)
        E.memset(ONES_L.ap, 1.0 / 1024.0)
        for g in range(4):
            w = 2 ** (g + 1)
            E.memset(WC.ap[:, g, :], 1.0)
            for t in range(w - 1):
                E.memset(WC.ap[:, g, t:t + 1], float(w) / float(t + 1))
            E.memset(PHALO[g].ap, 0.0)
        for h in range(4):
            E.memset(S32[h].ap, 0.0)
        E.memset(KHE.ap, 0.0)
        E.memset(KHO.ap, 0.0)
        E.memset(CHALO.ap, 0.0)
        return E.memset(PTMP.ap, 0.0)

    pro_tok = P.emit("pool", pro_pool)
    for b in (IDF, CMASK, MSK, ONES_R, ONES_L, WC, KHE, KHO, CHALO, PTMP, *PHALO, *S32):
        b.w = pro_tok
    misc_tok = (cast_sems["misc"], cast_sems["misc"].count)
    V.w = misc_tok
    POOLW.w = misc_tok
    WIN_B.w = (cast_sems["in"], cast_sems["in"].count)
    WOUT_B.w = (cast_sems["out"], cast_sems["out"].count)
    WUP_B.w = (cast_sems["up"], cast_sems["up"].count)
    WDN_B.w = (cast_sems["dn"], cast_sems["dn"].count)

    P.emit("dve", lambda E: E.tensor_copy(out=IDENT.ap, in_=IDF.ap), reads=[IDF], writes=[IDENT])
    P.emit("dve", lambda E: E.tensor_tensor(out=LBD.ap, in0=V[:, C_LB0:C_LB0 + 4], in1=V[:, C_LB1:C_LB1 + 4],
                                            op=ALU.subtract), reads=[V], writes=[LBD])
    P.emit("act", lambda E: E.activation(out=LBV.ap, in_=LBD.ap, func=AF.Sigmoid), reads=[LBD], writes=[LBV])
    P.emit("act", lambda E: E.activation(out=OML.ap, in_=LBD.ap, func=AF.Sigmoid, scale=-1.0),
           reads=[LBD], writes=[OML])
    for g in range(4):
        P.emit("dve", lambda E, g=g: E.tensor_scalar(out=PSC[:, g:g + 1], in0=V[:, C_PS + g:C_PS + g + 1],
                                                      scalar1=1.0 / (2 ** (g + 1)), scalar2=None, op0=ALU.mult),
               reads=[V], writes=[PSC])

    blocks = []
    for t in range(NT):
        for b in range(5):
            blocks.append((WIN_B, Win[b].rearrange("p k n -> p (k n)"), 4096))
        for b in range(2):
            blocks.append((WOUT_B, Wout[b].rearrange("p k n -> p (k n)"), 4096))
        for b in range(11):
            blocks.append((WUP_B, Wup[b].rearrange("p k n -> p (k n)"), 4096))
        for b in range(4):
            blocks.append((WDN_B, Wdn[b].rearrange("p k n -> p (k n)"), 5632))
    ring_state = {"next_load": 0, "next_use": 0}

    def ring_load():
        i = ring_state["next_load"]
        if i >= len(blocks):
            return
        ring_state["next_load"] += 1
        src_b, src, n = blocks[i]
        slot = i % NR
        P.emit("sp", lambda E: E.dma_start(out=RING[slot].ap[:, 0:n], in_=src),
               reads=[src_b], writes=[RING[slot]], sem=RSEM[slot], inc=16)

    def ring_take():
        i = ring_state["next_use"]
        ring_state["next_use"] += 1
        return RING[i % NR]

    def x_load(t):
        b = t % 2
        P.emit("sp", lambda E: E.dma_start(out=XF[b].ap,
                                           in_=xT[:, t * TT:(t + 1) * TT].rearrange("(k p) n -> p k n", p=128)),
               writes=[XF[b]], sem=XSEM[b], inc=16)

    for _ in range(NR):
        ring_load()
    x_load(0)

    def mm_group(out_ap, pairs, reads, writes):
        def fn(E):
            n = len(pairs)
            inst = None
            for i, (l, r) in enumerate(pairs):
                inst = E.matmul(out_ap, lhsT=l, rhs=r, start=(i == 0), stop=(i == n - 1))
            return inst
        return P.emit("pe", fn, reads=reads, writes=writes)

    def c3(ap):
        return ap.rearrange("p (c t) -> p c t", t=64)

    def vcol(c):
        return V.ap[:, c:c + 1]

    def ln_stat(j, zb, zs):
        P.emit("pe", lambda E: E.matmul(PSL[0].ap, lhsT=ONES_L.ap, rhs=zb.ap, start=(j == 0), stop=(j == 7)),
               reads=[zb, ONES_L], writes=[PSL[0]] if j == 0 else [])
        tok = P.emit("pe", lambda E: E.matmul(PSL[1].ap, lhsT=ONES_L.ap, rhs=zs.ap, start=(j == 0), stop=(j == 7)),
                     reads=[zs, ONES_L], writes=[PSL[1]] if j == 0 else [])
        if j == 7:
            PSL[0].w = tok
            PSL[1].w = tok

    def ln_chunk_in(res, j, ps, zb, zs):
        P.emit("dve", lambda E: E.scalar_tensor_tensor(
            out=res.ap[:, j, :], in0=res.ap[:, j, :], scalar=ALPHA, in1=ps.ap, op0=ALU.mult, op1=ALU.add),
            reads=[ps, res], writes=[res])
        P.emit("act", lambda E: E.activation(out=zb.ap, in_=res.ap[:, j, :], func=AF.Copy), reads=[res], writes=[zb])
        P.emit("act", lambda E: E.activation(out=zs.ap, in_=res.ap[:, j, :], func=AF.Square), reads=[res], writes=[zs])

    def ln_chunk_out(res, j, lt, gcol, bcol, out_f32, out_bf16_fn):
        P.emit("dve", lambda E: E.tensor_tensor(out=lt.ap, in0=res.ap[:, j, :], in1=PSL[0].ap, op=ALU.subtract),
               reads=[res, PSL[0]], writes=[lt])
        P.emit("dve", lambda E: E.tensor_tensor(out=lt.ap, in0=lt.ap, in1=PSL[1].ap, op=ALU.mult),
               reads=[lt, PSL[1]], writes=[lt])
        P.emit("act", lambda E: E.activation(out=out_f32.ap[:, j, :], in_=lt.ap, func=AF.Identity,
                                             scale=vcol(gcol + j), bias=vcol(bcol + j)),
               reads=[lt, V], writes=[out_f32])
        if out_bf16_fn is not None:
            ob, oap = out_bf16_fn(j)
            P.emit("act", lambda E: E.activation(out=oap, in_=lt.ap, func=AF.Identity,
                                                 scale=vcol(gcol + j), bias=vcol(bcol + j)),
                   reads=[lt, V], writes=[ob])

    def layer_norm(res, mm_for_chunk, gcol, bcol, out_f32, out_bf16_fn):
        pend = None
        for j in range(8):
            ps = mm_for_chunk(j)
            ln_chunk_in(res, j, ps, ZB[j % 2], ZSQ[j % 2])
            if pend is not None:
                ln_stat(*pend)
            pend = (j, ZB[j % 2], ZSQ[j % 2])
        ln_stat(*pend)
        P.emit("act", lambda E: E.activation(out=M2.ap, in_=PSL[0].ap, func=AF.Square), reads=[PSL[0]], writes=[M2])
        P.emit("dve", lambda E: E.tensor_tensor(out=VAR.ap, in0=PSL[1].ap, in1=M2.ap, op=ALU.subtract),
               reads=[PSL[1], M2], writes=[VAR])
        P.emit("dve", lambda E: E.tensor_scalar(out=VAR.ap, in0=VAR.ap, scalar1=LN_EPS, scalar2=None, op0=ALU.add),
               reads=[VAR], writes=[VAR])
        P.emit("act", lambda E: E.activation(out=VAR.ap, in_=VAR.ap, func=AF.Ln), reads=[VAR], writes=[VAR])
        P.emit("act", lambda E: E.activation(out=PSL[1].ap, in_=VAR.ap, func=AF.Exp, scale=-0.5),
               reads=[VAR], writes=[PSL[1]])
        for j in range(8):
            ln_chunk_out(res, j, LT[j % 2], gcol, bcol, out_f32, out_bf16_fn)

    def x_to_bf16(xf):
        P.emit("dve", lambda E: E.tensor_copy(out=ABH[0].ap, in_=xf.ap[:, 0:4, :]), reads=[xf], writes=[ABH[0]])
        P.emit("act", lambda E: E.activation(out=ABH[1].ap, in_=xf.ap[:, 4:8, :], func=AF.Copy),
               reads=[xf], writes=[ABH[1]])

    def proj_v_block(tb, w3, rb):
        ps = next_ps()
        mm_group(ps.ap, [(ab(k, tb * 128, (tb + 1) * 128), w3[:, k, :]) for k in range(8)],
                 reads=[ABH[0], ABH[1], rb], writes=[ps])
        P.emit("act", lambda E: E.activation(out=VT.ap[:, tb, :], in_=ps.ap, func=AF.Copy), reads=[ps], writes=[VT])

    def pool_level(dst, src, lo, sh):
        P.emit("pool", lambda E: E.tensor_tensor(
            out=dst.ap[:, lo:16 + TT], in0=src.ap[:, lo:16 + TT], in1=src.ap[:, lo - sh:16 + TT - sh], op=ALU.add),
            reads=[src], writes=[dst])

    def scan_step(h, c):
        P.emit("dve", lambda E: E.tensor_scalar(out=ST.ap[:, c, :], in0=S32[h].ap, scalar1=EM.ap[:, c:c + 1],
                                                scalar2=None, op0=ALU.mult),
               reads=[S32[h], EM], writes=[ST])
        P.emit("dve", lambda E: E.scalar_tensor_tensor(
            out=S32[h].ap, in0=S32[h].ap, scalar=EL.ap[:, c:c + 1],
            in1=PSL[c // 4].ap[:, (c % 4) * 128:(c % 4 + 1) * 128], op0=ALU.mult, op1=ALU.add),
            reads=[S32[h], EL, PSL[c // 4]], writes=[S32[h]])

    def head_body(t, h):
        rb = ring_take()
        w3 = rb.ap[:, 0:4096].rearrange("p (k n) -> p k n", n=512)
        pss = []
        for j in range(4):
            ps = next_ps()
            mm_group(ps.ap, [(w3[:, k, j * 128:(j + 1) * 128], ab(k)) for k in range(8)],
                     reads=[ABH[0], ABH[1], rb], writes=[ps])
            pss.append(ps)
        ring_load()
        ps_xp, ps_q, ps_f, ps_g = pss
        wv = float(2 ** (h + 1))
        g = h
        P.emit("act", lambda E: E.activation(out=SG.ap, in_=ps_f.ap, func=AF.Sigmoid), reads=[ps_f], writes=[SG])
        P.emit("act", lambda E: E.activation(out=QS.ap, in_=ps_q.ap, func=AF.Silu), reads=[ps_q], writes=[QS])
        P.emit("act", lambda E: E.activation(out=GS.ap, in_=ps_g.ap, func=AF.Silu), reads=[ps_g], writes=[GS])
        P.emit("act", lambda E: E.activation(out=XPG.ap[:, 16:16 + TT], in_=ps_xp.ap, func=AF.Copy),
               reads=[ps_xp], writes=[XPG])
        P.emit("act", lambda E: E.activation(out=XW.ap, in_=ps_xp.ap, func=AF.Copy, scale=-wv),
               reads=[ps_xp], writes=[XW])

        P.emit("pool", lambda E: E.tensor_copy(out=XPG.ap[:, 0:16], in_=PHALO[g].ap), reads=[PHALO[g]], writes=[XPG])
        src = XPG
        bufs = [PA, PB]
        sh = 1
        lo = 0
        for lvl in range(g + 1):
            dst = bufs[lvl % 2]
            lo = lo + sh
            pool_level(dst, src, lo, sh)
            src = dst
            sh *= 2
        ssum = src
        P.emit("pool", lambda E: E.tensor_tensor(out=DT.ap, in0=ssum.ap[:, 16:16 + TT], in1=XW.ap, op=ALU.add),
               reads=[ssum, XW], writes=[DT])
        if t == 0:
            P.emit("pool", lambda E: E.tensor_tensor(out=PTMP.ap, in0=ssum.ap[:, 16:32], in1=WC.ap[:, g, :], op=ALU.mult),
                   reads=[ssum, WC], writes=[PTMP])
            P.emit("pool", lambda E: E.tensor_tensor(out=DT.ap[:, 0:16], in0=PTMP.ap, in1=XW.ap[:, 0:16], op=ALU.add),
                   reads=[PTMP, XW], writes=[DT])
        P.emit("pool", lambda E: E.tensor_copy(out=PHALO[g].ap, in_=XPG.ap[:, TT:TT + 16]), reads=[XPG], writes=[PHALO[g]])
        psy = next_ps()
        mm_group(psy.ap, [(POOLW.ap[:, g, :], DT.ap)], reads=[POOLW, DT], writes=[psy])
        P.emit("act", lambda E: E.activation(out=YT[g].ap, in_=psy.ap, func=AF.Identity, scale=PSC.ap[:, g:g + 1]),
               reads=[psy, PSC], writes=[YT[g]])

        P.emit("dve", lambda E: E.tensor_scalar(out=SG.ap, in0=SG.ap, scalar1=OML.ap[:, h:h + 1],
                                                scalar2=LBV.ap[:, h:h + 1], op0=ALU.mult, op1=ALU.add),
               reads=[SG, OML, LBV], writes=[SG])
        P.emit("act", lambda E: E.activation(out=LF.ap, in_=SG.ap, func=AF.Ln), reads=[SG], writes=[LF])
        P.emit("dve", lambda E: E.tensor_scalar(out=KK.ap, in0=SG.ap, scalar1=-1.0, scalar2=1.0,
                                                op0=ALU.mult, op1=ALU.add), reads=[SG], writes=[KK])
        P.emit("dve", lambda E: E.tensor_tensor_scan(out=BB.ap, data0=MSK.ap, data1=LF.ap, initial=0.0,
                                                     op0=ALU.mult, op1=ALU.add), reads=[MSK, LF], writes=[BB])
        P.emit("dve", lambda E: E.tensor_tensor(out=c3(LF.ap), in0=c3(BB.ap),
                                                in1=c3(BB.ap)[:, :, 31:32].to_broadcast([128, 8, 64]),
                                                op=ALU.subtract), reads=[BB], writes=[LF])
        P.emit("act", lambda E: E.activation(out=EE.ap, in_=LF.ap, func=AF.Exp), reads=[LF], writes=[EE])
        P.emit("act", lambda E: E.activation(out=EN.ap, in_=LF.ap, func=AF.Exp, scale=-1.0), reads=[LF], writes=[EN])
        P.emit("act", lambda E: E.activation(out=EM.ap, in_=c3(BB.ap)[:, :, 31], func=AF.Exp), reads=[BB], writes=[EM])
        P.emit("act", lambda E: E.activation(out=EL.ap, in_=c3(BB.ap)[:, :, 63], func=AF.Exp), reads=[BB], writes=[EL])
        P.emit("dve", lambda E: E.tensor_tensor(out=QT.ap, in0=QS.ap, in1=EE.ap, op=ALU.mult),
               reads=[QS, EE], writes=[QT])
        P.emit("dve", lambda E: E.tensor_tensor(out=KK.ap, in0=KK.ap, in1=EN.ap, op=ALU.mult),
               reads=[KK, EN], writes=[KK])
        P.emit("act", lambda E: E.activation(out=KTB.ap, in_=KK.ap, func=AF.Copy), reads=[KK], writes=[KTB])
        P.emit("dve", lambda E: E.tensor_tensor(out=c3(KH.ap), in0=c3(KK.ap),
                                                in1=c3(EE.ap)[:, :, 63:64].to_broadcast([128, 8, 64]),
                                                op=ALU.mult), reads=[KK, EE], writes=[KH])
        psa = next_ps()

        def fn_at(E):
            inst = None
            for tb in range(4):
                inst = E.matmul(psa.ap[:, tb * 128:(tb + 1) * 128], lhsT=KTB.ap[:, tb * 128:(tb + 1) * 128],
                                rhs=QT.ap[:, tb * 128:(tb + 1) * 128], start=True, stop=True)
            return inst
        P.emit("pe", fn_at, reads=[KTB, QT], writes=[psa])
        P.emit("dve", lambda E: E.tensor_tensor(
            out=AT.ap, in0=psa.ap.rearrange("p (b t) -> p b t", t=128),
            in1=CMASK.ap.unsqueeze(1).to_broadcast([128, 4, 128]), op=ALU.mult),
            reads=[psa, CMASK], writes=[AT])

        def fn_tr(E):
            inst = None
            for tb in range(4):
                inst = E.transpose(out=PST.ap[:, tb * 128:(tb + 1) * 128], in_=KH.ap[:, tb * 128:(tb + 1) * 128],
                                   identity=IDENT.ap)
            return inst
        P.emit("pe", fn_tr, reads=[KH, IDENT], writes=[PST])
        P.emit("act", lambda E: E.activation(out=KHE.ap[0:64].rearrange("p b d -> p (b d)"), in_=PST.ap[0:64, 0:512],
                                             func=AF.Copy), reads=[PST], writes=[KHE])
        P.emit("act", lambda E: E.activation(out=KHO.ap[64:128].rearrange("p b d -> p (b d)"),
                                             in_=PST.ap[64:128, 0:512], func=AF.Copy), reads=[PST], writes=[KHO])

        def fn_ds(E):
            inst = None
            for c in range(8):
                tb = c // 2
                kh = KHE if c % 2 == 0 else KHO
                inst = E.matmul(PSL[c // 4].ap[:, (c % 4) * 128:(c % 4 + 1) * 128], lhsT=kh.ap[:, tb, :],
                                rhs=VT.ap[:, tb, h * 128:(h + 1) * 128], start=True, stop=True)
            return inst
        P.emit("pe", fn_ds, reads=[KHE, KHO, VT], writes=[PSL[0], PSL[1]])
        for c in range(8):
            scan_step(h, c)
        pso = next_ps()

        def fn_o(E):
            inst = None
            for tb in range(4):
                E.matmul(pso.ap[:, tb * 128:(tb + 1) * 128], lhsT=VT.ap[:, tb, h * 128:(h + 1) * 128],
                         rhs=AT.ap[:, tb, :], start=True, stop=False)
                for c in (2 * tb, 2 * tb + 1):
                    inst = E.matmul(pso.ap[:, c * 64:(c + 1) * 64], lhsT=ST.ap[:, c, :],
                                    rhs=QT.ap[:, c * 64:(c + 1) * 64], start=False, stop=(c == 2 * tb + 1))
            return inst
        P.emit("pe", fn_o, reads=[VT, AT, ST, QT], writes=[pso])
        P.emit("act", lambda E: E.activation(out=O32.ap, in_=pso.ap, func=AF.Copy), reads=[pso], writes=[O32])
        P.emit("act", lambda E: E.activation(out=OSQ.ap, in_=pso.ap, func=AF.Square), reads=[pso], writes=[OSQ])
        psm = next_ps()
        mm_group(psm.ap, [(ONES_R.ap, OSQ.ap)], reads=[ONES_R, OSQ], writes=[psm])
        P.emit("dve", lambda E: E.tensor_scalar(out=RS.ap, in0=psm.ap, scalar1=RMS_EPS, scalar2=None,
                                                op0=ALU.add), reads=[psm], writes=[RS])
        P.emit("act", lambda E: E.activation(out=RS.ap, in_=RS.ap, func=AF.Ln), reads=[RS], writes=[RS])
        P.emit("act", lambda E: E.activation(out=RS.ap, in_=RS.ap, func=AF.Exp, scale=-0.5), reads=[RS], writes=[RS])
        P.emit("dve", lambda E: E.tensor_tensor(out=T2.ap, in0=O32.ap, in1=RS.ap, op=ALU.mult),
               reads=[O32, RS], writes=[T2])
        P.emit("dve", lambda E: E.scalar_tensor_tensor(out=YT[4 + h].ap, in0=T2.ap, scalar=vcol(C_GN),
                                                       in1=GS.ap, op0=ALU.mult, op1=ALU.mult),
               reads=[T2, V, GS], writes=[YT[4 + h]])

    def conv_evac(t, ps, acc, ci):
        P.emit("act", lambda E: E.activation(out=acc.ap, in_=ps.ap, func=AF.Identity, scale=vcol(C_CW2 + ci),
                                             bias=vcol(C_CB + ci)), reads=[ps, V], writes=[acc])
        P.emit("dve", lambda E: E.scalar_tensor_tensor(
            out=acc.ap[:, 1:TT], in0=ps.ap[:, 0:TT - 1], scalar=vcol(C_CW1 + ci),
            in1=acc.ap[:, 1:TT], op0=ALU.mult, op1=ALU.add), reads=[ps, acc, V], writes=[acc])
        P.emit("dve", lambda E: E.scalar_tensor_tensor(
            out=acc.ap[:, 2:TT], in0=ps.ap[:, 0:TT - 2], scalar=vcol(C_CW0 + ci),
            in1=acc.ap[:, 2:TT], op0=ALU.mult, op1=ALU.add), reads=[ps, acc, V], writes=[acc])
        if t > 0:
            P.emit("dve", lambda E: E.scalar_tensor_tensor(
                out=acc.ap[:, 0:1], in0=CHALO.ap[:, ci, 1:2], scalar=vcol(C_CW1 + ci),
                in1=acc.ap[:, 0:1], op0=ALU.mult, op1=ALU.add), reads=[CHALO, acc, V], writes=[acc])
            P.emit("dve", lambda E: E.scalar_tensor_tensor(
                out=acc.ap[:, 0:2], in0=CHALO.ap[:, ci, 0:2], scalar=vcol(C_CW0 + ci),
                in1=acc.ap[:, 0:2], op0=ALU.mult, op1=ALU.add), reads=[CHALO, acc, V], writes=[acc])
        if t + 1 < NT:
            P.emit("act", lambda E: E.activation(out=CHALO.ap[:, ci, :], in_=ps.ap[:, TT - 2:TT], func=AF.Copy),
                   reads=[ps], writes=[CHALO])

    def ffn_pair(t, j, jj, w3, rb, last):
        psa = next_ps()
        mm_group(psa.ap, [(w3[:, k, jj * 128:(jj + 1) * 128], ab(k)) for k in range(8)],
                 reads=[ABH[0], ABH[1], rb], writes=[psa])
        psv = next_ps()
        mm_group(psv.ap, [(w3[:, k, 256 + jj * 128:256 + (jj + 1) * 128], ab(k)) for k in range(8)],
                 reads=[ABH[0], ABH[1], rb], writes=[psv])
        if last:
            ring_load()
        aca, acv, ga = ACA[j % 2], ACV[j % 2], GA[j % 2]
        conv_evac(t, psa, aca, j)
        conv_evac(t, psv, acv, NFC + j)
        P.emit("act", lambda E: E.activation(out=ga.ap, in_=aca.ap, func=AF.Gelu_apprx_tanh), reads=[aca], writes=[ga])
        P.emit("pool", lambda E: E.tensor_tensor(out=GT[j].ap, in0=ga.ap, in1=acv.ap, op=ALU.mult),
               reads=[ga, acv], writes=[GT[j]])

    def tile_body(t):
        xf = XF[t % 2]
        x_to_bf16(xf)
        rb = ring_take()
        w3 = rb.ap[:, 0:4096].rearrange("p (k n) -> p k n", n=512)
        for tb in range(4):
            proj_v_block(tb, w3, rb)
        ring_load()
        if t + 1 < NT:
            x_load(t + 1)
        for h in range(4):
            head_body(t, h)

        wo_rb = [None, None]

        def mix_chunk(j):
            if j % 4 == 0:
                wo_rb[j // 4] = ring_take()
            rb_ = wo_rb[j // 4]
            w3_ = rb_.ap[:, 0:4096].rearrange("p (k n) -> p k n", n=512)
            ps = next_ps()
            mm_group(ps.ap, [(w3_[:, k, (j % 4) * 128:(j % 4 + 1) * 128], YT[k].ap) for k in range(8)],
                     reads=[rb_] + YT, writes=[ps])
            if j % 4 == 3:
                ring_load()
            return ps
        layer_norm(xf, mix_chunk, C_L1G, C_L1B, H1F, lambda j: (ABH[j // 4], ab(j)))

        for b in range(11):
            rb = ring_take()
            w3 = rb.ap[:, 0:4096].rearrange("p (k n) -> p k n", n=512)
            for jj in range(2):
                ffn_pair(t, 2 * b + jj, jj, w3, rb, jj == 1)

        wd_rb = [None] * 4

        def down_chunk(j):
            if j % 2 == 0:
                wd_rb[j // 2] = ring_take()
            rb_ = wd_rb[j // 2]
            w3_ = rb_.ap[:, 0:5632].rearrange("p (k n) -> p k n", n=256)
            ps = next_ps()
            mm_group(ps.ap, [(w3_[:, k, (j % 2) * 128:(j % 2 + 1) * 128], GT[k].ap) for k in range(NFC)],
                     reads=[rb_] + GT, writes=[ps])
            if j % 2 == 1:
                ring_load()
            return ps
        layer_norm(H1F, down_chunk, C_L2G, C_L2B, xf, None)
        P.emit("sp", lambda E: E.dma_start(
            out=outT[:, t * TT:(t + 1) * TT].rearrange("(k p) n -> p k n", p=128), in_=xf.ap),
            reads=[xf], sem=OSEM[t % 2], inc=16)

    for t in range(NT):
        tile_body(t)

    P.wait_only("sp", [(OSEM[i], OSEM[i].count) for i in range(2)])

    with nc.Block() as block:
        for ename, attr in ENG_ATTR.items():
            ops = P.ops[ename]

            def body(E, ops=ops):
                for f in ops:
                    f(E)
            getattr(block, attr)(body)
    return nc


_NC_CACHE = {}


def _pack_vecs(pool_scale, hgrn_lb, hgrn_gnorm, ln1_g, ln1_b, ln2_g, ln2_b, conv_w, conv_b):
    cols = []
    cols.append(np.asarray(pool_scale)[0].reshape(4, 128).T)
    cols.append(np.asarray(hgrn_lb)[0].reshape(4, 128).T)
    cols.append(np.asarray(hgrn_lb)[1].reshape(4, 128).T)
    cols.append(np.asarray(hgrn_gnorm)[0].reshape(1, 128).T)
    for v in (ln1_g, ln1_b, ln2_g, ln2_b):
        cols.append(np.asarray(v)[0].reshape(8, 128).T)
    cw = np.asarray(conv_w)[0]
    for j in range(3):
        cols.append(cw[j].reshape(2 * NFC, 128).T)
    cols.append(np.asarray(conv_b)[0].reshape(2 * NFC, 128).T)
    v = np.ascontiguousarray(np.concatenate(cols, axis=1).astype(np.float32))
    assert v.shape == (128, NV), v.shape
    return v


def kernel(x, w_in, pool_w, pool_scale, hgrn_lb, hgrn_gnorm, w_out, ln1_g, ln1_b,
           w_up, conv_w, conv_b, w_down, ln2_g, ln2_b):
    x = np.asarray(x, dtype=np.float32)
    if "nc" not in _NC_CACHE:
        _NC_CACHE["nc"] = build()
    nc = _NC_CACHE["nc"]
    vecs = _pack_vecs(pool_scale, hgrn_lb, hgrn_gnorm, ln1_g, ln1_b, ln2_g, ln2_b, conv_w, conv_b)
    shared = {
        "w_in": np.ascontiguousarray(np.asarray(w_in, np.float32)[0]),
        "pool_w": np.ascontiguousarray(np.asarray(pool_w, np.float32)[0]),
        "w_out": np.ascontiguousarray(np.asarray(w_out, np.float32)[0]),
        "w_up": np.ascontiguousarray(np.asarray(w_up, np.float32)[0]),
        "w_down": np.ascontiguousarray(np.asarray(w_down, np.float32)[0]),
        "vecs": vecs,
    }
    in_maps = []
    for b in range(8):
        m = dict(shared)
        m["xT"] = np.ascontiguousarray(x[b].T)
        in_maps.append(m)
    res = run_bass_kernel_spmd(nc, in_maps, core_ids=list(range(8)))
    out = np.empty((8, T, D), dtype=np.float32)
    for b in range(8):
        out[b] = res.results[b]["outT"].T
    return out
```

```python
import numpy as np
import concourse.bass as bass
import concourse.mybir as mybir
from concourse.bass_utils import run_bass_kernel_spmd

F32 = mybir.dt.float32
BF16 = mybir.dt.bfloat16
AF = mybir.ActivationFunctionType
ALU = mybir.AluOpType

T = 4096
D = 1024
TT = 512
NT = T // TT
DFF = 2816
NFC = DFF // 128
ALPHA = 2.0 ** 0.25
LN_EPS = 1e-5
RMS_EPS = 1e-6

C_PS = 0
C_LB0 = 4
C_LB1 = 8
C_GN = 12
C_L1G = 13
C_L1B = 21
C_L2G = 29
C_L2B = 37
C_CW0 = 45
C_CW1 = 89
C_CW2 = 133
C_CB = 177
NV = 221

ENG_ATTR = {"pe": "tensor", "act": "scalar", "dve": "vector", "pool": "gpsimd", "sp": "sync"}


class Sem:
    def __init__(self, handle):
        self.handle = handle
        self.count = 0


class Buf:
    def __init__(self, ap):
        self.ap = ap
        self.w = None
        self.r = {}

    def __getitem__(self, k):
        return self.ap[k]


class Prog:
    def __init__(self, nc):
        self.nc = nc
        self.ops = {e: [] for e in ENG_ATTR}
        self.sems = {e: Sem(nc.alloc_semaphore("sem_" + e)) for e in ENG_ATTR}
        self.waited = {e: {} for e in ENG_ATTR}

    def new_sem(self, name):
        return Sem(self.nc.alloc_semaphore(name))

    def emit(self, eng, fn, reads=(), writes=(), extra=(), sem=None, inc=1):
        deps = list(extra)
        for b in reads:
            if b.w is not None:
                deps.append(b.w)
        for b in writes:
            if b.w is not None:
                deps.append(b.w)
            deps.extend(b.r.values())
        waits = {}
        wd = self.waited[eng]
        for (s, v) in deps:
            if wd.get(s, 0) < v and waits.get(s, 0) < v:
                waits[s] = v
        for s, v in waits.items():
            wd[s] = v
        if sem is None:
            sem = self.sems[eng]
        sem.count += inc
        tok = (sem, sem.count)
        wl = [(s.handle, v) for s, v in waits.items()]
        h = sem.handle

        def run(E, wl=wl, fn=fn, h=h, inc=inc):
            for (sh, v) in wl:
                E.wait_ge(sh, v)
            fn(E).then_inc(h, inc)

        self.ops[eng].append(run)
        for b in reads:
            old = b.r.get(sem)
            if old is None or old[1] < tok[1]:
                b.r[sem] = tok
        for b in writes:
            b.w = tok
            b.r = {}
        return tok

    def wait_only(self, eng, toks):
        wl = [(s.handle, v) for (s, v) in toks]

        def run(E, wl=wl):
            for (sh, v) in wl:
                E.wait_ge(sh, v)

        self.ops[eng].append(run)


DEBUG = False


def _emit_program(nc, order):
    P = Prog(nc)
    record = order is None
    take_log = []

    xT = nc.dram_tensor("xT", [D, T], F32, kind="ExternalInput").ap()
    w_in = nc.dram_tensor("w_in", [D, 2560], F32, kind="ExternalInput").ap()
    pool_w = nc.dram_tensor("pool_w", [4, 128, 128], F32, kind="ExternalInput").ap()
    w_out = nc.dram_tensor("w_out", [D, D], F32, kind="ExternalInput").ap()
    w_up = nc.dram_tensor("w_up", [D, 2 * DFF], F32, kind="ExternalInput").ap()
    w_down = nc.dram_tensor("w_down", [DFF, D], F32, kind="ExternalInput").ap()
    vecs = nc.dram_tensor("vecs", [128, NV], F32, kind="ExternalInput").ap()
    outT = nc.dram_tensor("outT", [D, T], F32, kind="ExternalOutput").ap()
    dbg = nc.dram_tensor("dbg", [4, D, TT], F32, kind="ExternalOutput").ap() if DEBUG else None
    DSEM = P.new_sem("dsem") if DEBUG else None

    Win = nc.dram_tensor("Win_s", [5, 128, 8, 512], BF16).ap()
    Wout = nc.dram_tensor("Wout_s", [2, 128, 8, 512], BF16).ap()
    Wup = nc.dram_tensor("Wup_s", [11, 128, 8, 512], BF16).ap()
    Wdn = nc.dram_tensor("Wdn_s", [8, 128, 22, 128], BF16).ap()

    def sb(name, shape, dt=F32):
        return Buf(nc.alloc_sbuf_tensor(name, list(shape), dt).ap())

    def chunks(name, n, dt=F32):
        full = nc.alloc_sbuf_tensor(name, [128, n, TT], dt).ap()
        return full, [Buf(full[:, j, :]) for j in range(n)]

    V = sb("V", [128, NV])
    LBV = sb("LBV", [128, 4])
    OML = sb("OML", [128, 4])
    LBD = sb("LBD", [128, 4])
    PSC = sb("PSC", [128, 4])
    POOLW = sb("POOLW", [128, 4, 128], BF16)
    IDENT = sb("IDENT", [128, 128], BF16)
    IDF = sb("IDF", [128, 128])
    CMASK = sb("CMASK", [128, 128])
    MSK = sb("MSK", [128, TT])
    ONES_R = sb("ONES_R", [128, 128], BF16)
    ONES_L = sb("ONES_L", [128, 128], BF16)
    WC = sb("WC", [128, 4, 16])

    XF_full, XFc = [], []
    for i in range(2):
        f, c = chunks("XF%d" % i, 8)
        XF_full.append(f)
        XFc.append(c)
    XB_full, XBc = chunks("XB", 8, BF16)
    H1B_full, H1Bc = chunks("H1B", 8, BF16)
    H1F_full, H1Fc = chunks("H1F", 8)
    RING_E = 4096
    NR = 4
    RING = [sb("RING%d" % i, [128, RING_E], BF16) for i in range(NR)]
    RSEM = [P.new_sem("rsem%d" % i) for i in range(NR)]
    XSEM = [P.new_sem("xsem%d" % i) for i in range(2)]
    XBSEM = P.new_sem("xbsem")
    OSEM = [P.new_sem("osem%d" % i) for i in range(2)]

    XPG = sb("XPG", [128, 16 + TT])
    XW = sb("XW", [128, TT])
    PA = sb("PA", [128, 16 + TT])
    PB = sb("PB", [128, 16 + TT])
    PTMP = sb("PTMP", [128, 16])
    PHALO = [sb("PHALO%d" % g, [128, 16]) for g in range(4)]
    DT = sb("DT", [128, TT], BF16)
    YT = [sb("YT%d" % k, [128, TT], BF16) for k in range(8)]

    QS = sb("QS", [128, TT])
    SG = sb("SG", [128, TT])
    GS = sb("GS", [128, TT])
    KK = sb("KK", [128, TT])
    LF = sb("LF", [128, TT])
    BB = sb("BB", [128, TT])
    EE = sb("EE", [128, TT])
    EN = sb("EN", [128, TT])
    O32, RS, T2 = LF, BB, EN
    EM = sb("EM", [128, 8])
    EL = sb("EL", [128, 8])
    QT = sb("QT", [128, TT], BF16)
    KTB = sb("KTB", [128, TT], BF16)
    KH = sb("KH", [128, TT], BF16)
    VT = sb("VT", [128, 4, 512], BF16)
    AT = sb("AT", [128, 4, 128], BF16)
    KHE = sb("KHE", [128, 4, 128], BF16)
    KHO = sb("KHO", [128, 4, 128], BF16)
    S32 = [sb("S32_%d" % h, [128, 128]) for h in range(4)]
    ST = sb("ST", [128, 8, 128], BF16)
    OSQ = sb("OSQ", [128, TT], BF16)

    ZB = [sb("ZB%d" % i, [128, TT], BF16) for i in range(2)]
    ZSQ = [sb("ZSQ%d" % i, [128, TT], BF16) for i in range(2)]
    LT = [sb("LT%d" % i, [128, TT]) for i in range(2)]
    M2, VAR = LT[0], LT[1]

    ACA = [sb("ACA%d" % i, [128, TT]) for i in range(2)]
    ACV = [sb("ACV%d" % i, [128, TT]) for i in range(2)]
    GA = [sb("GA%d" % i, [128, TT]) for i in range(2)]
    GT = [sb("GT%d" % k, [128, TT], BF16) for k in range(NFC)]
    CHALO = [sb("CHALO%d" % i, [128, 2]) for i in range(2 * NFC)]

    PSG = [Buf(nc.alloc_psum_tensor("psg%d" % i, [128, 512], F32).ap()) for i in range(5)]
    PSL = [Buf(nc.alloc_psum_tensor("psl%d" % i, [128, 512], F32).ap()) for i in range(2)]
    PST = Buf(nc.alloc_psum_tensor("pst", [128, 1024], BF16).ap())
    ps_rr = [0]

    def next_ps():
        b = PSG[ps_rr[0] % 5]
        ps_rr[0] += 1
        return b

    WSRC = {"in": Buf(Win), "out": Buf(Wout), "up": Buf(Wup), "dn": Buf(Wdn)}
    WAP = {"in": Win, "out": Wout, "up": Wup, "dn": Wdn}
    WN = {"in": 4096, "out": 4096, "up": 4096, "dn": 2816}
    cast_sems = {k: P.new_sem("cs_" + k) for k in ("in", "out", "up", "dn", "v", "pw")}

    def cast(dst, src_, key, bufs=()):
        s = cast_sems[key]
        tok = P.emit("pool", lambda E: E.dma_start(out=dst, in_=src_), sem=s, inc=16)
        for b in bufs:
            b.w = tok
        if key in WSRC:
            WSRC[key].w = tok
        return tok

    def pmemset(buf, ap, val):
        P.emit("pool", lambda E: E.memset(ap, val), writes=[buf])

    cast(V.ap, vecs, "v", [V])
    cast(POOLW.ap, pool_w.rearrange("g c d -> c g d"), "pw", [POOLW])
    win_r = w_in.rearrange("(k p) n -> p k n", p=128)
    cast(Win[0], win_r[:, :, 1536:2048], "in")
    for h in range(4):
        for j, base in enumerate((0, 512, 1024, 2048)):
            cast(Win[1 + h][:, :, j * 128:(j + 1) * 128],
                 win_r[:, :, base + h * 128: base + (h + 1) * 128], "in")
    wout_r = w_out.rearrange("(k p) n -> p k n", p=128)
    wup_r = w_up.rearrange("(k p) n -> p k n", p=128)
    wdn_r = w_down.rearrange("(k p) n -> p k n", p=128)
    late_casts = []
    for b in range(2):
        late_casts.append((Wout[b], wout_r[:, :, b * 512:(b + 1) * 512], "out"))
    for b in range(11):
        late_casts.append((Wup[b][:, :, 0:256], wup_r[:, :, b * 256:(b + 1) * 256], "up"))
        late_casts.append((Wup[b][:, :, 256:512], wup_r[:, :, DFF + b * 256: DFF + (b + 1) * 256], "up"))
    for b in range(8):
        late_casts.append((Wdn[b], wdn_r[:, :, b * 128:(b + 1) * 128], "dn"))

    def issue_late_casts(n):
        for _ in range(n):
            if late_casts:
                cast(*late_casts.pop(0))

    pmemset(IDF, IDF.ap, 0.0)
    P.emit("pool", lambda E: E.affine_select(out=IDF.ap, in_=IDF.ap, pattern=[[-1, 128]], compare_op=ALU.not_equal,
                                             fill=1.0, base=0, channel_multiplier=1), reads=[IDF], writes=[IDF])
    pmemset(CMASK, CMASK.ap, 1.0)
    P.emit("pool", lambda E: E.affine_select(out=CMASK.ap, in_=CMASK.ap, pattern=[[1, 128]], compare_op=ALU.is_ge,
                                             fill=0.0, base=0, channel_multiplier=-1), reads=[CMASK], writes=[CMASK])
    pmemset(CMASK, CMASK.ap[0:64, 64:128], 0.0)
    pmemset(MSK, MSK.ap, 1.0)
    pmemset(MSK, MSK.ap.rearrange("p (c t) -> p c t", t=64)[:, :, 0:1], 0.0)
    pmemset(ONES_R, ONES_R.ap, 1.0 / 128.0)
    pmemset(ONES_L, ONES_L.ap, 1.0 / 1024.0)
    for g in range(4):
        w = 2 ** (g + 1)
        pmemset(WC, WC.ap[:, g, :], 1.0)
        for t_ in range(w - 1):
            pmemset(WC, WC.ap[:, g, t_:t_ + 1], float(w) / float(t_ + 1))
        pmemset(PHALO[g], PHALO[g].ap, 0.0)
    for h in range(4):
        pmemset(S32[h], S32[h].ap, 0.0)
    pmemset(KHE, KHE.ap, 0.0)
    pmemset(KHO, KHO.ap, 0.0)
    pmemset(PTMP, PTMP.ap, 0.0)
    issue_late_casts(2)

    P.emit("dve", lambda E: E.tensor_copy(out=IDENT.ap, in_=IDF.ap), reads=[IDF], writes=[IDENT])
    P.emit("dve", lambda E: E.tensor_tensor(out=LBD.ap, in0=V.ap[:, C_LB0:C_LB0 + 4], in1=V.ap[:, C_LB1:C_LB1 + 4],
                                            op=ALU.subtract), reads=[V], writes=[LBD])
    P.emit("act", lambda E: E.activation(out=LBV.ap, in_=LBD.ap, func=AF.Sigmoid), reads=[LBD], writes=[LBV])
    P.emit("act", lambda E: E.activation(out=OML.ap, in_=LBD.ap, func=AF.Sigmoid, scale=-1.0),
           reads=[LBD], writes=[OML])

    def psc_op(g):
        P.emit("dve", lambda E: E.tensor_scalar(out=PSC.ap[:, g:g + 1], in0=V.ap[:, C_PS + g:C_PS + g + 1],
                                                scalar1=1.0 / (2 ** (g + 1)), scalar2=None, op0=ALU.mult),
               reads=[V], writes=[PSC])
    for g in range(4):
        psc_op(g)

    ring_state = {"next_load": 0, "next_use": 0}
    slot_of = {}

    def _load_into(slot):
        i = ring_state["next_load"]
        if i >= len(order):
            return
        ring_state["next_load"] += 1
        key, b = order[i]
        assert not any(c[2] == key for c in late_casts), "ring load emitted before all casts of %s" % key
        n = WN[key]
        src = WAP[key][b].rearrange("p k n -> p (k n)")
        slot_of[i] = slot
        P.emit("sp", lambda E: E.dma_start(out=RING[slot].ap[:, 0:n], in_=src),
               reads=[WSRC[key]], writes=[RING[slot]], sem=RSEM[slot], inc=16)

    def ring_init():
        if record:
            return
        for s in range(NR):
            _load_into(s)

    def ring_release(rb):
        if record:
            return
        _load_into(RING.index(rb))

    def ring_take(key, b):
        i = ring_state["next_use"]
        ring_state["next_use"] += 1
        if record:
            take_log.append((key, b))
            return RING[0]
        assert order[i] == (key, b), (i, order[i], key, b)
        return RING[slot_of[i]]

    def x_load(t):
        b = t % 2
        P.emit("sp", lambda E: E.dma_start(out=XF_full[b],
                                           in_=xT[:, t * TT:(t + 1) * TT].rearrange("(k p) n -> p k n", p=128)),
               writes=XFc[b], sem=XSEM[b], inc=16)

    def xb_load(t):
        P.emit("pool", lambda E: E.dma_start(out=XB_full,
                                             in_=xT[:, t * TT:(t + 1) * TT].rearrange("(k p) n -> p k n", p=128)),
               writes=XBc, sem=XBSEM, inc=16)

    def mm_group(out_ap, pairs, reads, writes):
        def fn(E):
            n = len(pairs)
            inst = None
            for i, (l, r) in enumerate(pairs):
                inst = E.matmul(out_ap, lhsT=l, rhs=r, start=(i == 0), stop=(i == n - 1))
            return inst
        return P.emit("pe", fn, reads=reads, writes=writes)

    def c3(ap):
        return ap.rearrange("p (c t) -> p c t", t=64)

    def vcol(c):
        return V.ap[:, c:c + 1]

    def ln_phase1(src_c, dst_c, j, ps):
        P.emit("dve", lambda E: E.scalar_tensor_tensor(
            out=dst_c[j].ap, in0=src_c[j].ap, scalar=ALPHA, in1=ps.ap, op0=ALU.mult, op1=ALU.add),
            reads=[ps, src_c[j]], writes=[dst_c[j]])

    def ln_stat(j, zb, zs):
        P.emit("pe", lambda E: E.matmul(PSL[0].ap, lhsT=ONES_L.ap, rhs=zb.ap, start=(j == 0), stop=(j == 7)),
               reads=[zb, ONES_L], writes=[PSL[0]] if j == 0 else [])
        tok = P.emit("pe", lambda E: E.matmul(PSL[1].ap, lhsT=ONES_L.ap, rhs=zs.ap, start=(j == 0), stop=(j == 7)),
                     reads=[zs, ONES_L], writes=[PSL[1]] if j == 0 else [])
        if j == 7:
            PSL[0].w = tok
            PSL[1].w = tok

    def ln_phase2_chunk(z_c, j):
        zb, zs = ZB[j % 2], ZSQ[j % 2]
        P.emit("act", lambda E: E.activation(out=zb.ap, in_=z_c[j].ap, func=AF.Copy), reads=[z_c[j]], writes=[zb])
        P.emit("act", lambda E: E.activation(out=zs.ap, in_=z_c[j].ap, func=AF.Square), reads=[z_c[j]], writes=[zs])
        ln_stat(j, zb, zs)

    def ln_rstd():
        P.emit("act", lambda E: E.activation(out=M2.ap, in_=PSL[0].ap, func=AF.Square), reads=[PSL[0]], writes=[M2])
        P.emit("dve", lambda E: E.tensor_tensor(out=VAR.ap, in0=PSL[1].ap, in1=M2.ap, op=ALU.subtract),
               reads=[PSL[1], M2], writes=[VAR])
        P.emit("dve", lambda E: E.tensor_scalar(out=VAR.ap, in0=VAR.ap, scalar1=LN_EPS, scalar2=None, op0=ALU.add),
               reads=[VAR], writes=[VAR])
        P.emit("act", lambda E: E.activation(out=VAR.ap, in_=VAR.ap, func=AF.Ln), reads=[VAR], writes=[VAR])
        P.emit("act", lambda E: E.activation(out=PSL[1].ap, in_=VAR.ap, func=AF.Exp, scale=-0.5),
               reads=[VAR], writes=[PSL[1]])

    def ln_out_chunk(z_c, j, gcol, bcol, outs):
        lt = LT[j % 2]
        P.emit("dve", lambda E: E.tensor_tensor(out=lt.ap, in0=z_c[j].ap, in1=PSL[0].ap, op=ALU.subtract),
               reads=[z_c[j], PSL[0]], writes=[lt])
        P.emit("dve", lambda E: E.tensor_tensor(out=lt.ap, in0=lt.ap, in1=PSL[1].ap, op=ALU.mult),
               reads=[lt, PSL[1]], writes=[lt])
        for (ob, oap) in outs:
            P.emit("act", lambda E, oap=oap: E.activation(out=oap, in_=lt.ap, func=AF.Identity,
                                                          scale=vcol(gcol + j), bias=vcol(bcol + j)),
                   reads=[lt, V], writes=[ob])

    def ln_finish_gen(z_c, gcol, bcol, outs_fn, after_chunk=None):
        for j in range(8):
            ln_phase2_chunk(z_c, j)
            yield
        ln_rstd()
        yield
        for j in range(8):
            ln_out_chunk(z_c, j, gcol, bcol, outs_fn(j))
            if after_chunk is not None:
                after_chunk(j)
            yield

    def proj_v_block(tb, w3, rb):
        ps = next_ps()
        mm_group(ps.ap, [(XBc[k].ap[:, tb * 128:(tb + 1) * 128], w3[:, k, :]) for k in range(8)],
                 reads=XBc + [rb], writes=[ps])
        P.emit("act", lambda E: E.activation(out=VT.ap[:, tb, :], in_=ps.ap, func=AF.Copy), reads=[ps], writes=[VT])

    def pool_level(dst, src, lo, sh):
        P.emit("pool", lambda E: E.tensor_tensor(
            out=dst.ap[:, lo:16 + TT], in0=src.ap[:, lo:16 + TT], in1=src.ap[:, lo - sh:16 + TT - sh], op=ALU.add),
            reads=[src], writes=[dst])

    def scan_step(h, c):
        P.emit("dve", lambda E: E.tensor_scalar(out=ST.ap[:, c, :], in0=S32[h].ap, scalar1=EM.ap[:, c:c + 1],
                                                scalar2=None, op0=ALU.mult),
               reads=[S32[h], EM], writes=[ST])
        P.emit("dve", lambda E: E.scalar_tensor_tensor(
            out=S32[h].ap, in0=S32[h].ap, scalar=EL.ap[:, c:c + 1],
            in1=PSL[c // 4].ap[:, (c % 4) * 128:(c % 4 + 1) * 128], op0=ALU.mult, op1=ALU.add),
            reads=[S32[h], EL, PSL[c // 4]], writes=[S32[h]])

    def proj_group(w3, rb, j):
        ps = next_ps()
        mm_group(ps.ap, [(w3[:, k, j * 128:(j + 1) * 128], XBc[k].ap) for k in range(8)],
                 reads=XBc + [rb], writes=[ps])
        return ps

    def head_gen(t, h):
        rb = ring_take("in", 1 + h)
        w3 = rb.ap[:, 0:4096].rearrange("p (k n) -> p k n", n=512)
        wv = float(2 ** (h + 1))
        g = h
        ps_f = proj_group(w3, rb, 2)
        P.emit("act", lambda E: E.activation(out=SG.ap, in_=ps_f.ap, func=AF.Sigmoid), reads=[ps_f], writes=[SG])
        yield
        ps_q = proj_group(w3, rb, 1)
        P.emit("act", lambda E: E.activation(out=QS.ap, in_=ps_q.ap, func=AF.Silu), reads=[ps_q], writes=[QS])
        yield
        ps_xp = proj_group(w3, rb, 0)
        P.emit("act", lambda E: E.activation(out=XPG.ap[:, 16:16 + TT], in_=ps_xp.ap, func=AF.Copy),
               reads=[ps_xp], writes=[XPG])
        P.emit("act", lambda E: E.activation(out=XW.ap, in_=ps_xp.ap, func=AF.Copy, scale=-wv),
               reads=[ps_xp], writes=[XW])
        yield
        ps_g = proj_group(w3, rb, 3)
        ring_release(rb)
        P.emit("act", lambda E: E.activation(out=GS.ap, in_=ps_g.ap, func=AF.Silu), reads=[ps_g], writes=[GS])
        yield

        P.emit("dve", lambda E: E.tensor_scalar(out=SG.ap, in0=SG.ap, scalar1=OML.ap[:, h:h + 1],
                                                scalar2=LBV.ap[:, h:h + 1], op0=ALU.mult, op1=ALU.add),
               reads=[SG, OML, LBV], writes=[SG])
        P.emit("act", lambda E: E.activation(out=LF.ap, in_=SG.ap, func=AF.Ln), reads=[SG], writes=[LF])
        P.emit("dve", lambda E: E.tensor_scalar(out=KK.ap, in0=SG.ap, scalar1=-1.0, scalar2=1.0,
                                                op0=ALU.mult, op1=ALU.add), reads=[SG], writes=[KK])
        yield
        P.emit("dve", lambda E: E.tensor_tensor_scan(out=BB.ap, data0=MSK.ap, data1=LF.ap, initial=0.0,
                                                     op0=ALU.mult, op1=ALU.add), reads=[MSK, LF], writes=[BB])
        P.emit("dve", lambda E: E.tensor_tensor(out=c3(LF.ap), in0=c3(BB.ap),
                                                in1=c3(BB.ap)[:, :, 31:32].to_broadcast([128, 8, 64]),
                                                op=ALU.subtract), reads=[BB], writes=[LF])
        yield
        P.emit("act", lambda E: E.activation(out=EE.ap, in_=LF.ap, func=AF.Exp), reads=[LF], writes=[EE])
        P.emit("act", lambda E: E.activation(out=EN.ap, in_=LF.ap, func=AF.Exp, scale=-1.0), reads=[LF], writes=[EN])
        P.emit("act", lambda E: E.activation(out=EM.ap, in_=c3(BB.ap)[:, :, 31], func=AF.Exp), reads=[BB], writes=[EM])
        P.emit("act", lambda E: E.activation(out=EL.ap, in_=c3(BB.ap)[:, :, 63], func=AF.Exp), reads=[BB], writes=[EL])
        yield
        P.emit("pool", lambda E: E.tensor_tensor(out=QT.ap, in0=QS.ap, in1=EE.ap, op=ALU.mult),
               reads=[QS, EE], writes=[QT])
        P.emit("dve", lambda E: E.tensor_tensor(out=KK.ap, in0=KK.ap, in1=EN.ap, op=ALU.mult),
               reads=[KK, EN], writes=[KK])
        P.emit("act", lambda E: E.activation(out=KTB.ap, in_=KK.ap, func=AF.Copy), reads=[KK], writes=[KTB])
        P.emit("dve", lambda E: E.tensor_tensor(out=c3(KH.ap), in0=c3(KK.ap),
                                                in1=c3(EE.ap)[:, :, 63:64].to_broadcast([128, 8, 64]),
                                                op=ALU.mult), reads=[KK, EE], writes=[KH])
        yield

        P.emit("pool", lambda E: E.tensor_copy(out=XPG.ap[:, 0:16], in_=PHALO[g].ap), reads=[PHALO[g]], writes=[XPG])
        src = XPG
        bufs = [PA, PB]
        sh = 1
        lo = 0
        for lvl in range(g + 1):
            dst = bufs[lvl % 2]
            lo = lo + sh
            pool_level(dst, src, lo, sh)
            src = dst
            sh *= 2
        ssum = src
        P.emit("pool", lambda E: E.tensor_tensor(out=DT.ap, in0=ssum.ap[:, 16:16 + TT], in1=XW.ap, op=ALU.add),
               reads=[ssum, XW], writes=[DT])
        if t == 0:
            P.emit("pool", lambda E: E.tensor_tensor(out=PTMP.ap, in0=ssum.ap[:, 16:32], in1=WC.ap[:, g, :], op=ALU.mult),
                   reads=[ssum, WC], writes=[PTMP])
            P.emit("pool", lambda E: E.tensor_tensor(out=DT.ap[:, 0:16], in0=PTMP.ap, in1=XW.ap[:, 0:16], op=ALU.add),
                   reads=[PTMP, XW], writes=[DT])
        P.emit("pool", lambda E: E.tensor_copy(out=PHALO[g].ap, in_=XPG.ap[:, TT:TT + 16]), reads=[XPG], writes=[PHALO[g]])
        yield

        psa = next_ps()

        def fn_at(E):
            inst = None
            for tb in range(4):
                inst = E.matmul(psa.ap[:, tb * 128:(tb + 1) * 128], lhsT=KTB.ap[:, tb * 128:(tb + 1) * 128],
                                rhs=QT.ap[:, tb * 128:(tb + 1) * 128], start=True, stop=True)
            return inst
        P.emit("pe", fn_at, reads=[KTB, QT], writes=[psa])
        P.emit("dve", lambda E: E.tensor_tensor(
            out=AT.ap, in0=psa.ap.rearrange("p (b t) -> p b t", t=128),
            in1=CMASK.ap.unsqueeze(1).to_broadcast([128, 4, 128]), op=ALU.mult),
            reads=[psa, CMASK], writes=[AT])

        def fn_tr(E):
            inst = None
            for tb in range(4):
                inst = E.transpose(out=PST.ap[:, tb * 128:(tb + 1) * 128], in_=KH.ap[:, tb * 128:(tb + 1) * 128],
                                   identity=IDENT.ap)
            return inst
        P.emit("pe", fn_tr, reads=[KH, IDENT], writes=[PST])
        P.emit("act", lambda E: E.activation(out=KHE.ap[0:64].rearrange("p b d -> p (b d)"), in_=PST.ap[0:64, 0:512],
                                             func=AF.Copy), reads=[PST], writes=[KHE])
        P.emit("act", lambda E: E.activation(out=KHO.ap[64:128].rearrange("p b d -> p (b d)"),
                                             in_=PST.ap[64:128, 0:512], func=AF.Copy), reads=[PST], writes=[KHO])
        psy = next_ps()
        mm_group(psy.ap, [(POOLW.ap[:, g, :], DT.ap)], reads=[POOLW, DT], writes=[psy])
        P.emit("act", lambda E: E.activation(out=YT[g].ap, in_=psy.ap, func=AF.Identity, scale=PSC.ap[:, g:g + 1]),
               reads=[psy, PSC], writes=[YT[g]])
        yield

        def fn_ds(E):
            inst = None
            for c in range(8):
                tb = c // 2
                kh = KHE if c % 2 == 0 else KHO
                inst = E.matmul(PSL[c // 4].ap[:, (c % 4) * 128:(c % 4 + 1) * 128], lhsT=kh.ap[:, tb, :],
                                rhs=VT.ap[:, tb, h * 128:(h + 1) * 128], start=True, stop=True)
            return inst
        P.emit("pe", fn_ds, reads=[KHE, KHO, VT], writes=[PSL[0], PSL[1]])
        yield
        for c in range(8):
            scan_step(h, c)
            if c % 4 == 3:
                yield
        pso = next_ps()

        def fn_o(E):
            inst = None
            for tb in range(4):
                E.matmul(pso.ap[:, tb * 128:(tb + 1) * 128], lhsT=VT.ap[:, tb, h * 128:(h + 1) * 128],
                         rhs=AT.ap[:, tb, :], start=True, stop=False)
                for c in (2 * tb, 2 * tb + 1):
                    inst = E.matmul(pso.ap[:, c * 64:(c + 1) * 64], lhsT=ST.ap[:, c, :],
                                    rhs=QT.ap[:, c * 64:(c + 1) * 64], start=False, stop=(c == 2 * tb + 1))
            return inst
        P.emit("pe", fn_o, reads=[VT, AT, ST, QT], writes=[pso])
        P.emit("act", lambda E: E.activation(out=O32.ap, in_=pso.ap, func=AF.Copy), reads=[pso], writes=[O32])
        P.emit("act", lambda E: E.activation(out=OSQ.ap, in_=pso.ap, func=AF.Square), reads=[pso], writes=[OSQ])
        yield
        psm = next_ps()
        mm_group(psm.ap, [(ONES_R.ap, OSQ.ap)], reads=[ONES_R, OSQ], writes=[psm])
        P.emit("dve", lambda E: E.tensor_scalar(out=RS.ap, in0=psm.ap, scalar1=RMS_EPS, scalar2=None,
                                                op0=ALU.add), reads=[psm], writes=[RS])
        yield
        P.emit("act", lambda E: E.activation(out=RS.ap, in_=RS.ap, func=AF.Ln), reads=[RS], writes=[RS])
        P.emit("act", lambda E: E.activation(out=RS.ap, in_=RS.ap, func=AF.Exp, scale=-0.5), reads=[RS], writes=[RS])
        yield
        P.emit("pool", lambda E: E.tensor_tensor(out=T2.ap, in0=O32.ap, in1=RS.ap, op=ALU.mult),
               reads=[O32, RS], writes=[T2])
        P.emit("dve", lambda E: E.scalar_tensor_tensor(out=YT[4 + h].ap, in0=T2.ap, scalar=vcol(C_GN),
                                                       in1=GS.ap, op0=ALU.mult, op1=ALU.mult),
               reads=[T2, V, GS], writes=[YT[4 + h]])
        yield

    def x_to_bf16(t):
        xc = XFc[t % 2]
        P.emit("dve", lambda E: E.tensor_copy(out=XB_full[:, 0:4, :], in_=XF_full[t % 2][:, 0:4, :]),
               reads=xc[0:4], writes=XBc[0:4])
        P.emit("act", lambda E: E.activation(out=XB_full[:, 4:8, :], in_=XF_full[t % 2][:, 4:8, :], func=AF.Copy),
               reads=xc[4:8], writes=XBc[4:8])

    def mixer_gen(t):
        x_to_bf16(t)
        yield
        rb = ring_take("in", 0)
        w3 = rb.ap[:, 0:4096].rearrange("p (k n) -> p k n", n=512)
        for tb in range(4):
            proj_v_block(tb, w3, rb)
            if tb == 3:
                ring_release(rb)
            yield
        for h in range(4):
            yield from head_gen(t, h)
            if t == 0 and h < 2:
                issue_late_casts(15)

    def mix_gen(t):
        xc = XFc[t % 2]
        for b in range(2):
            rb = ring_take("out", b)
            w3 = rb.ap[:, 0:4096].rearrange("p (k n) -> p k n", n=512)
            for jj in range(4):
                j = 4 * b + jj
                ps = next_ps()
                mm_group(ps.ap, [(w3[:, k, jj * 128:(jj + 1) * 128], YT[k].ap) for k in range(8)],
                         reads=[rb] + YT, writes=[ps])
                if jj == 3:
                    ring_release(rb)
                ln_phase1(xc, xc, j, ps)
                yield

    def conv_evac(t, ps, acc, ci):
        ch = CHALO[ci]
        P.emit("act", lambda E: E.activation(out=acc.ap, in_=ps.ap, func=AF.Identity, scale=vcol(C_CW2 + ci),
                                             bias=vcol(C_CB + ci)), reads=[ps, V], writes=[acc])
        P.emit("dve", lambda E: E.scalar_tensor_tensor(
            out=acc.ap[:, 1:TT], in0=ps.ap[:, 0:TT - 1], scalar=vcol(C_CW1 + ci),
            in1=acc.ap[:, 1:TT], op0=ALU.mult, op1=ALU.add), reads=[ps, acc, V], writes=[acc])
        P.emit("dve", lambda E: E.scalar_tensor_tensor(
            out=acc.ap[:, 2:TT], in0=ps.ap[:, 0:TT - 2], scalar=vcol(C_CW0 + ci),
            in1=acc.ap[:, 2:TT], op0=ALU.mult, op1=ALU.add), reads=[ps, acc, V], writes=[acc])
        if t > 0:
            P.emit("dve", lambda E: E.scalar_tensor_tensor(
                out=acc.ap[:, 0:1], in0=ch.ap[:, 1:2], scalar=vcol(C_CW1 + ci),
                in1=acc.ap[:, 0:1], op0=ALU.mult, op1=ALU.add), reads=[ch, acc, V], writes=[acc])
            P.emit("dve", lambda E: E.scalar_tensor_tensor(
                out=acc.ap[:, 0:2], in0=ch.ap[:, 0:2], scalar=vcol(C_CW0 + ci),
                in1=acc.ap[:, 0:2], op0=ALU.mult, op1=ALU.add), reads=[ch, acc, V], writes=[acc])
        if t + 1 < NT:
            P.emit("act", lambda E: E.activation(out=ch.ap, in_=ps.ap[:, TT - 2:TT], func=AF.Copy),
                   reads=[ps], writes=[ch])

    def ffn_pair(t, j, jj, w3, rb, last):
        psa = next_ps()
        mm_group(psa.ap, [(w3[:, k, jj * 128:(jj + 1) * 128], H1Bc[k].ap) for k in range(8)],
                 reads=H1Bc + [rb], writes=[psa])
        psv = next_ps()
        mm_group(psv.ap, [(w3[:, k, 256 + jj * 128:256 + (jj + 1) * 128], H1Bc[k].ap) for k in range(8)],
                 reads=H1Bc + [rb], writes=[psv])
        if last:
            ring_release(rb)
        aca, acv, ga = ACA[j % 2], ACV[j % 2], GA[j % 2]
        conv_evac(t, psa, aca, j)
        conv_evac(t, psv, acv, NFC + j)
        P.emit("act", lambda E: E.activation(out=ga.ap, in_=aca.ap, func=AF.Gelu_apprx_tanh), reads=[aca], writes=[ga])
        P.emit("pool", lambda E: E.tensor_tensor(out=GT[j].ap, in0=ga.ap, in1=acv.ap, op=ALU.mult),
               reads=[ga, acv], writes=[GT[j]])

    def ffn_up_gen(t):
        for b in range(11):
            rb = ring_take("up", b)
            w3 = rb.ap[:, 0:4096].rearrange("p (k n) -> p k n", n=512)
            for jj in range(2):
                ffn_pair(t, 2 * b + jj, jj, w3, rb, jj == 1)
                yield

    def ffn_down_gen(t):
        zc = XFc[t % 2]
        for j in range(8):
            rb = ring_take("dn", j)
            w3 = rb.ap[:, 0:2816].rearrange("p (k n) -> p k n", n=128)
            ps = next_ps()
            mm_group(ps.ap, [(w3[:, k, :], GT[k].ap) for k in range(NFC)], reads=[rb] + GT, writes=[ps])
            ring_release(rb)
            ln_phase1(H1Fc, zc, j, ps)
            yield

    def store_out(t):
        b = t % 2
        P.emit("sp", lambda E: E.dma_start(
            out=outT[:, t * TT:(t + 1) * TT].rearrange("(k p) n -> p k n", p=128), in_=XF_full[b]),
            reads=XFc[b], sem=OSEM[b], inc=16)

    def ln1_finish_gen(t):
        zc = XFc[t % 2]
        yield from ln_finish_gen(zc, C_L1G, C_L1B, lambda j: [(H1Fc[j], H1Fc[j].ap), (H1Bc[j], H1Bc[j].ap)])
        if DEBUG and t < 2:
            P.emit("sp", lambda E: E.dma_start(out=dbg[t].rearrange("(k p) n -> p k n", p=128), in_=H1F_full),
                   reads=H1Fc, sem=DSEM, inc=16)

    def ln2_finish_gen(t):
        zc = XFc[t % 2]
        return ln_finish_gen(zc, C_L2G, C_L2B, lambda j: [(zc[j], zc[j].ap)])

    def drain(g):
        for _ in g:
            pass

    def step(g, n=1):
        for _ in range(n):
            try:
                next(g)
            except StopIteration:
                return False
        return True

    def interleave(main, side, k):
        alive = True
        for _ in main:
            if alive and side is not None:
                alive = step(side, k)
        if side is not None and alive:
            drain(side)

    ring_init()
    x_load(0)
    x_load(1)

    drain(mixer_gen(0))
    issue_late_casts(100)
    drain(mix_gen(0))
    drain(ln1_finish_gen(0))

    for t in range(NT):
        nxt = t + 1 < NT
        interleave(ffn_up_gen(t), mixer_gen(t + 1) if nxt else None, 4)
        interleave(ffn_down_gen(t), mix_gen(t + 1) if nxt else None, 1)
        if nxt:
            drain(ln1_finish_gen(t + 1))
        drain(ln2_finish_gen(t))
        store_out(t)
        if t + 2 < NT:
            x_load(t + 2)

    P.wait_only("sp", [(OSEM[i], OSEM[i].count) for i in range(2)] + ([(DSEM, DSEM.count)] if DEBUG else []))

    if record:
        return take_log

    with nc.Block() as block:
        for ename, attr in ENG_ATTR.items():
            ops = P.ops[ename]

            def body(E, ops=ops):
                for f in ops:
                    f(E)
            getattr(block, attr)(body)
    return nc


def build():
    order = _emit_program(bass.Bass("TRN2", target_bir_lowering=False), None)
    return _emit_program(bass.Bass("TRN2", target_bir_lowering=False), order)


_NC_CACHE = {}


def _pack_vecs(pool_scale, hgrn_lb, hgrn_gnorm, ln1_g, ln1_b, ln2_g, ln2_b, conv_w, conv_b):
    cols = []
    cols.append(np.asarray(pool_scale)[0].reshape(4, 128).T)
    cols.append(np.asarray(hgrn_lb)[0].reshape(4, 128).T)
    cols.append(np.asarray(hgrn_lb)[1].reshape(4, 128).T)
    cols.append(np.asarray(hgrn_gnorm)[0].reshape(1, 128).T)
    for v in (ln1_g, ln1_b, ln2_g, ln2_b):
        cols.append(np.asarray(v)[0].reshape(8, 128).T)
    cw = np.asarray(conv_w)[0]
    for j in range(3):
        cols.append(cw[j].reshape(2 * NFC, 128).T)
    cols.append(np.asarray(conv_b)[0].reshape(2 * NFC, 128).T)
    v = np.ascontiguousarray(np.concatenate(cols, axis=1).astype(np.float32))
    assert v.shape == (128, NV), v.shape
    return v


def kernel(x, w_in, pool_w, pool_scale, hgrn_lb, hgrn_gnorm, w_out, ln1_g, ln1_b,
           w_up, conv_w, conv_b, w_down, ln2_g, ln2_b):
    x = np.asarray(x, dtype=np.float32)
    if "nc" not in _NC_CACHE:
        _NC_CACHE["nc"] = build()
    nc = _NC_CACHE["nc"]
    vecs = _pack_vecs(pool_scale, hgrn_lb, hgrn_gnorm, ln1_g, ln1_b, ln2_g, ln2_b, conv_w, conv_b)
    shared = {
        "w_in": np.ascontiguousarray(np.asarray(w_in, np.float32)[0]),
        "pool_w": np.ascontiguousarray(np.asarray(pool_w, np.float32)[0]),
        "w_out": np.ascontiguousarray(np.asarray(w_out, np.float32)[0]),
        "w_up": np.ascontiguousarray(np.asarray(w_up, np.float32)[0]),
        "w_down": np.ascontiguousarray(np.asarray(w_down, np.float32)[0]),
        "vecs": vecs,
    }
    in_maps = []
    for b in range(8):
        m = dict(shared)
        m["xT"] = np.ascontiguousarray(x[b].T)
        in_maps.append(m)
    res = run_bass_kernel_spmd(nc, in_maps, core_ids=list(range(8)))
    out = np.empty((8, T, D), dtype=np.float32)
    for b in range(8):
        out[b] = res.results[b]["outT"].T
    return out
```

```python
import numpy as np
import concourse.bass as bass
import concourse.mybir as mybir
from concourse.bass_utils import run_bass_kernel_spmd

F32 = mybir.dt.float32
BF16 = mybir.dt.bfloat16
AF = mybir.ActivationFunctionType
ALU = mybir.AluOpType

T = 4096
D = 1024
TT = 512
NT = T // TT
DFF = 2816
NFC = DFF // 128
ALPHA = 2.0 ** 0.25
LN_EPS = 1e-5
RMS_EPS = 1e-6

C_PS = 0
C_LB0 = 4
C_LB1 = 8
C_GN = 12
C_L1G = 13
C_L1B = 21
C_L2G = 29
C_L2B = 37
C_CW0 = 45
C_CW1 = 89
C_CW2 = 133
C_CB = 177
NV = 221

ENG_ATTR = {"pe": "tensor", "act": "scalar", "dve": "vector", "pool": "gpsimd", "sp": "sync"}


class Sem:
    def __init__(self, handle):
        self.handle = handle
        self.count = 0


class Buf:
    def __init__(self, ap, excl=False):
        self.ap = ap
        self.w = None
        self.r = {}
        self.excl = excl

    def __getitem__(self, k):
        return self.ap[k]


class Prog:
    def __init__(self, nc):
        self.nc = nc
        self.ops = {e: [] for e in ENG_ATTR}
        self.sems = {e: Sem(nc.alloc_semaphore("sem_" + e)) for e in ENG_ATTR}
        self.waited = {e: {} for e in ENG_ATTR}

    def new_sem(self, name):
        return Sem(self.nc.alloc_semaphore(name))

    def emit(self, eng, fn, reads=(), writes=(), extra=(), sem=None, inc=1):
        deps = list(extra)
        for b in reads:
            if b.w is not None:
                deps.append(b.w)
            if b.excl:
                deps.extend(tok for s_, tok in b.r.items() if s_ is not self.sems[eng])
        for b in writes:
            if b.w is not None:
                deps.append(b.w)
            deps.extend(b.r.values())
        waits = {}
        wd = self.waited[eng]
        for (s, v) in deps:
            if wd.get(s, 0) < v and waits.get(s, 0) < v:
                waits[s] = v
        for s, v in waits.items():
            wd[s] = v
        if sem is None:
            sem = self.sems[eng]
        sem.count += inc
        tok = (sem, sem.count)
        wl = [(s.handle, v) for s, v in waits.items()]
        h = sem.handle

        def run(E, wl=wl, fn=fn, h=h, inc=inc):
            for (sh, v) in wl:
                E.wait_ge(sh, v)
            fn(E).then_inc(h, inc)

        self.ops[eng].append(run)
        for b in reads:
            old = b.r.get(sem)
            if old is None or old[1] < tok[1]:
                b.r[sem] = tok
        for b in writes:
            b.w = tok
            b.r = {}
        return tok

    def wait_only(self, eng, toks):
        wl = [(s.handle, v) for (s, v) in toks]

        def run(E, wl=wl):
            for (sh, v) in wl:
                E.wait_ge(sh, v)

        self.ops[eng].append(run)


DEBUG = False
SIDE_K = 4


def _emit_program(nc, order):
    P = Prog(nc)
    record = order is None
    take_log = []

    xT = nc.dram_tensor("xT", [D, T], F32, kind="ExternalInput").ap()
    w_in = nc.dram_tensor("w_in", [D, 2560], F32, kind="ExternalInput").ap()
    pool_w = nc.dram_tensor("pool_w", [4, 128, 128], F32, kind="ExternalInput").ap()
    w_out = nc.dram_tensor("w_out", [D, D], F32, kind="ExternalInput").ap()
    w_up = nc.dram_tensor("w_up", [D, 2 * DFF], F32, kind="ExternalInput").ap()
    w_down = nc.dram_tensor("w_down", [DFF, D], F32, kind="ExternalInput").ap()
    vecs = nc.dram_tensor("vecs", [128, NV], F32, kind="ExternalInput").ap()
    outT = nc.dram_tensor("outT", [D, T], F32, kind="ExternalOutput").ap()
    dbg = nc.dram_tensor("dbg", [4, D, TT], F32, kind="ExternalOutput").ap() if DEBUG else None
    DSEM = P.new_sem("dsem") if DEBUG else None

    Win = nc.dram_tensor("Win_s", [5, 128, 8, 512], BF16).ap()
    Wout = nc.dram_tensor("Wout_s", [2, 128, 8, 512], BF16).ap()
    Wup = nc.dram_tensor("Wup_s", [11, 128, 8, 512], BF16).ap()
    Wdn = nc.dram_tensor("Wdn_s", [8, 128, 22, 128], BF16).ap()

    def sb(name, shape, dt=F32):
        return Buf(nc.alloc_sbuf_tensor(name, list(shape), dt).ap())

    def chunks(name, n, dt=F32):
        full = nc.alloc_sbuf_tensor(name, [128, n, TT], dt).ap()
        return full, [Buf(full[:, j, :]) for j in range(n)]

    V = sb("V", [128, NV])
    LBV = sb("LBV", [128, 4])
    OML = sb("OML", [128, 4])
    LBD = sb("LBD", [128, 4])
    PSC = sb("PSC", [128, 4])
    POOLW = sb("POOLW", [128, 4, 128], BF16)
    IDENT = sb("IDENT", [128, 128], BF16)
    IDF = sb("IDF", [128, 128])
    CMASK = sb("CMASK", [128, 128])
    MSK = sb("MSK", [128, TT])
    ONES_R = sb("ONES_R", [128, 128], BF16)
    ONES_L = sb("ONES_L", [128, 128], BF16)
    WC = sb("WC", [128, 4, 16])

    XF_full, XFc = [], []
    for i in range(2):
        f, c = chunks("XF%d" % i, 8)
        XF_full.append(f)
        XFc.append(c)
    XB_full, XBc = chunks("XB", 8, BF16)
    H1B_full, H1Bc, H1F_full, H1Fc = [], [], [], []
    _f, _c = chunks("H1B", 8, BF16)
    for i in range(2):
        H1B_full.append(_f)
        H1Bc.append(_c)
        f, c = chunks("H1F%d" % i, 8)
        H1F_full.append(f)
        H1Fc.append(c)
    RING_E = 4096
    NR = 4
    RING = [sb("RING%d" % i, [128, RING_E], BF16) for i in range(NR)]
    RSEM = [P.new_sem("rsem%d" % i) for i in range(NR)]
    XSEM = [P.new_sem("xsem%d" % i) for i in range(2)]
    XBSEM = P.new_sem("xbsem")
    OSEM = [P.new_sem("osem%d" % i) for i in range(2)]

    XPG = sb("XPG", [128, 16 + TT])
    XW = sb("XW", [128, TT])
    PA = sb("PA", [128, 16 + TT])
    PB = sb("PB", [128, 16 + TT])
    PTMP = sb("PTMP", [128, 16])
    PHALO = [sb("PHALO%d" % g, [128, 16]) for g in range(4)]
    DT = sb("DT", [128, TT], BF16)
    YT = [sb("YT%d" % k, [128, TT], BF16) for k in range(8)]

    QS = sb("QS", [128, TT])
    SG = sb("SG", [128, TT])
    GS = sb("GS", [128, TT])
    KK = sb("KK", [128, TT])
    LF = sb("LF", [128, TT])
    BB = sb("BB", [128, TT])
    EE = sb("EE", [128, TT])
    EN = sb("EN", [128, TT])
    O32, RS, T2 = LF, BB, EN
    EM = sb("EM", [128, 8])
    EL = sb("EL", [128, 8])
    QT = sb("QT", [128, TT], BF16)
    KTB = sb("KTB", [128, TT], BF16)
    KH = sb("KH", [128, TT], BF16)
    VT = sb("VT", [128, 4, 512], BF16)
    AT = sb("AT", [128, 4, 128], BF16)
    KHE = sb("KHE", [128, 4, 128], BF16)
    KHO = sb("KHO", [128, 4, 128], BF16)
    S32 = [sb("S32_%d" % h, [128, 128]) for h in range(4)]
    ST = sb("ST", [128, 8, 128], BF16)
    OSQ = sb("OSQ", [128, TT], BF16)

    ZB = [sb("ZB%d" % i, [128, TT], BF16) for i in range(2)]
    ZSQ = [sb("ZSQ%d" % i, [128, TT], BF16) for i in range(2)]
    LT = [sb("LT%d" % i, [128, TT]) for i in range(2)]
    M2, VAR = LT[0], LT[1]

    ACA = [sb("ACA%d" % i, [128, TT]) for i in range(2)]
    ACV = [sb("ACV%d" % i, [128, TT]) for i in range(2)]
    _ga = sb("GA", [128, TT])
    GA = [_ga, _ga]
    GT = [sb("GT%d" % k, [128, TT], BF16) for k in range(NFC)]
    HS = [sb("HS%d" % i, [128, 2 * NFC, 2]) for i in range(2)]
    HW = sb("HW", [128, 2 * NFC, 2])
    HTMP = sb("HTMP", [128, 2 * NFC])

    if record:
        print("sbuf bytes remaining:", nc.sbuf_bytes_remaining)
    PSG = [Buf(nc.alloc_psum_tensor("psg%d" % i, [128, 512], F32).ap(), excl=True) for i in range(5)]
    PSL = [Buf(nc.alloc_psum_tensor("psl%d" % i, [128, 512], F32).ap(), excl=True) for i in range(2)]
    PST = Buf(nc.alloc_psum_tensor("pst", [128, 1024], BF16).ap(), excl=True)
    ps_rr = [0]

    def next_ps():
        b = PSG[ps_rr[0] % 5]
        ps_rr[0] += 1
        return b

    WSRC = {"in": Buf(Win), "out": Buf(Wout), "up": Buf(Wup), "dn": Buf(Wdn)}
    WAP = {"in": Win, "out": Wout, "up": Wup, "dn": Wdn}
    WN = {"in": 4096, "out": 4096, "up": 4096, "dn": 2816}
    cast_sems = {k: P.new_sem("cs_" + k) for k in ("in", "out", "up", "dn", "v", "pw")}

    def cast(dst, src_, key, bufs=()):
        s = cast_sems[key]
        tok = P.emit("pool", lambda E: E.dma_start(out=dst, in_=src_), sem=s, inc=16)
        for b in bufs:
            b.w = tok
        if key in WSRC:
            WSRC[key].w = tok
        return tok

    def pmemset(buf, ap, val):
        P.emit("pool", lambda E: E.memset(ap, val), writes=[buf])

    cast(V.ap, vecs, "v", [V])
    cast(POOLW.ap, pool_w.rearrange("g c d -> c g d"), "pw", [POOLW])
    win_r = w_in.rearrange("(k p) n -> p k n", p=128)
    cast(Win[0], win_r[:, :, 1536:2048], "in")
    for h in range(4):
        for j, base in enumerate((0, 512, 1024, 2048)):
            cast(Win[1 + h][:, :, j * 128:(j + 1) * 128],
                 win_r[:, :, base + h * 128: base + (h + 1) * 128], "in")
    wout_r = w_out.rearrange("(k p) n -> p k n", p=128)
    wup_r = w_up.rearrange("(k p) n -> p k n", p=128)
    wdn_r = w_down.rearrange("(k p) n -> p k n", p=128)
    late_casts = []
    for b in range(2):
        late_casts.append((Wout[b], wout_r[:, :, b * 512:(b + 1) * 512], "out"))
    for b in range(11):
        late_casts.append((Wup[b][:, :, 0:256], wup_r[:, :, b * 256:(b + 1) * 256], "up"))
        late_casts.append((Wup[b][:, :, 256:512], wup_r[:, :, DFF + b * 256: DFF + (b + 1) * 256], "up"))
    for b in range(8):
        late_casts.append((Wdn[b], wdn_r[:, :, b * 128:(b + 1) * 128], "dn"))

    def issue_late_casts(n):
        for _ in range(n):
            if late_casts:
                cast(*late_casts.pop(0))

    pmemset(IDF, IDF.ap, 0.0)
    P.emit("pool", lambda E: E.affine_select(out=IDF.ap, in_=IDF.ap, pattern=[[-1, 128]], compare_op=ALU.not_equal,
                                             fill=1.0, base=0, channel_multiplier=1), reads=[IDF], writes=[IDF])
    pmemset(CMASK, CMASK.ap, 1.0)
    P.emit("pool", lambda E: E.affine_select(out=CMASK.ap, in_=CMASK.ap, pattern=[[1, 128]], compare_op=ALU.is_ge,
                                             fill=0.0, base=0, channel_multiplier=-1), reads=[CMASK], writes=[CMASK])
    pmemset(CMASK, CMASK.ap[0:64, 64:128], 0.0)
    pmemset(MSK, MSK.ap, 1.0)
    pmemset(MSK, MSK.ap.rearrange("p (c t) -> p c t", t=64)[:, :, 0:1], 0.0)
    pmemset(ONES_R, ONES_R.ap, 1.0 / 128.0)
    pmemset(ONES_L, ONES_L.ap, 1.0 / 1024.0)
    for g in range(4):
        w = 2 ** (g + 1)
        pmemset(WC, WC.ap[:, g, :], 1.0)
        for t_ in range(w - 1):
            pmemset(WC, WC.ap[:, g, t_:t_ + 1], float(w) / float(t_ + 1))
        pmemset(PHALO[g], PHALO[g].ap, 0.0)
    for h in range(4):
        pmemset(S32[h], S32[h].ap, 0.0)
    pmemset(KHE, KHE.ap, 0.0)
    pmemset(KHO, KHO.ap, 0.0)
    pmemset(PTMP, PTMP.ap, 0.0)
    issue_late_casts(2)

    P.emit("dve", lambda E: E.tensor_copy(out=IDENT.ap, in_=IDF.ap), reads=[IDF], writes=[IDENT])
    P.emit("dve", lambda E: E.tensor_tensor(out=LBD.ap, in0=V.ap[:, C_LB0:C_LB0 + 4], in1=V.ap[:, C_LB1:C_LB1 + 4],
                                            op=ALU.subtract), reads=[V], writes=[LBD])
    P.emit("act", lambda E: E.activation(out=LBV.ap, in_=LBD.ap, func=AF.Sigmoid), reads=[LBD], writes=[LBV])
    P.emit("act", lambda E: E.activation(out=OML.ap, in_=LBD.ap, func=AF.Sigmoid, scale=-1.0),
           reads=[LBD], writes=[OML])

    def psc_op(g):
        P.emit("dve", lambda E: E.tensor_scalar(out=PSC.ap[:, g:g + 1], in0=V.ap[:, C_PS + g:C_PS + g + 1],
                                                scalar1=1.0 / (2 ** (g + 1)), scalar2=None, op0=ALU.mult),
               reads=[V], writes=[PSC])
    for g in range(4):
        psc_op(g)

    ring_state = {"next_load": 0, "next_use": 0}
    slot_of = {}

    def _load_into(slot):
        i = ring_state["next_load"]
        if i >= len(order):
            return
        ring_state["next_load"] += 1
        key, b = order[i]
        assert not any(c[2] == key for c in late_casts), "ring load emitted before all casts of %s" % key
        n = WN[key]
        src = WAP[key][b].rearrange("p k n -> p (k n)")
        slot_of[i] = slot
        P.emit("sp", lambda E: E.dma_start(out=RING[slot].ap[:, 0:n], in_=src),
               reads=[WSRC[key]], writes=[RING[slot]], sem=RSEM[slot], inc=16)

    def ring_init():
        if record:
            return
        for s in range(NR):
            _load_into(s)

    def ring_release(rb):
        if record:
            return
        _load_into(RING.index(rb))

    def ring_take(key, b):
        i = ring_state["next_use"]
        ring_state["next_use"] += 1
        if record:
            take_log.append((key, b))
            return RING[0]
        assert order[i] == (key, b), (i, order[i], key, b)
        return RING[slot_of[i]]

    def x_load(t):
        b = t % 2
        P.emit("sp", lambda E: E.dma_start(out=XF_full[b],
                                           in_=xT[:, t * TT:(t + 1) * TT].rearrange("(k p) n -> p k n", p=128)),
               writes=XFc[b], sem=XSEM[b], inc=16)

    def xb_load(t):
        P.emit("pool", lambda E: E.dma_start(out=XB_full,
                                             in_=xT[:, t * TT:(t + 1) * TT].rearrange("(k p) n -> p k n", p=128)),
               writes=XBc, sem=XBSEM, inc=16)

    def mm_group(out_ap, pairs, reads, writes):
        def fn(E):
            n = len(pairs)
            inst = None
            for i, (l, r) in enumerate(pairs):
                inst = E.matmul(out_ap, lhsT=l, rhs=r, start=(i == 0), stop=(i == n - 1))
            return inst
        return P.emit("pe", fn, reads=reads, writes=writes)

    def c3(ap):
        return ap.rearrange("p (c t) -> p c t", t=64)

    def vcol(c):
        return V.ap[:, c:c + 1]

    def ln_phase1(src_c, dst_c, j, ps):
        P.emit("dve", lambda E: E.scalar_tensor_tensor(
            out=dst_c[j].ap, in0=src_c[j].ap, scalar=ALPHA, in1=ps.ap, op0=ALU.mult, op1=ALU.add),
            reads=[ps, src_c[j]], writes=[dst_c[j]])

    def ln_stat(j, zb, zs):
        P.emit("pe", lambda E: E.matmul(PSL[0].ap, lhsT=ONES_L.ap, rhs=zb.ap, start=(j == 0), stop=(j == 7)),
               reads=[zb, ONES_L], writes=[PSL[0]] if j == 0 else [])
        tok = P.emit("pe", lambda E: E.matmul(PSL[1].ap, lhsT=ONES_L.ap, rhs=zs.ap, start=(j == 0), stop=(j == 7)),
                     reads=[zs, ONES_L], writes=[PSL[1]] if j == 0 else [])
        if j == 7:
            PSL[0].w = tok
            PSL[1].w = tok

    def ln_phase2_chunk(z_c, j):
        zb, zs = ZB[j % 2], ZSQ[j % 2]
        P.emit("act", lambda E: E.activation(out=zb.ap, in_=z_c[j].ap, func=AF.Copy), reads=[z_c[j]], writes=[zb])
        P.emit("act", lambda E: E.activation(out=zs.ap, in_=z_c[j].ap, func=AF.Square), reads=[z_c[j]], writes=[zs])
        ln_stat(j, zb, zs)

    def ln_rstd():
        P.emit("act", lambda E: E.activation(out=M2.ap, in_=PSL[0].ap, func=AF.Square), reads=[PSL[0]], writes=[M2])
        P.emit("dve", lambda E: E.tensor_tensor(out=VAR.ap, in0=PSL[1].ap, in1=M2.ap, op=ALU.subtract),
               reads=[PSL[1], M2], writes=[VAR])
        P.emit("dve", lambda E: E.tensor_scalar(out=VAR.ap, in0=VAR.ap, scalar1=LN_EPS, scalar2=None, op0=ALU.add),
               reads=[VAR], writes=[VAR])
        P.emit("act", lambda E: E.activation(out=VAR.ap, in_=VAR.ap, func=AF.Ln), reads=[VAR], writes=[VAR])
        P.emit("act", lambda E: E.activation(out=PSL[1].ap, in_=VAR.ap, func=AF.Exp, scale=-0.5),
               reads=[VAR], writes=[PSL[1]])

    def ln_out_chunk(z_c, j, gcol, bcol, outs):
        lt = LT[j % 2]
        P.emit("dve", lambda E: E.tensor_tensor(out=lt.ap, in0=z_c[j].ap, in1=PSL[0].ap, op=ALU.subtract),
               reads=[z_c[j], PSL[0]], writes=[lt])
        P.emit("dve", lambda E: E.tensor_tensor(out=lt.ap, in0=lt.ap, in1=PSL[1].ap, op=ALU.mult),
               reads=[lt, PSL[1]], writes=[lt])
        for (ob, oap) in outs:
            P.emit("act", lambda E, oap=oap: E.activation(out=oap, in_=lt.ap, func=AF.Identity,
                                                          scale=vcol(gcol + j), bias=vcol(bcol + j)),
                   reads=[lt, V], writes=[ob])

    def ln_finish_gen(z_c, gcol, bcol, outs_fn, after_chunk=None):
        for j in range(8):
            ln_phase2_chunk(z_c, j)
            yield
        ln_rstd()
        yield
        for j in range(8):
            ln_out_chunk(z_c, j, gcol, bcol, outs_fn(j))
            if after_chunk is not None:
                after_chunk(j)
            yield

    def proj_v_block(tb, w3, rb):
        ps = next_ps()
        mm_group(ps.ap, [(XBc[k].ap[:, tb * 128:(tb + 1) * 128], w3[:, k, :]) for k in range(8)],
                 reads=XBc + [rb], writes=[ps])
        P.emit("act", lambda E: E.activation(out=VT.ap[:, tb, :], in_=ps.ap, func=AF.Copy), reads=[ps], writes=[VT])

    def pool_level(dst, src, lo, sh):
        P.emit("pool", lambda E: E.tensor_tensor(
            out=dst.ap[:, lo:16 + TT], in0=src.ap[:, lo:16 + TT], in1=src.ap[:, lo - sh:16 + TT - sh], op=ALU.add),
            reads=[src], writes=[dst])

    def scan_step(h, c):
        P.emit("dve", lambda E: E.tensor_scalar(out=ST.ap[:, c, :], in0=S32[h].ap, scalar1=EM.ap[:, c:c + 1],
                                                scalar2=None, op0=ALU.mult),
               reads=[S32[h], EM], writes=[ST])
        P.emit("dve", lambda E: E.scalar_tensor_tensor(
            out=S32[h].ap, in0=S32[h].ap, scalar=EL.ap[:, c:c + 1],
            in1=PSL[c // 4].ap[:, (c % 4) * 128:(c % 4 + 1) * 128], op0=ALU.mult, op1=ALU.add),
            reads=[S32[h], EL, PSL[c // 4]], writes=[S32[h]])

    def proj_group(w3, rb, j):
        ps = next_ps()
        mm_group(ps.ap, [(w3[:, k, j * 128:(j + 1) * 128], XBc[k].ap) for k in range(8)],
                 reads=XBc + [rb], writes=[ps])
        return ps

    def head_gen(t, h):
        rb = ring_take("in", 1 + h)
        w3 = rb.ap[:, 0:4096].rearrange("p (k n) -> p k n", n=512)
        wv = float(2 ** (h + 1))
        g = h
        ps_f = proj_group(w3, rb, 2)
        P.emit("act", lambda E: E.activation(out=SG.ap, in_=ps_f.ap, func=AF.Sigmoid), reads=[ps_f], writes=[SG])
        yield
        ps_q = proj_group(w3, rb, 1)
        P.emit("act", lambda E: E.activation(out=QS.ap, in_=ps_q.ap, func=AF.Silu), reads=[ps_q], writes=[QS])
        yield
        ps_xp = proj_group(w3, rb, 0)
        P.emit("act", lambda E: E.activation(out=XPG.ap[:, 16:16 + TT], in_=ps_xp.ap, func=AF.Copy),
               reads=[ps_xp], writes=[XPG])
        P.emit("act", lambda E: E.activation(out=XW.ap, in_=ps_xp.ap, func=AF.Copy, scale=-wv),
               reads=[ps_xp], writes=[XW])
        yield
        ps_g = proj_group(w3, rb, 3)
        ring_release(rb)
        P.emit("act", lambda E: E.activation(out=GS.ap, in_=ps_g.ap, func=AF.Silu), reads=[ps_g], writes=[GS])
        yield

        P.emit("dve", lambda E: E.tensor_scalar(out=SG.ap, in0=SG.ap, scalar1=OML.ap[:, h:h + 1],
                                                scalar2=LBV.ap[:, h:h + 1], op0=ALU.mult, op1=ALU.add),
               reads=[SG, OML, LBV], writes=[SG])
        P.emit("act", lambda E: E.activation(out=LF.ap, in_=SG.ap, func=AF.Ln), reads=[SG], writes=[LF])
        P.emit("dve", lambda E: E.tensor_scalar(out=KK.ap, in0=SG.ap, scalar1=-1.0, scalar2=1.0,
                                                op0=ALU.mult, op1=ALU.add), reads=[SG], writes=[KK])
        yield
        P.emit("dve", lambda E: E.tensor_tensor_scan(out=BB.ap, data0=MSK.ap, data1=LF.ap, initial=0.0,
                                                     op0=ALU.mult, op1=ALU.add), reads=[MSK, LF], writes=[BB])
        P.emit("dve", lambda E: E.tensor_tensor(out=c3(LF.ap), in0=c3(BB.ap),
                                                in1=c3(BB.ap)[:, :, 31:32].to_broadcast([128, 8, 64]),
                                                op=ALU.subtract), reads=[BB], writes=[LF])
        yield
        P.emit("act", lambda E: E.activation(out=EE.ap, in_=LF.ap, func=AF.Exp), reads=[LF], writes=[EE])
        P.emit("act", lambda E: E.activation(out=EN.ap, in_=LF.ap, func=AF.Exp, scale=-1.0), reads=[LF], writes=[EN])
        P.emit("act", lambda E: E.activation(out=EM.ap, in_=c3(BB.ap)[:, :, 31], func=AF.Exp), reads=[BB], writes=[EM])
        P.emit("act", lambda E: E.activation(out=EL.ap, in_=c3(BB.ap)[:, :, 63], func=AF.Exp), reads=[BB], writes=[EL])
        yield
        P.emit("pool", lambda E: E.tensor_tensor(out=QT.ap, in0=QS.ap, in1=EE.ap, op=ALU.mult),
               reads=[QS, EE], writes=[QT])
        P.emit("dve", lambda E: E.tensor_tensor(out=KK.ap, in0=KK.ap, in1=EN.ap, op=ALU.mult),
               reads=[KK, EN], writes=[KK])
        P.emit("act", lambda E: E.activation(out=KTB.ap, in_=KK.ap, func=AF.Copy), reads=[KK], writes=[KTB])
        P.emit("dve", lambda E: E.tensor_tensor(out=c3(KH.ap), in0=c3(KK.ap),
                                                in1=c3(EE.ap)[:, :, 63:64].to_broadcast([128, 8, 64]),
                                                op=ALU.mult), reads=[KK, EE], writes=[KH])
        yield

        P.emit("pool", lambda E: E.tensor_copy(out=XPG.ap[:, 0:16], in_=PHALO[g].ap), reads=[PHALO[g]], writes=[XPG])
        src = XPG
        bufs = [PA, PB]
        sh = 1
        lo = 0
        for lvl in range(g + 1):
            dst = bufs[lvl % 2]
            lo = lo + sh
            pool_level(dst, src, lo, sh)
            src = dst
            sh *= 2
        ssum = src
        P.emit("pool", lambda E: E.tensor_tensor(out=DT.ap, in0=ssum.ap[:, 16:16 + TT], in1=XW.ap, op=ALU.add),
               reads=[ssum, XW], writes=[DT])
        if t == 0:
            P.emit("pool", lambda E: E.tensor_tensor(out=PTMP.ap, in0=ssum.ap[:, 16:32], in1=WC.ap[:, g, :], op=ALU.mult),
                   reads=[ssum, WC], writes=[PTMP])
            P.emit("pool", lambda E: E.tensor_tensor(out=DT.ap[:, 0:16], in0=PTMP.ap, in1=XW.ap[:, 0:16], op=ALU.add),
                   reads=[PTMP, XW], writes=[DT])
        P.emit("pool", lambda E: E.tensor_copy(out=PHALO[g].ap, in_=XPG.ap[:, TT:TT + 16]), reads=[XPG], writes=[PHALO[g]])
        yield

        psa = next_ps()

        def fn_at(E):
            inst = None
            for tb in range(4):
                inst = E.matmul(psa.ap[:, tb * 128:(tb + 1) * 128], lhsT=KTB.ap[:, tb * 128:(tb + 1) * 128],
                                rhs=QT.ap[:, tb * 128:(tb + 1) * 128], start=True, stop=True)
            return inst
        P.emit("pe", fn_at, reads=[KTB, QT], writes=[psa])
        P.emit("dve", lambda E: E.tensor_tensor(
            out=AT.ap, in0=psa.ap.rearrange("p (b t) -> p b t", t=128),
            in1=CMASK.ap.unsqueeze(1).to_broadcast([128, 4, 128]), op=ALU.mult),
            reads=[psa, CMASK], writes=[AT])

        def fn_tr(E):
            inst = None
            for tb in range(4):
                inst = E.transpose(out=PST.ap[:, tb * 128:(tb + 1) * 128], in_=KH.ap[:, tb * 128:(tb + 1) * 128],
                                   identity=IDENT.ap)
            return inst
        P.emit("pe", fn_tr, reads=[KH, IDENT], writes=[PST])
        P.emit("act", lambda E: E.activation(out=KHE.ap[0:64].rearrange("p b d -> p (b d)"), in_=PST.ap[0:64, 0:512],
                                             func=AF.Copy), reads=[PST], writes=[KHE])
        P.emit("act", lambda E: E.activation(out=KHO.ap[64:128].rearrange("p b d -> p (b d)"),
                                             in_=PST.ap[64:128, 0:512], func=AF.Copy), reads=[PST], writes=[KHO])
        psy = next_ps()
        mm_group(psy.ap, [(POOLW.ap[:, g, :], DT.ap)], reads=[POOLW, DT], writes=[psy])
        P.emit("act", lambda E: E.activation(out=YT[g].ap, in_=psy.ap, func=AF.Identity, scale=PSC.ap[:, g:g + 1]),
               reads=[psy, PSC], writes=[YT[g]])
        yield

        def fn_ds(E):
            inst = None
            for c in range(8):
                tb = c // 2
                kh = KHE if c % 2 == 0 else KHO
                inst = E.matmul(PSL[c // 4].ap[:, (c % 4) * 128:(c % 4 + 1) * 128], lhsT=kh.ap[:, tb, :],
                                rhs=VT.ap[:, tb, h * 128:(h + 1) * 128], start=True, stop=True)
            return inst
        P.emit("pe", fn_ds, reads=[KHE, KHO, VT], writes=[PSL[0], PSL[1]])
        yield
        for c in range(8):
            scan_step(h, c)
            if c % 4 == 3:
                yield
        pso = next_ps()

        def fn_o(E):
            inst = None
            for tb in range(4):
                E.matmul(pso.ap[:, tb * 128:(tb + 1) * 128], lhsT=VT.ap[:, tb, h * 128:(h + 1) * 128],
                         rhs=AT.ap[:, tb, :], start=True, stop=False)
                for c in (2 * tb, 2 * tb + 1):
                    inst = E.matmul(pso.ap[:, c * 64:(c + 1) * 64], lhsT=ST.ap[:, c, :],
                                    rhs=QT.ap[:, c * 64:(c + 1) * 64], start=False, stop=(c == 2 * tb + 1))
            return inst
        P.emit("pe", fn_o, reads=[VT, AT, ST, QT], writes=[pso])
        P.emit("act", lambda E: E.activation(out=O32.ap, in_=pso.ap, func=AF.Copy), reads=[pso], writes=[O32])
        P.emit("act", lambda E: E.activation(out=OSQ.ap, in_=pso.ap, func=AF.Square), reads=[pso], writes=[OSQ])
        yield
        psm = next_ps()
        mm_group(psm.ap, [(ONES_R.ap, OSQ.ap)], reads=[ONES_R, OSQ], writes=[psm])
        P.emit("dve", lambda E: E.tensor_scalar(out=RS.ap, in0=psm.ap, scalar1=RMS_EPS, scalar2=None,
                                                op0=ALU.add), reads=[psm], writes=[RS])
        yield
        P.emit("act", lambda E: E.activation(out=RS.ap, in_=RS.ap, func=AF.Ln), reads=[RS], writes=[RS])
        P.emit("act", lambda E: E.activation(out=RS.ap, in_=RS.ap, func=AF.Exp, scale=-0.5), reads=[RS], writes=[RS])
        yield
        P.emit("pool", lambda E: E.tensor_tensor(out=T2.ap, in0=O32.ap, in1=RS.ap, op=ALU.mult),
               reads=[O32, RS], writes=[T2])
        P.emit("dve", lambda E: E.scalar_tensor_tensor(out=YT[4 + h].ap, in0=T2.ap, scalar=vcol(C_GN),
                                                       in1=GS.ap, op0=ALU.mult, op1=ALU.mult),
               reads=[T2, V, GS], writes=[YT[4 + h]])
        yield

    def x_to_bf16(t):
        xc = XFc[t % 2]
        P.emit("dve", lambda E: E.tensor_copy(out=XB_full[:, 0:4, :], in_=XF_full[t % 2][:, 0:4, :]),
               reads=xc[0:4], writes=XBc[0:4])
        P.emit("act", lambda E: E.activation(out=XB_full[:, 4:8, :], in_=XF_full[t % 2][:, 4:8, :], func=AF.Copy),
               reads=xc[4:8], writes=XBc[4:8])

    def mixer_gen(t):
        x_to_bf16(t)
        yield
        rb = ring_take("in", 0)
        w3 = rb.ap[:, 0:4096].rearrange("p (k n) -> p k n", n=512)
        for tb in range(4):
            proj_v_block(tb, w3, rb)
            if tb == 3:
                ring_release(rb)
            yield
        for h in range(4):
            yield from head_gen(t, h)
            if t == 0 and h < 2:
                issue_late_casts(15)

    def mix_gen(t):
        xc = XFc[t % 2]
        for b in range(2):
            rb = ring_take("out", b)
            w3 = rb.ap[:, 0:4096].rearrange("p (k n) -> p k n", n=512)
            for jj in range(4):
                j = 4 * b + jj
                ps = next_ps()
                mm_group(ps.ap, [(w3[:, k, jj * 128:(jj + 1) * 128], YT[k].ap) for k in range(8)],
                         reads=[rb] + YT, writes=[ps])
                if jj == 3:
                    ring_release(rb)
                ln_phase1(xc, xc, j, ps)
                yield

    def halo_prep(t):
        hs = HS[(t - 1) % 2]
        w1 = V.ap[:, C_CW1:C_CW1 + 2 * NFC]
        w0 = V.ap[:, C_CW0:C_CW0 + 2 * NFC]
        P.emit("dve", lambda E: E.tensor_tensor(out=HW.ap[:, :, 0], in0=hs.ap[:, :, 1], in1=w1, op=ALU.mult),
               reads=[hs, V], writes=[HW])
        P.emit("dve", lambda E: E.tensor_tensor(out=HTMP.ap, in0=hs.ap[:, :, 0], in1=w0, op=ALU.mult),
               reads=[hs, V], writes=[HTMP])
        P.emit("dve", lambda E: E.tensor_tensor(out=HW.ap[:, :, 0], in0=HW.ap[:, :, 0], in1=HTMP.ap, op=ALU.add),
               reads=[HW, HTMP], writes=[HW])
        P.emit("dve", lambda E: E.tensor_tensor(out=HW.ap[:, :, 1], in0=hs.ap[:, :, 1], in1=w0, op=ALU.mult),
               reads=[hs, V], writes=[HW])

    def conv_act(t, ps, acc, ci):
        P.emit("act", lambda E: E.activation(out=acc.ap, in_=ps.ap, func=AF.Identity, scale=vcol(C_CW2 + ci),
                                             bias=vcol(C_CB + ci)), reads=[ps, V], writes=[acc])

    def conv_save(t, ps, ci):
        if t + 1 < NT:
            hs = HS[t % 2]
            P.emit("act", lambda E: E.activation(out=hs.ap[:, ci, :], in_=ps.ap[:, TT - 2:TT], func=AF.Copy),
                   reads=[ps], writes=[hs])

    def conv_dve(t, ps, acc, ci):
        P.emit("dve", lambda E: E.scalar_tensor_tensor(
            out=acc.ap[:, 1:TT], in0=ps.ap[:, 0:TT - 1], scalar=vcol(C_CW1 + ci),
            in1=acc.ap[:, 1:TT], op0=ALU.mult, op1=ALU.add), reads=[ps, acc, V], writes=[acc])
        P.emit("dve", lambda E: E.scalar_tensor_tensor(
            out=acc.ap[:, 2:TT], in0=ps.ap[:, 0:TT - 2], scalar=vcol(C_CW0 + ci),
            in1=acc.ap[:, 2:TT], op0=ALU.mult, op1=ALU.add), reads=[ps, acc, V], writes=[acc])
        if t > 0:
            P.emit("dve", lambda E: E.tensor_tensor(out=acc.ap[:, 0:2], in0=acc.ap[:, 0:2], in1=HW.ap[:, ci, :],
                                                    op=ALU.add), reads=[acc, HW], writes=[acc])

    def gate(j):
        aca, acv, ga = ACA[j % 2], ACV[j % 2], GA[j % 2]
        P.emit("act", lambda E: E.activation(out=ga.ap, in_=aca.ap, func=AF.Gelu_apprx_tanh), reads=[aca], writes=[ga])
        P.emit("pool", lambda E: E.tensor_tensor(out=GT[j].ap, in0=ga.ap, in1=acv.ap, op=ALU.mult),
               reads=[ga, acv], writes=[GT[j]])

    def ffn_pair(t, j, jj, w3, rb, last, prev_gate):
        hb = H1Bc[t % 2]
        psa = next_ps()
        mm_group(psa.ap, [(w3[:, k, jj * 128:(jj + 1) * 128], hb[k].ap) for k in range(8)],
                 reads=hb + [rb], writes=[psa])
        psv = next_ps()
        mm_group(psv.ap, [(w3[:, k, 256 + jj * 128:256 + (jj + 1) * 128], hb[k].ap) for k in range(8)],
                 reads=hb + [rb], writes=[psv])
        if last:
            ring_release(rb)
        aca, acv = ACA[j % 2], ACV[j % 2]
        conv_save(t, psa, j)
        conv_act(t, psa, aca, j)
        conv_save(t, psv, NFC + j)
        conv_act(t, psv, acv, NFC + j)
        if prev_gate is not None:
            gate(prev_gate)
        conv_dve(t, psa, aca, j)
        conv_dve(t, psv, acv, NFC + j)

    def ffn_up_gen(t):
        if t > 0:
            halo_prep(t)
        prev = None
        for b in range(11):
            rb = ring_take("up", b)
            w3 = rb.ap[:, 0:4096].rearrange("p (k n) -> p k n", n=512)
            for jj in range(2):
                j = 2 * b + jj
                ffn_pair(t, j, jj, w3, rb, jj == 1, prev)
                prev = j
                yield
        gate(prev)
        yield

    def ffn_down_gen(t):
        zc = H1Fc[t % 2]
        for j in range(8):
            rb = ring_take("dn", j)
            w3 = rb.ap[:, 0:2816].rearrange("p (k n) -> p k n", n=128)
            ps = next_ps()
            mm_group(ps.ap, [(w3[:, k, :], GT[k].ap) for k in range(NFC)], reads=[rb] + GT, writes=[ps])
            ring_release(rb)
            ln_phase1(zc, zc, j, ps)
            yield

    def ln1_finish_gen(t):
        zc = XFc[t % 2]
        hf, hb = H1Fc[t % 2], H1Bc[t % 2]
        yield from ln_finish_gen(zc, C_L1G, C_L1B, lambda j: [(hf[j], hf[j].ap), (hb[j], hb[j].ap)])
        if DEBUG and t < 2:
            P.emit("sp", lambda E: E.dma_start(out=dbg[t].rearrange("(k p) n -> p k n", p=128), in_=H1F_full[t % 2]),
                   reads=hf, sem=DSEM, inc=16)
        if t + 2 < NT:
            x_load(t + 2)
        yield

    def ln2_finish_gen(t):
        zc = H1Fc[t % 2]
        yield from ln_finish_gen(zc, C_L2G, C_L2B, lambda j: [(zc[j], zc[j].ap)])
        b_ = t % 2
        P.emit("sp", lambda E: E.dma_start(
            out=outT[:, t * TT:(t + 1) * TT].rearrange("(k p) n -> p k n", p=128), in_=H1F_full[b_]),
            reads=zc, sem=OSEM[b_], inc=16)
        yield

    def drain(g):
        for _ in g:
            pass

    def chain(*gens):
        for g in gens:
            yield from g

    def step(g, n=1):
        for _ in range(n):
            try:
                next(g)
            except StopIteration:
                return False
        return True

    def interleave(main, side, k):
        alive = side is not None
        for _ in main:
            if alive:
                alive = step(side, k)
        if alive:
            drain(side)

    ring_init()
    x_load(0)
    x_load(1)

    drain(mixer_gen(0))
    issue_late_casts(100)
    drain(mix_gen(0))
    drain(ln1_finish_gen(0))

    for t in range(NT):
        side = []
        if t >= 1:
            side.append(ln2_finish_gen(t - 1))
        if t + 1 < NT:
            side += [mixer_gen(t + 1), mix_gen(t + 1), ln1_finish_gen(t + 1)]
        interleave(chain(ffn_up_gen(t), ffn_down_gen(t)), chain(*side) if side else None, SIDE_K)
    drain(ln2_finish_gen(NT - 1))

    P.wait_only("sp", [(OSEM[i], OSEM[i].count) for i in range(2)] + ([(DSEM, DSEM.count)] if DEBUG else []))

    if record:
        return take_log

    with nc.Block() as block:
        for ename, attr in ENG_ATTR.items():
            ops = P.ops[ename]

            def body(E, ops=ops):
                for f in ops:
                    f(E)
            getattr(block, attr)(body)
    return nc


def build():
    order = _emit_program(bass.Bass("TRN2", target_bir_lowering=False), None)
    return _emit_program(bass.Bass("TRN2", target_bir_lowering=False), order)


_NC_CACHE = {}


def _pack_vecs(pool_scale, hgrn_lb, hgrn_gnorm, ln1_g, ln1_b, ln2_g, ln2_b, conv_w, conv_b):
    cols = []
    cols.append(np.asarray(pool_scale)[0].reshape(4, 128).T)
    cols.append(np.asarray(hgrn_lb)[0].reshape(4, 128).T)
    cols.append(np.asarray(hgrn_lb)[1].reshape(4, 128).T)
    cols.append(np.asarray(hgrn_gnorm)[0].reshape(1, 128).T)
    for v in (ln1_g, ln1_b, ln2_g, ln2_b):
        cols.append(np.asarray(v)[0].reshape(8, 128).T)
    cw = np.asarray(conv_w)[0]
    for j in range(3):
        cols.append(cw[j].reshape(2 * NFC, 128).T)
    cols.append(np.asarray(conv_b)[0].reshape(2 * NFC, 128).T)
    v = np.ascontiguousarray(np.concatenate(cols, axis=1).astype(np.float32))
    assert v.shape == (128, NV), v.shape
    return v


def kernel(x, w_in, pool_w, pool_scale, hgrn_lb, hgrn_gnorm, w_out, ln1_g, ln1_b,
           w_up, conv_w, conv_b, w_down, ln2_g, ln2_b):
    x = np.asarray(x, dtype=np.float32)
    if "nc" not in _NC_CACHE:
        _NC_CACHE["nc"] = build()
    nc = _NC_CACHE["nc"]
    vecs = _pack_vecs(pool_scale, hgrn_lb, hgrn_gnorm, ln1_g, ln1_b, ln2_g, ln2_b, conv_w, conv_b)
    shared = {
        "w_in": np.ascontiguousarray(np.asarray(w_in, np.float32)[0]),
        "pool_w": np.ascontiguousarray(np.asarray(pool_w, np.float32)[0]),
        "w_out": np.ascontiguousarray(np.asarray(w_out, np.float32)[0]),
        "w_up": np.ascontiguousarray(np.asarray(w_up, np.float32)[0]),
        "w_down": np.ascontiguousarray(np.asarray(w_down, np.float32)[0]),
        "vecs": vecs,
    }
    in_maps = []
    for b in range(8):
        m = dict(shared)
        m["xT"] = np.ascontiguousarray(x[b].T)
        in_maps.append(m)
    res = run_bass_kernel_spmd(nc, in_maps, core_ids=list(range(8)))
    out = np.empty((8, T, D), dtype=np.float32)
    for b in range(8):
        out[b] = res.results[b]["outT"].T
    return out
```

```python
import numpy as np
import concourse.bass as bass
import concourse.mybir as mybir
from concourse.bass_utils import run_bass_kernel_spmd

F32 = mybir.dt.float32
BF16 = mybir.dt.bfloat16
AF = mybir.ActivationFunctionType
ALU = mybir.AluOpType

T = 4096
D = 1024
TT = 512
NT = T // TT
DFF = 2816
NFC = DFF // 128
ALPHA = 2.0 ** 0.25
LN_EPS = 1e-5
RMS_EPS = 1e-6

C_PS = 0
C_LB0 = 4
C_LB1 = 8
C_GN = 12
C_L1G = 13
C_L1B = 21
C_L2G = 29
C_L2B = 37
C_CW0 = 45
C_CW1 = 89
C_CW2 = 133
C_CB = 177
NV = 221

ENG_ATTR = {"pe": "tensor", "act": "scalar", "dve": "vector", "pool": "gpsimd", "sp": "sync"}


class Sem:
    def __init__(self, handle):
        self.handle = handle
        self.count = 0


class Buf:
    def __init__(self, ap, excl=False):
        self.ap = ap
        self.w = None
        self.r = {}
        self.excl = excl

    def __getitem__(self, k):
        return self.ap[k]


class Prog:
    def __init__(self, nc):
        self.nc = nc
        self.ops = {e: [] for e in ENG_ATTR}
        self.sems = {e: Sem(nc.alloc_semaphore("sem_" + e)) for e in ENG_ATTR}
        self.waited = {e: {} for e in ENG_ATTR}

    def new_sem(self, name):
        return Sem(self.nc.alloc_semaphore(name))

    def emit(self, eng, fn, reads=(), writes=(), extra=(), sem=None, inc=1):
        deps = list(extra)
        for b in reads:
            if b.w is not None:
                deps.append(b.w)
            if b.excl:
                deps.extend(tok for s_, tok in b.r.items() if s_ is not self.sems[eng])
        for b in writes:
            if b.w is not None:
                deps.append(b.w)
            deps.extend(b.r.values())
        waits = {}
        wd = self.waited[eng]
        for (s, v) in deps:
            if wd.get(s, 0) < v and waits.get(s, 0) < v:
                waits[s] = v
        for s, v in waits.items():
            wd[s] = v
        if sem is None:
            sem = self.sems[eng]
        sem.count += inc
        tok = (sem, sem.count)
        wl = [(s.handle, v) for s, v in waits.items()]
        h = sem.handle

        def run(E, wl=wl, fn=fn, h=h, inc=inc):
            for (sh, v) in wl:
                E.wait_ge(sh, v)
            fn(E).then_inc(h, inc)

        self.ops[eng].append(run)
        for b in reads:
            old = b.r.get(sem)
            if old is None or old[1] < tok[1]:
                b.r[sem] = tok
        for b in writes:
            b.w = tok
            b.r = {}
        return tok

    def wait_only(self, eng, toks):
        wl = [(s.handle, v) for (s, v) in toks]

        def run(E, wl=wl):
            for (sh, v) in wl:
                E.wait_ge(sh, v)

        self.ops[eng].append(run)


DEBUG = False
SIDE_K = 4
SIDE_K2 = 4


def _emit_program(nc, order):
    P = Prog(nc)
    record = order is None
    take_log = []

    xT = nc.dram_tensor("xT", [D, T], F32, kind="ExternalInput").ap()
    w_in = nc.dram_tensor("w_in", [D, 2560], F32, kind="ExternalInput").ap()
    pool_w = nc.dram_tensor("pool_w", [4, 128, 128], F32, kind="ExternalInput").ap()
    w_out = nc.dram_tensor("w_out", [D, D], F32, kind="ExternalInput").ap()
    w_up = nc.dram_tensor("w_up", [D, 2 * DFF], F32, kind="ExternalInput").ap()
    w_down = nc.dram_tensor("w_down", [DFF, D], F32, kind="ExternalInput").ap()
    vecs = nc.dram_tensor("vecs", [128, NV], F32, kind="ExternalInput").ap()
    outT = nc.dram_tensor("outT", [D, T], F32, kind="ExternalOutput").ap()
    dbg = nc.dram_tensor("dbg", [4, D, TT], F32, kind="ExternalOutput").ap() if DEBUG else None
    DSEM = P.new_sem("dsem") if DEBUG else None

    Win = nc.dram_tensor("Win_s", [5, 128, 8, 512], BF16).ap()
    Wout = nc.dram_tensor("Wout_s", [2, 128, 8, 512], BF16).ap()
    Wup = nc.dram_tensor("Wup_s", [11, 128, 8, 512], BF16).ap()
    Wdn = nc.dram_tensor("Wdn_s", [8, 128, 22, 128], BF16).ap()

    def sb(name, shape, dt=F32):
        return Buf(nc.alloc_sbuf_tensor(name, list(shape), dt).ap())

    def chunks(name, n, dt=F32):
        full = nc.alloc_sbuf_tensor(name, [128, n, TT], dt).ap()
        return full, [Buf(full[:, j, :]) for j in range(n)]

    V = sb("V", [128, NV])
    LBV = sb("LBV", [128, 4])
    OML = sb("OML", [128, 4])
    LBD = sb("LBD", [128, 4])
    PSC = sb("PSC", [128, 4])
    A1 = sb("A1", [128, 4])
    A2 = sb("A2", [128, 4])
    GNH = sb("GNH", [128, 1])
    POOLW = sb("POOLW", [128, 4, 128], BF16)
    IDENT = sb("IDENT", [128, 128], BF16)
    CMASK = sb("CMASK", [128, 128])
    MSK = sb("MSK", [128, TT])
    ONES_R = sb("ONES_R", [128, 128], BF16)
    ONES_L = sb("ONES_L", [128, 128], BF16)
    WC = sb("WC", [128, 4, 16])

    XF_full, XFc = [], []
    for i in range(2):
        f, c = chunks("XF%d" % i, 8)
        XF_full.append(f)
        XFc.append(c)
    XB_full, XBc = chunks("XB", 8, BF16)
    H1B_full, H1Bc, H1F_full, H1Fc = [], [], [], []
    _f, _c = chunks("H1B", 8, BF16)
    for i in range(2):
        H1B_full.append(_f)
        H1Bc.append(_c)
        f, c = chunks("H1F%d" % i, 8)
        H1F_full.append(f)
        H1Fc.append(c)
    RING_E = 4096
    NR = 4
    RING = [sb("RING%d" % i, [128, RING_E], BF16) for i in range(NR)]
    RSEM = [P.new_sem("rsem%d" % i) for i in range(NR)]
    XSEM = [P.new_sem("xsem%d" % i) for i in range(2)]
    XBSEM = P.new_sem("xbsem")
    OSEM = [P.new_sem("osem%d" % i) for i in range(2)]

    XPG = sb("XPG", [128, 16 + TT])
    XW = sb("XW", [128, TT])
    PA = sb("PA", [128, 16 + TT])
    PB = sb("PB", [128, 16 + TT])
    PTMP = sb("PTMP", [128, 16])
    PHALO = [sb("PHALO%d" % g, [128, 16]) for g in range(4)]
    DT = sb("DT", [128, TT], BF16)
    YT = [sb("YT%d" % k, [128, TT], BF16) for k in range(8)]

    QS = sb("QS", [128, TT])
    SG = sb("SG", [128, TT])
    GS = sb("GS", [128, TT])
    KK = sb("KK", [128, TT])
    LF = sb("LF", [128, TT])
    BB = sb("BB", [128, TT])
    EE = sb("EE", [128, TT])
    EN = sb("EN", [128, TT])
    O32, RS, T2 = LF, BB, EN
    EM = sb("EM", [128, 8])
    EL = sb("EL", [128, 8])
    QT = sb("QT", [128, TT], BF16)
    KTB = sb("KTB", [128, TT], BF16)
    KH = sb("KH", [128, TT], BF16)
    VT = sb("VT", [128, 4, 512], BF16)
    AT = sb("AT", [128, 4, 128], BF16)
    KHE = sb("KHE", [128, 4, 128], BF16)
    KHO = sb("KHO", [128, 4, 128], BF16)
    S32 = [sb("S32_%d" % h, [128, 128]) for h in range(4)]
    ST = sb("ST", [128, 8, 128], BF16)
    OSQ = sb("OSQ", [128, TT], BF16)

    ZB = [sb("ZB%d" % i, [128, TT], BF16) for i in range(2)]
    ZSQ = [sb("ZSQ%d" % i, [128, TT], BF16) for i in range(2)]
    LT = [sb("LT%d" % i, [128, TT]) for i in range(2)]
    M2, VAR = LT[0], LT[1]
    IDF = LT[0]

    ACA = [sb("ACA%d" % i, [128, TT]) for i in range(2)]
    ACV = [sb("ACV%d" % i, [128, TT]) for i in range(2)]
    _ga = sb("GA", [128, TT])
    GA = [_ga, _ga]
    GT = [sb("GT%d" % k, [128, TT], BF16) for k in range(NFC)]
    HS = [sb("HS%d" % i, [128, 2 * NFC, 2]) for i in range(2)]
    HW = sb("HW", [128, 2 * NFC, 2])
    HTMP = sb("HTMP", [128, 2 * NFC])

    if record:
        print("sbuf bytes remaining:", nc.sbuf_bytes_remaining)
    PSG = [Buf(nc.alloc_psum_tensor("psg%d" % i, [128, 512], F32).ap(), excl=True) for i in range(5)]
    PSL = [Buf(nc.alloc_psum_tensor("psl%d" % i, [128, 512], F32).ap(), excl=True) for i in range(2)]
    PST = Buf(nc.alloc_psum_tensor("pst", [128, 1024], BF16).ap(), excl=True)
    ps_rr = [0]

    def next_ps():
        b = PSG[ps_rr[0] % 5]
        ps_rr[0] += 1
        return b

    WSRC = {"in": Buf(Win), "out": Buf(Wout), "up": Buf(Wup), "dn": Buf(Wdn)}
    WAP = {"in": Win, "out": Wout, "up": Wup, "dn": Wdn}
    WN = {"in": 4096, "out": 4096, "up": 4096, "dn": 2816}
    cast_sems = {k: P.new_sem("cs_" + k) for k in ("in", "out", "up", "dn", "v", "pw")}

    def cast(dst, src_, key, bufs=()):
        s = cast_sems[key]
        tok = P.emit("pool", lambda E: E.dma_start(out=dst, in_=src_), sem=s, inc=16)
        for b in bufs:
            b.w = tok
        if key in WSRC:
            WSRC[key].w = tok
        return tok

    def pmemset(buf, ap, val):
        P.emit("pool", lambda E: E.memset(ap, val), writes=[buf])

    cast(V.ap, vecs, "v", [V])
    cast(POOLW.ap, pool_w.rearrange("g c d -> c g d"), "pw", [POOLW])
    win_r = w_in.rearrange("(k p) n -> p k n", p=128)
    cast(Win[0], win_r[:, :, 1536:2048], "in")
    for h in range(4):
        for j, base in enumerate((0, 512, 1024, 2048)):
            cast(Win[1 + h][:, :, j * 128:(j + 1) * 128],
                 win_r[:, :, base + h * 128: base + (h + 1) * 128], "in")
    wout_r = w_out.rearrange("(k p) n -> p k n", p=128)
    wup_r = w_up.rearrange("(k p) n -> p k n", p=128)
    wdn_r = w_down.rearrange("(k p) n -> p k n", p=128)
    late_casts = []
    for b in range(2):
        late_casts.append((Wout[b], wout_r[:, :, b * 512:(b + 1) * 512], "out"))
    for b in range(11):
        late_casts.append((Wup[b][:, :, 0:256], wup_r[:, :, b * 256:(b + 1) * 256], "up"))
        late_casts.append((Wup[b][:, :, 256:512], wup_r[:, :, DFF + b * 256: DFF + (b + 1) * 256], "up"))
    for b in range(8):
        late_casts.append((Wdn[b], wdn_r[:, :, b * 128:(b + 1) * 128], "dn"))

    def issue_late_casts(n):
        for _ in range(n):
            if late_casts:
                cast(*late_casts.pop(0))

    pmemset(IDF, IDF.ap[:, 0:128], 0.0)
    P.emit("pool", lambda E: E.affine_select(out=IDF.ap[:, 0:128], in_=IDF.ap[:, 0:128], pattern=[[-1, 128]],
                                             compare_op=ALU.not_equal, fill=1.0, base=0, channel_multiplier=1),
           reads=[IDF], writes=[IDF])
    pmemset(CMASK, CMASK.ap, 1.0)
    P.emit("pool", lambda E: E.affine_select(out=CMASK.ap, in_=CMASK.ap, pattern=[[1, 128]], compare_op=ALU.is_ge,
                                             fill=0.0, base=0, channel_multiplier=-1), reads=[CMASK], writes=[CMASK])
    pmemset(CMASK, CMASK.ap[0:64, 64:128], 0.0)
    pmemset(MSK, MSK.ap, 0.0)
    pmemset(MSK, MSK.ap.rearrange("p (c t) -> p c t", t=64)[:, :, 0:1], 1.0)
    pmemset(ONES_R, ONES_R.ap, 1.0 / 128.0)
    pmemset(ONES_L, ONES_L.ap, 1.0 / 1024.0)
    for g in range(4):
        w = 2 ** (g + 1)
        pmemset(WC, WC.ap[:, g, :], 1.0)
        for t_ in range(w - 1):
            pmemset(WC, WC.ap[:, g, t_:t_ + 1], float(w) / float(t_ + 1))
        pmemset(PHALO[g], PHALO[g].ap, 0.0)
    for h in range(4):
        pmemset(S32[h], S32[h].ap, 0.0)
    pmemset(KHE, KHE.ap, 0.0)
    pmemset(KHO, KHO.ap, 0.0)
    pmemset(PTMP, PTMP.ap, 0.0)
    issue_late_casts(2)

    P.emit("dve", lambda E: E.tensor_copy(out=IDENT.ap, in_=IDF.ap[:, 0:128]), reads=[IDF], writes=[IDENT])
    P.emit("dve", lambda E: E.tensor_tensor(out=LBD.ap, in0=V.ap[:, C_LB0:C_LB0 + 4], in1=V.ap[:, C_LB1:C_LB1 + 4],
                                            op=ALU.subtract), reads=[V], writes=[LBD])
    P.emit("act", lambda E: E.activation(out=LBV.ap, in_=LBD.ap, func=AF.Sigmoid), reads=[LBD], writes=[LBV])
    P.emit("act", lambda E: E.activation(out=OML.ap, in_=LBD.ap, func=AF.Sigmoid, scale=-1.0),
           reads=[LBD], writes=[OML])

    P.emit("dve", lambda E: E.tensor_scalar(out=A1.ap, in0=OML.ap, scalar1=0.5, scalar2=None, op0=ALU.mult),
           reads=[OML], writes=[A1])
    P.emit("dve", lambda E: E.tensor_tensor(out=A2.ap, in0=LBV.ap, in1=A1.ap, op=ALU.add),
           reads=[LBV, A1], writes=[A2])
    P.emit("dve", lambda E: E.tensor_scalar(out=GNH.ap, in0=V.ap[:, C_GN:C_GN + 1], scalar1=0.5, scalar2=None,
                                            op0=ALU.mult), reads=[V], writes=[GNH])

    def psc_op(g):
        P.emit("dve", lambda E: E.tensor_scalar(out=PSC.ap[:, g:g + 1], in0=V.ap[:, C_PS + g:C_PS + g + 1],
                                                scalar1=1.0 / (2 ** (g + 1)), scalar2=None, op0=ALU.mult),
               reads=[V], writes=[PSC])
    for g in range(4):
        psc_op(g)

    ring_state = {"next_load": 0, "next_use": 0}
    slot_of = {}

    def _load_into(slot):
        i = ring_state["next_load"]
        if i >= len(order):
            return
        ring_state["next_load"] += 1
        key, b = order[i]
        assert not any(c[2] == key for c in late_casts), "ring load emitted before all casts of %s" % key
        n = WN[key]
        src = WAP[key][b].rearrange("p k n -> p (k n)")
        slot_of[i] = slot
        P.emit("sp", lambda E: E.dma_start(out=RING[slot].ap[:, 0:n], in_=src),
               reads=[WSRC[key]], writes=[RING[slot]], sem=RSEM[slot], inc=16)

    def ring_init():
        if record:
            return
        for s in range(NR):
            _load_into(s)

    def ring_release(rb):
        if record:
            return
        _load_into(RING.index(rb))

    def ring_take(key, b):
        i = ring_state["next_use"]
        ring_state["next_use"] += 1
        if record:
            take_log.append((key, b))
            return RING[0]
        assert order[i] == (key, b), (i, order[i], key, b)
        return RING[slot_of[i]]

    def x_load(t):
        b = t % 2
        P.emit("sp", lambda E: E.dma_start(out=XF_full[b],
                                           in_=xT[:, t * TT:(t + 1) * TT].rearrange("(k p) n -> p k n", p=128)),
               writes=XFc[b], sem=XSEM[b], inc=16)

    def xb_load(t):
        P.emit("pool", lambda E: E.dma_start(out=XB_full,
                                             in_=xT[:, t * TT:(t + 1) * TT].rearrange("(k p) n -> p k n", p=128)),
               writes=XBc, sem=XBSEM, inc=16)

    def mm_group(out_ap, pairs, reads, writes):
        def fn(E):
            n = len(pairs)
            inst = None
            for i, (l, r) in enumerate(pairs):
                inst = E.matmul(out_ap, lhsT=l, rhs=r, start=(i == 0), stop=(i == n - 1))
            return inst
        return P.emit("pe", fn, reads=reads, writes=writes)

    def c3(ap):
        return ap.rearrange("p (c t) -> p c t", t=64)

    def vcol(c):
        return V.ap[:, c:c + 1]

    def ln_phase1(src_c, dst_c, j, ps):
        P.emit("dve", lambda E: E.scalar_tensor_tensor(
            out=dst_c[j].ap, in0=src_c[j].ap, scalar=ALPHA, in1=ps.ap, op0=ALU.mult, op1=ALU.add),
            reads=[ps, src_c[j]], writes=[dst_c[j]])

    def ln_stat(j, zb, zs):
        P.emit("pe", lambda E: E.matmul(PSL[0].ap, lhsT=ONES_L.ap, rhs=zb.ap, start=(j == 0), stop=(j == 7)),
               reads=[zb, ONES_L], writes=[PSL[0]] if j == 0 else [])
        tok = P.emit("pe", lambda E: E.matmul(PSL[1].ap, lhsT=ONES_L.ap, rhs=zs.ap, start=(j == 0), stop=(j == 7)),
                     reads=[zs, ONES_L], writes=[PSL[1]] if j == 0 else [])
        if j == 7:
            PSL[0].w = tok
            PSL[1].w = tok

    def ln_phase2_chunk(z_c, j):
        zb, zs = ZB[j % 2], ZSQ[j % 2]
        P.emit("act", lambda E: E.activation(out=zb.ap, in_=z_c[j].ap, func=AF.Copy), reads=[z_c[j]], writes=[zb])
        P.emit("act", lambda E: E.activation(out=zs.ap, in_=z_c[j].ap, func=AF.Square), reads=[z_c[j]], writes=[zs])
        ln_stat(j, zb, zs)

    def ln_rstd():
        P.emit("act", lambda E: E.activation(out=M2.ap, in_=PSL[0].ap, func=AF.Square), reads=[PSL[0]], writes=[M2])
        P.emit("dve", lambda E: E.tensor_tensor(out=VAR.ap, in0=PSL[1].ap, in1=M2.ap, op=ALU.subtract),
               reads=[PSL[1], M2], writes=[VAR])
        P.emit("dve", lambda E: E.tensor_scalar(out=VAR.ap, in0=VAR.ap, scalar1=LN_EPS, scalar2=None, op0=ALU.add),
               reads=[VAR], writes=[VAR])
        P.emit("act", lambda E: E.activation(out=VAR.ap, in_=VAR.ap, func=AF.Ln), reads=[VAR], writes=[VAR])
        P.emit("act", lambda E: E.activation(out=PSL[1].ap, in_=VAR.ap, func=AF.Exp, scale=-0.5),
               reads=[VAR], writes=[PSL[1]])

    def ln_out_chunk(z_c, j, gcol, bcol, outs):
        lt = LT[j % 2]
        P.emit("dve", lambda E: E.tensor_tensor(out=lt.ap, in0=z_c[j].ap, in1=PSL[0].ap, op=ALU.subtract),
               reads=[z_c[j], PSL[0]], writes=[lt])
        P.emit("dve", lambda E: E.tensor_tensor(out=lt.ap, in0=lt.ap, in1=PSL[1].ap, op=ALU.mult),
               reads=[lt, PSL[1]], writes=[lt])
        for (ob, oap) in outs:
            P.emit("act", lambda E, oap=oap: E.activation(out=oap, in_=lt.ap, func=AF.Identity,
                                                          scale=vcol(gcol + j), bias=vcol(bcol + j)),
                   reads=[lt, V], writes=[ob])

    def ln_finish_gen(z_c, gcol, bcol, outs_fn, after_chunk=None):
        for j in range(8):
            ln_phase2_chunk(z_c, j)
            yield
        ln_rstd()
        yield
        for j in range(8):
            ln_out_chunk(z_c, j, gcol, bcol, outs_fn(j))
            if after_chunk is not None:
                after_chunk(j)
            yield

    def proj_v_block(tb, w3, rb):
        ps = next_ps()
        mm_group(ps.ap, [(XBc[k].ap[:, tb * 128:(tb + 1) * 128], w3[:, k, :]) for k in range(8)],
                 reads=XBc + [rb], writes=[ps])
        P.emit("act", lambda E: E.activation(out=VT.ap[:, tb, :], in_=ps.ap, func=AF.Copy), reads=[ps], writes=[VT])

    def pool_level(dst, src, lo, sh):
        P.emit("pool", lambda E: E.tensor_tensor(
            out=dst.ap[:, lo:16 + TT], in0=src.ap[:, lo:16 + TT], in1=src.ap[:, lo - sh:16 + TT - sh], op=ALU.add),
            reads=[src], writes=[dst])

    def scan_step(h, c):
        P.emit("dve", lambda E: E.tensor_scalar(out=ST.ap[:, c, :], in0=S32[h].ap, scalar1=EM.ap[:, c:c + 1],
                                                scalar2=None, op0=ALU.mult),
               reads=[S32[h], EM], writes=[ST])
        P.emit("dve", lambda E: E.scalar_tensor_tensor(
            out=S32[h].ap, in0=S32[h].ap, scalar=EL.ap[:, c:c + 1],
            in1=PSL[c // 4].ap[:, (c % 4) * 128:(c % 4 + 1) * 128], op0=ALU.mult, op1=ALU.add),
            reads=[S32[h], EL, PSL[c // 4]], writes=[S32[h]])

    def proj_group(w3, rb, j):
        ps = next_ps()
        mm_group(ps.ap, [(w3[:, k, j * 128:(j + 1) * 128], XBc[k].ap) for k in range(8)],
                 reads=XBc + [rb], writes=[ps])
        return ps

    def head_gen(t, h):
        rb = ring_take("in", 1 + h)
        w3 = rb.ap[:, 0:4096].rearrange("p (k n) -> p k n", n=512)
        wv = float(2 ** (h + 1))
        g = h
        ps_f = proj_group(w3, rb, 2)
        P.emit("act", lambda E: E.activation(out=SG.ap, in_=ps_f.ap, func=AF.Tanh, scale=0.5), reads=[ps_f], writes=[SG])
        yield
        ps_q = proj_group(w3, rb, 1)
        P.emit("act", lambda E: E.activation(out=QS.ap, in_=ps_q.ap, func=AF.Tanh, scale=0.5), reads=[ps_q], writes=[QS])
        P.emit("dve", lambda E: E.scalar_tensor_tensor(out=QS.ap, in0=QS.ap, scalar=1.0, in1=ps_q.ap,
                                                       op0=ALU.add, op1=ALU.mult), reads=[QS, ps_q], writes=[QS])
        yield
        ps_xp = proj_group(w3, rb, 0)
        P.emit("act", lambda E: E.activation(out=XPG.ap[:, 16:16 + TT], in_=ps_xp.ap, func=AF.Copy),
               reads=[ps_xp], writes=[XPG])
        P.emit("act", lambda E: E.activation(out=XW.ap, in_=ps_xp.ap, func=AF.Copy, scale=-wv),
               reads=[ps_xp], writes=[XW])
        yield
        ps_g = proj_group(w3, rb, 3)
        ring_release(rb)
        P.emit("act", lambda E: E.activation(out=GS.ap, in_=ps_g.ap, func=AF.Tanh, scale=0.5), reads=[ps_g], writes=[GS])
        P.emit("dve", lambda E: E.scalar_tensor_tensor(out=GS.ap, in0=GS.ap, scalar=1.0, in1=ps_g.ap,
                                                       op0=ALU.add, op1=ALU.mult), reads=[GS, ps_g], writes=[GS])
        yield

        P.emit("dve", lambda E: E.tensor_scalar(out=SG.ap, in0=SG.ap, scalar1=A1.ap[:, h:h + 1],
                                                scalar2=A2.ap[:, h:h + 1], op0=ALU.mult, op1=ALU.add),
               reads=[SG, A1, A2], writes=[SG])
        P.emit("dve", lambda E: E.tensor_scalar(out=KK.ap, in0=SG.ap, scalar1=-1.0, scalar2=1.0,
                                                op0=ALU.mult, op1=ALU.add), reads=[SG], writes=[KK])
        yield
        P.emit("dve", lambda E: E.tensor_tensor_scan(out=BB.ap, data0=MSK.ap, data1=SG.ap, initial=0.0,
                                                     op0=ALU.max, op1=ALU.mult), reads=[MSK, SG], writes=[BB])
        P.emit("dve", lambda E: E.reciprocal(out=LF.ap, in_=BB.ap), reads=[BB], writes=[LF])
        yield
        P.emit("dve", lambda E: E.tensor_tensor(out=c3(EE.ap), in0=c3(BB.ap),
                                                in1=c3(LF.ap)[:, :, 31:32].to_broadcast([128, 8, 64]),
                                                op=ALU.mult), reads=[BB, LF], writes=[EE])
        P.emit("dve", lambda E: E.tensor_tensor(out=c3(EN.ap), in0=c3(LF.ap),
                                                in1=c3(BB.ap)[:, :, 31:32].to_broadcast([128, 8, 64]),
                                                op=ALU.mult), reads=[BB, LF], writes=[EN])
        yield
        P.emit("pool", lambda E: E.tensor_tensor(out=QT.ap, in0=QS.ap, in1=EE.ap, op=ALU.mult),
               reads=[QS, EE], writes=[QT])
        P.emit("dve", lambda E: E.tensor_tensor(out=KK.ap, in0=KK.ap, in1=EN.ap, op=ALU.mult),
               reads=[KK, EN], writes=[KK])
        P.emit("act", lambda E: E.activation(out=KTB.ap, in_=KK.ap, func=AF.Copy), reads=[KK], writes=[KTB])
        P.emit("dve", lambda E: E.tensor_tensor(out=c3(KH.ap), in0=c3(KK.ap),
                                                in1=c3(EE.ap)[:, :, 63:64].to_broadcast([128, 8, 64]),
                                                op=ALU.mult), reads=[KK, EE], writes=[KH])
        P.emit("dve", lambda E: E.tensor_copy(out=EM.ap, in_=c3(BB.ap)[:, :, 31]), reads=[BB], writes=[EM])
        P.emit("dve", lambda E: E.tensor_copy(out=EL.ap, in_=c3(BB.ap)[:, :, 63]), reads=[BB], writes=[EL])
        yield

        P.emit("pool", lambda E: E.tensor_copy(out=XPG.ap[:, 0:16], in_=PHALO[g].ap), reads=[PHALO[g]], writes=[XPG])
        src = XPG
        bufs = [PA, PB]
        sh = 1
        lo = 0
        for lvl in range(g + 1):
            dst = bufs[lvl % 2]
            lo = lo + sh
            pool_level(dst, src, lo, sh)
            src = dst
            sh *= 2
        ssum = src
        P.emit("pool", lambda E: E.tensor_tensor(out=DT.ap, in0=ssum.ap[:, 16:16 + TT], in1=XW.ap, op=ALU.add),
               reads=[ssum, XW], writes=[DT])
        if t == 0:
            P.emit("pool", lambda E: E.tensor_tensor(out=PTMP.ap, in0=ssum.ap[:, 16:32], in1=WC.ap[:, g, :], op=ALU.mult),
                   reads=[ssum, WC], writes=[PTMP])
            P.emit("pool", lambda E: E.tensor_tensor(out=DT.ap[:, 0:16], in0=PTMP.ap, in1=XW.ap[:, 0:16], op=ALU.add),
                   reads=[PTMP, XW], writes=[DT])
        P.emit("pool", lambda E: E.tensor_copy(out=PHALO[g].ap, in_=XPG.ap[:, TT:TT + 16]), reads=[XPG], writes=[PHALO[g]])
        yield

        psa = next_ps()

        def fn_at(E):
            inst = None
            for tb in range(4):
                inst = E.matmul(psa.ap[:, tb * 128:(tb + 1) * 128], lhsT=KTB.ap[:, tb * 128:(tb + 1) * 128],
                                rhs=QT.ap[:, tb * 128:(tb + 1) * 128], start=True, stop=True)
            return inst
        P.emit("pe", fn_at, reads=[KTB, QT], writes=[psa])
        P.emit("dve", lambda E: E.tensor_tensor(
            out=AT.ap, in0=psa.ap.rearrange("p (b t) -> p b t", t=128),
            in1=CMASK.ap.unsqueeze(1).to_broadcast([128, 4, 128]), op=ALU.mult),
            reads=[psa, CMASK], writes=[AT])

        def fn_tr(E):
            inst = None
            for tb in range(4):
                inst = E.transpose(out=PST.ap[:, tb * 128:(tb + 1) * 128], in_=KH.ap[:, tb * 128:(tb + 1) * 128],
                                   identity=IDENT.ap)
            return inst
        P.emit("pe", fn_tr, reads=[KH, IDENT], writes=[PST])
        P.emit("act", lambda E: E.activation(out=KHE.ap[0:64].rearrange("p b d -> p (b d)"), in_=PST.ap[0:64, 0:512],
                                             func=AF.Copy), reads=[PST], writes=[KHE])
        P.emit("act", lambda E: E.activation(out=KHO.ap[64:128].rearrange("p b d -> p (b d)"),
                                             in_=PST.ap[64:128, 0:512], func=AF.Copy), reads=[PST], writes=[KHO])
        psy = next_ps()
        mm_group(psy.ap, [(POOLW.ap[:, g, :], DT.ap)], reads=[POOLW, DT], writes=[psy])
        P.emit("act", lambda E: E.activation(out=YT[g].ap, in_=psy.ap, func=AF.Identity, scale=PSC.ap[:, g:g + 1]),
               reads=[psy, PSC], writes=[YT[g]])
        yield

        def fn_ds(E):
            inst = None
            for c in range(8):
                tb = c // 2
                kh = KHE if c % 2 == 0 else KHO
                inst = E.matmul(PSL[c // 4].ap[:, (c % 4) * 128:(c % 4 + 1) * 128], lhsT=kh.ap[:, tb, :],
                                rhs=VT.ap[:, tb, h * 128:(h + 1) * 128], start=True, stop=True)
            return inst
        P.emit("pe", fn_ds, reads=[KHE, KHO, VT], writes=[PSL[0], PSL[1]])
        yield
        for c in range(8):
            scan_step(h, c)
            if c % 4 == 3:
                yield
        pso = next_ps()

        def fn_o(E):
            inst = None
            for tb in range(4):
                E.matmul(pso.ap[:, tb * 128:(tb + 1) * 128], lhsT=VT.ap[:, tb, h * 128:(h + 1) * 128],
                         rhs=AT.ap[:, tb, :], start=True, stop=False)
                for c in (2 * tb, 2 * tb + 1):
                    inst = E.matmul(pso.ap[:, c * 64:(c + 1) * 64], lhsT=ST.ap[:, c, :],
                                    rhs=QT.ap[:, c * 64:(c + 1) * 64], start=False, stop=(c == 2 * tb + 1))
            return inst
        P.emit("pe", fn_o, reads=[VT, AT, ST, QT], writes=[pso])
        P.emit("act", lambda E: E.activation(out=O32.ap, in_=pso.ap, func=AF.Copy), reads=[pso], writes=[O32])
        P.emit("act", lambda E: E.activation(out=OSQ.ap, in_=pso.ap, func=AF.Square), reads=[pso], writes=[OSQ])
        yield
        psm = next_ps()
        mm_group(psm.ap, [(ONES_R.ap, OSQ.ap)], reads=[ONES_R, OSQ], writes=[psm])
        P.emit("dve", lambda E: E.tensor_scalar(out=RS.ap, in0=psm.ap, scalar1=4.0 * RMS_EPS, scalar2=None,
                                                op0=ALU.add), reads=[psm], writes=[RS])
        yield
        P.emit("act", lambda E: E.activation(out=RS.ap, in_=RS.ap, func=AF.Ln), reads=[RS], writes=[RS])
        P.emit("act", lambda E: E.activation(out=RS.ap, in_=RS.ap, func=AF.Exp, scale=-0.5), reads=[RS], writes=[RS])
        yield
        P.emit("pool", lambda E: E.tensor_tensor(out=T2.ap, in0=O32.ap, in1=RS.ap, op=ALU.mult),
               reads=[O32, RS], writes=[T2])
        P.emit("dve", lambda E: E.scalar_tensor_tensor(out=YT[4 + h].ap, in0=T2.ap, scalar=GNH.ap[:, 0:1],
                                                       in1=GS.ap, op0=ALU.mult, op1=ALU.mult),
               reads=[T2, GNH, GS], writes=[YT[4 + h]])
        yield

    def x_to_bf16(t):
        xc = XFc[t % 2]
        P.emit("dve", lambda E: E.tensor_copy(out=XB_full[:, 0:4, :], in_=XF_full[t % 2][:, 0:4, :]),
               reads=xc[0:4], writes=XBc[0:4])
        P.emit("act", lambda E: E.activation(out=XB_full[:, 4:8, :], in_=XF_full[t % 2][:, 4:8, :], func=AF.Copy),
               reads=xc[4:8], writes=XBc[4:8])

    def mixer_gen(t):
        x_to_bf16(t)
        yield
        rb = ring_take("in", 0)
        w3 = rb.ap[:, 0:4096].rearrange("p (k n) -> p k n", n=512)
        for tb in range(4):
            proj_v_block(tb, w3, rb)
            if tb == 3:
                ring_release(rb)
            yield
        for h in range(4):
            yield from head_gen(t, h)
            if t == 0 and h < 2:
                issue_late_casts(15)

    def mix_gen(t):
        xc = XFc[t % 2]
        for b in range(2):
            rb = ring_take("out", b)
            w3 = rb.ap[:, 0:4096].rearrange("p (k n) -> p k n", n=512)
            for jj in range(4):
                j = 4 * b + jj
                ps = next_ps()
                mm_group(ps.ap, [(w3[:, k, jj * 128:(jj + 1) * 128], YT[k].ap) for k in range(8)],
                         reads=[rb] + YT, writes=[ps])
                if jj == 3:
                    ring_release(rb)
                ln_phase1(xc, xc, j, ps)
                yield

    def halo_prep(t):
        hs = HS[(t - 1) % 2]
        w1 = V.ap[:, C_CW1:C_CW1 + 2 * NFC]
        w0 = V.ap[:, C_CW0:C_CW0 + 2 * NFC]
        P.emit("dve", lambda E: E.tensor_tensor(out=HW.ap[:, :, 0], in0=hs.ap[:, :, 1], in1=w1, op=ALU.mult),
               reads=[hs, V], writes=[HW])
        P.emit("dve", lambda E: E.tensor_tensor(out=HTMP.ap, in0=hs.ap[:, :, 0], in1=w0, op=ALU.mult),
               reads=[hs, V], writes=[HTMP])
        P.emit("dve", lambda E: E.tensor_tensor(out=HW.ap[:, :, 0], in0=HW.ap[:, :, 0], in1=HTMP.ap, op=ALU.add),
               reads=[HW, HTMP], writes=[HW])
        P.emit("dve", lambda E: E.tensor_tensor(out=HW.ap[:, :, 1], in0=hs.ap[:, :, 1], in1=w0, op=ALU.mult),
               reads=[hs, V], writes=[HW])

    def conv_act(t, ps, acc, ci):
        P.emit("act", lambda E: E.activation(out=acc.ap, in_=ps.ap, func=AF.Identity, scale=vcol(C_CW2 + ci),
                                             bias=vcol(C_CB + ci)), reads=[ps, V], writes=[acc])

    def conv_save(t, ps, ci):
        if t + 1 < NT:
            hs = HS[t % 2]
            P.emit("act", lambda E: E.activation(out=hs.ap[:, ci, :], in_=ps.ap[:, TT - 2:TT], func=AF.Copy),
                   reads=[ps], writes=[hs])

    def conv_dve(t, ps, acc, ci):
        P.emit("dve", lambda E: E.scalar_tensor_tensor(
            out=acc.ap[:, 1:TT], in0=ps.ap[:, 0:TT - 1], scalar=vcol(C_CW1 + ci),
            in1=acc.ap[:, 1:TT], op0=ALU.mult, op1=ALU.add), reads=[ps, acc, V], writes=[acc])
        P.emit("dve", lambda E: E.scalar_tensor_tensor(
            out=acc.ap[:, 2:TT], in0=ps.ap[:, 0:TT - 2], scalar=vcol(C_CW0 + ci),
            in1=acc.ap[:, 2:TT], op0=ALU.mult, op1=ALU.add), reads=[ps, acc, V], writes=[acc])
        if t > 0:
            P.emit("dve", lambda E: E.tensor_tensor(out=acc.ap[:, 0:2], in0=acc.ap[:, 0:2], in1=HW.ap[:, ci, :],
                                                    op=ALU.add), reads=[acc, HW], writes=[acc])

    def gate(j):
        aca, acv, ga = ACA[j % 2], ACV[j % 2], GA[j % 2]
        P.emit("act", lambda E: E.activation(out=ga.ap, in_=aca.ap, func=AF.Gelu_apprx_tanh), reads=[aca], writes=[ga])
        P.emit("pool", lambda E: E.tensor_tensor(out=GT[j].ap, in0=ga.ap, in1=acv.ap, op=ALU.mult),
               reads=[ga, acv], writes=[GT[j]])

    def ffn_pair(t, j, jj, w3, rb, last, prev_gate):
        hb = H1Bc[t % 2]
        psa = next_ps()
        mm_group(psa.ap, [(w3[:, k, jj * 128:(jj + 1) * 128], hb[k].ap) for k in range(8)],
                 reads=hb + [rb], writes=[psa])
        psv = next_ps()
        mm_group(psv.ap, [(w3[:, k, 256 + jj * 128:256 + (jj + 1) * 128], hb[k].ap) for k in range(8)],
                 reads=hb + [rb], writes=[psv])
        if last:
            ring_release(rb)
        aca, acv = ACA[j % 2], ACV[j % 2]
        conv_save(t, psa, j)
        conv_act(t, psa, aca, j)
        conv_save(t, psv, NFC + j)
        conv_act(t, psv, acv, NFC + j)
        if prev_gate is not None:
            gate(prev_gate)
        conv_dve(t, psa, aca, j)
        conv_dve(t, psv, acv, NFC + j)

    def ffn_up_gen(t):
        if t > 0:
            halo_prep(t)
        prev = None
        for b in range(11):
            rb = ring_take("up", b)
            w3 = rb.ap[:, 0:4096].rearrange("p (k n) -> p k n", n=512)
            for jj in range(2):
                j = 2 * b + jj
                ffn_pair(t, j, jj, w3, rb, jj == 1, prev)
                prev = j
                yield
        gate(prev)
        yield

    def ffn_down_gen(t):
        zc = H1Fc[t % 2]
        for j in range(8):
            rb = ring_take("dn", j)
            w3 = rb.ap[:, 0:2816].rearrange("p (k n) -> p k n", n=128)
            ps = next_ps()
            mm_group(ps.ap, [(w3[:, k, :], GT[k].ap) for k in range(NFC)], reads=[rb] + GT, writes=[ps])
            ring_release(rb)
            ln_phase1(zc, zc, j, ps)
            yield

    def ln1_finish_gen(t):
        zc = XFc[t % 2]
        hf, hb = H1Fc[t % 2], H1Bc[t % 2]
        yield from ln_finish_gen(zc, C_L1G, C_L1B, lambda j: [(hf[j], hf[j].ap), (hb[j], hb[j].ap)])
        if DEBUG and t < 2:
            P.emit("sp", lambda E: E.dma_start(out=dbg[t].rearrange("(k p) n -> p k n", p=128), in_=H1F_full[t % 2]),
                   reads=hf, sem=DSEM, inc=16)
        if t + 2 < NT:
            x_load(t + 2)
        yield

    def ln2_finish_gen(t):
        zc = H1Fc[t % 2]
        yield from ln_finish_gen(zc, C_L2G, C_L2B, lambda j: [(zc[j], zc[j].ap)])
        b_ = t % 2
        P.emit("sp", lambda E: E.dma_start(
            out=outT[:, t * TT:(t + 1) * TT].rearrange("(k p) n -> p k n", p=128), in_=H1F_full[b_]),
            reads=zc, sem=OSEM[b_], inc=16)
        yield

    def drain(g):
        for _ in g:
            pass

    def chain(*gens):
        for g in gens:
            yield from g

    def step(g, n=1):
        for _ in range(n):
            try:
                next(g)
            except StopIteration:
                return False
        return True

    def interleave(main, side, k):
        alive = side is not None
        for _ in main:
            if alive:
                alive = step(side, k)
        if alive:
            drain(side)

    ring_init()
    x_load(0)
    x_load(1)

    drain(mixer_gen(0))
    issue_late_casts(100)
    drain(mix_gen(0))
    drain(ln1_finish_gen(0))

    def interleave_keep(main, side, k):
        alive = side is not None
        for _ in main:
            if alive:
                alive = step(side, k)
        return alive

    for t in range(NT):
        sa = []
        if t >= 1:
            sa.append(ln2_finish_gen(t - 1))
        if t + 1 < NT:
            sa += [mixer_gen(t + 1), mix_gen(t + 1)]
        side_a = chain(*sa) if sa else None
        alive = interleave_keep(ffn_up_gen(t), side_a, SIDE_K)
        rest = [side_a] if alive else []
        if t + 1 < NT:
            rest.append(ln1_finish_gen(t + 1))
        interleave(ffn_down_gen(t), chain(*rest) if rest else None, SIDE_K2)
    drain(ln2_finish_gen(NT - 1))

    P.wait_only("sp", [(OSEM[i], OSEM[i].count) for i in range(2)] + ([(DSEM, DSEM.count)] if DEBUG else []))

    if record:
        return take_log

    with nc.Block() as block:
        for ename, attr in ENG_ATTR.items():
            ops = P.ops[ename]

            def body(E, ops=ops):
                for f in ops:
                    f(E)
            getattr(block, attr)(body)
    return nc


def build():
    order = _emit_program(bass.Bass("TRN2", target_bir_lowering=False), None)
    return _emit_program(bass.Bass("TRN2", target_bir_lowering=False), order)


_NC_CACHE = {}


def _pack_vecs(pool_scale, hgrn_lb, hgrn_gnorm, ln1_g, ln1_b, ln2_g, ln2_b, conv_w, conv_b):
    cols = []
    cols.append(np.asarray(pool_scale)[0].reshape(4, 128).T)
    cols.append(np.asarray(hgrn_lb)[0].reshape(4, 128).T)
    cols.append(np.asarray(hgrn_lb)[1].reshape(4, 128).T)
    cols.append(np.asarray(hgrn_gnorm)[0].reshape(1, 128).T)
    for v in (ln1_g, ln1_b, ln2_g, ln2_b):
        cols.append(np.asarray(v)[0].reshape(8, 128).T)
    cw = np.asarray(conv_w)[0]
    for j in range(3):
        cols.append(cw[j].reshape(2 * NFC, 128).T)
    cols.append(np.asarray(conv_b)[0].reshape(2 * NFC, 128).T)
    v = np.ascontiguousarray(np.concatenate(cols, axis=1).astype(np.float32))
    assert v.shape == (128, NV), v.shape
    return v


def kernel(x, w_in, pool_w, pool_scale, hgrn_lb, hgrn_gnorm, w_out, ln1_g, ln1_b,
           w_up, conv_w, conv_b, w_down, ln2_g, ln2_b):
    x = np.asarray(x, dtype=np.float32)
    if "nc" not in _NC_CACHE:
        _NC_CACHE["nc"] = build()
    nc = _NC_CACHE["nc"]
    vecs = _pack_vecs(pool_scale, hgrn_lb, hgrn_gnorm, ln1_g, ln1_b, ln2_g, ln2_b, conv_w, conv_b)
    shared = {
        "w_in": np.ascontiguousarray(np.asarray(w_in, np.float32)[0]),
        "pool_w": np.ascontiguousarray(np.asarray(pool_w, np.float32)[0]),
        "w_out": np.ascontiguousarray(np.asarray(w_out, np.float32)[0]),
        "w_up": np.ascontiguousarray(np.asarray(w_up, np.float32)[0]),
        "w_down": np.ascontiguousarray(np.asarray(w_down, np.float32)[0]),
        "vecs": vecs,
    }
    in_maps = []
    for b in range(8):
        m = dict(shared)
        m["xT"] = np.ascontiguousarray(x[b].T)
        in_maps.append(m)
    res = run_bass_kernel_spmd(nc, in_maps, core_ids=list(range(8)))
    out = np.empty((8, T, D), dtype=np.float32)
    for b in range(8):
        out[b] = res.results[b]["outT"].T
    return out
```

```python
import numpy as np
import concourse.bass as bass
import concourse.mybir as mybir
from concourse.bass_utils import run_bass_kernel_spmd

F32 = mybir.dt.float32
BF16 = mybir.dt.bfloat16
AF = mybir.ActivationFunctionType
ALU = mybir.AluOpType

T = 4096
D = 1024
TT = 512
NT = T // TT
DFF = 2816
NFC = DFF // 128
ALPHA = 2.0 ** 0.25
LN_EPS = 1e-5
RMS_EPS = 1e-6

C_PS = 0
C_LB0 = 4
C_LB1 = 8
C_GN = 12
C_L1G = 13
C_L1B = 21
C_L2G = 29
C_L2B = 37
C_CW0 = 45
C_CW1 = 89
C_CW2 = 133
C_CB = 177
NV = 221

ENG_ATTR = {"pe": "tensor", "act": "scalar", "dve": "vector", "pool": "gpsimd", "sp": "sync"}


class Sem:
    def __init__(self, handle):
        self.handle = handle
        self.count = 0


class Buf:
    def __init__(self, ap, excl=False):
        self.ap = ap
        self.w = None
        self.r = {}
        self.excl = excl

    def __getitem__(self, k):
        return self.ap[k]


class Prog:
    def __init__(self, nc):
        self.nc = nc
        self.ops = {e: [] for e in ENG_ATTR}
        self.sems = {e: Sem(nc.alloc_semaphore("sem_" + e)) for e in ENG_ATTR}
        self.waited = {e: {} for e in ENG_ATTR}

    def new_sem(self, name):
        return Sem(self.nc.alloc_semaphore(name))

    def emit(self, eng, fn, reads=(), writes=(), extra=(), sem=None, inc=1):
        deps = list(extra)
        for b in reads:
            if b.w is not None:
                deps.append(b.w)
            if b.excl:
                deps.extend(tok for s_, tok in b.r.items() if s_ is not self.sems[eng])
        for b in writes:
            if b.w is not None:
                deps.append(b.w)
            deps.extend(b.r.values())
        waits = {}
        wd = self.waited[eng]
        for (s, v) in deps:
            if wd.get(s, 0) < v and waits.get(s, 0) < v:
                waits[s] = v
        for s, v in waits.items():
            wd[s] = v
        if sem is None:
            sem = self.sems[eng]
        sem.count += inc
        tok = (sem, sem.count)
        wl = [(s.handle, v) for s, v in waits.items()]
        h = sem.handle

        def run(E, wl=wl, fn=fn, h=h, inc=inc):
            for (sh, v) in wl:
                E.wait_ge(sh, v)
            fn(E).then_inc(h, inc)

        self.ops[eng].append(run)
        for b in reads:
            old = b.r.get(sem)
            if old is None or old[1] < tok[1]:
                b.r[sem] = tok
        for b in writes:
            b.w = tok
            b.r = {}
        return tok

    def wait_only(self, eng, toks):
        wl = [(s.handle, v) for (s, v) in toks]

        def run(E, wl=wl):
            for (sh, v) in wl:
                E.wait_ge(sh, v)

        self.ops[eng].append(run)


DEBUG = False
SIDE_K = 4
SIDE_K2 = 4


def _emit_program(nc, order):
    P = Prog(nc)
    record = order is None
    take_log = []

    xT = nc.dram_tensor("xT", [D, T], F32, kind="ExternalInput").ap()
    w_in = nc.dram_tensor("w_in", [D, 2560], F32, kind="ExternalInput").ap()
    pool_w = nc.dram_tensor("pool_w", [4, 128, 128], F32, kind="ExternalInput").ap()
    w_out = nc.dram_tensor("w_out", [D, D], F32, kind="ExternalInput").ap()
    w_up = nc.dram_tensor("w_up", [D, 2 * DFF], F32, kind="ExternalInput").ap()
    w_down = nc.dram_tensor("w_down", [DFF, D], F32, kind="ExternalInput").ap()
    vecs = nc.dram_tensor("vecs", [128, NV], F32, kind="ExternalInput").ap()
    outT = nc.dram_tensor("outT", [D, T], F32, kind="ExternalOutput").ap()
    dbg = nc.dram_tensor("dbg", [4, D, TT], F32, kind="ExternalOutput").ap() if DEBUG else None
    DSEM = P.new_sem("dsem") if DEBUG else None

    Win = nc.dram_tensor("Win_s", [5, 128, 8, 512], BF16).ap()
    Wout = nc.dram_tensor("Wout_s", [2, 128, 8, 512], BF16).ap()
    Wup = nc.dram_tensor("Wup_s", [11, 128, 8, 512], BF16).ap()
    Wdn = nc.dram_tensor("Wdn_s", [8, 128, 22, 128], BF16).ap()

    def sb(name, shape, dt=F32):
        return Buf(nc.alloc_sbuf_tensor(name, list(shape), dt).ap())

    def chunks(name, n, dt=F32):
        full = nc.alloc_sbuf_tensor(name, [128, n, TT], dt).ap()
        return full, [Buf(full[:, j, :]) for j in range(n)]

    V = sb("V", [128, NV])
    LBV = sb("LBV", [128, 4])
    OML = sb("OML", [128, 4])
    LBD = sb("LBD", [128, 4])
    PSC = sb("PSC", [128, 4])
    A1 = sb("A1", [128, 4])
    A2 = sb("A2", [128, 4])
    GNH = sb("GNH", [128, 1])
    POOLW = sb("POOLW", [128, 4, 128], BF16)
    IDENT = sb("IDENT", [128, 128], BF16)
    CMASK = sb("CMASK", [128, 128])
    MSK = sb("MSK", [128, TT])
    ONES_R = sb("ONES_R", [128, 128], BF16)
    ONES_L = sb("ONES_L", [128, 128], BF16)
    WC = sb("WC", [128, 4, 16])

    XF_full, XFc = [], []
    for i in range(2):
        f, c = chunks("XF%d" % i, 8)
        XF_full.append(f)
        XFc.append(c)
    XB_full, XBc = chunks("XB", 8, BF16)
    H1B_full, H1Bc, H1F_full, H1Fc = [], [], [], []
    _f, _c = chunks("H1B", 8, BF16)
    for i in range(2):
        H1B_full.append(_f)
        H1Bc.append(_c)
        f, c = chunks("H1F%d" % i, 8)
        H1F_full.append(f)
        H1Fc.append(c)
    RING_E = 4096
    NR = 4
    RING = [sb("RING%d" % i, [128, RING_E], BF16) for i in range(NR)]
    RSEM = [P.new_sem("rsem%d" % i) for i in range(NR)]
    XSEM = [P.new_sem("xsem%d" % i) for i in range(2)]
    XBSEM = P.new_sem("xbsem")
    OSEM = [P.new_sem("osem%d" % i) for i in range(2)]

    XPG = sb("XPG", [128, 16 + TT])
    PA = sb("PA", [128, 16 + TT])
    PB = sb("PB", [128, 16 + TT])
    PTMP = sb("PTMP", [128, 16])
    PHALO = [sb("PHALO%d" % g, [128, 16]) for g in range(4)]
    DT = sb("DT", [128, TT], BF16)
    YT = [sb("YT%d" % k, [128, TT], BF16) for k in range(8)]

    QS = sb("QS", [128, TT])
    SG = sb("SG", [128, TT])
    GS = sb("GS", [128, TT])
    KK = sb("KK", [128, TT])
    LF = sb("LF", [128, TT])
    BB = sb("BB", [128, TT])
    EE = sb("EE", [128, TT])
    EN = sb("EN", [128, TT])
    O32, RS, T2 = LF, BB, EN
    EM = sb("EM", [128, 8])
    EL = sb("EL", [128, 8])
    QT = sb("QT", [128, TT], BF16)
    KTB = sb("KTB", [128, TT], BF16)
    KH = sb("KH", [128, TT], BF16)
    VT = sb("VT", [128, 4, 512], BF16)
    AT = sb("AT", [128, 4, 128], BF16)
    KHE = sb("KHE", [128, 4, 128], BF16)
    KHO = sb("KHO", [128, 4, 128], BF16)
    S32 = [sb("S32_%d" % h, [128, 128]) for h in range(4)]
    S32B = [sb("S32B_%d" % h, [128, 128]) for h in range(4)]
    ST = sb("ST", [128, 8, 128], BF16)
    OSQ = sb("OSQ", [128, TT], BF16)

    ZB = [sb("ZB%d" % i, [128, TT], BF16) for i in range(2)]
    ZSQ = [sb("ZSQ%d" % i, [128, TT], BF16) for i in range(2)]
    LT = [sb("LT%d" % i, [128, TT]) for i in range(2)]
    M2, VAR = LT[0], LT[1]
    IDF = LT[0]

    ACA = [sb("ACA%d" % i, [128, TT]) for i in range(2)]
    ACV = [sb("ACV%d" % i, [128, TT]) for i in range(2)]
    _ga = sb("GA", [128, TT])
    GA = [_ga, _ga]
    GT = [sb("GT%d" % k, [128, TT], BF16) for k in range(NFC)]
    HS = [sb("HS%d" % i, [128, 2 * NFC, 2]) for i in range(2)]
    HW = sb("HW", [128, 2 * NFC, 2])
    HTMP = sb("HTMP", [128, 2 * NFC])

    if record:
        print("sbuf bytes remaining:", nc.sbuf_bytes_remaining)
    PSG = [Buf(nc.alloc_psum_tensor("psg%d" % i, [128, 512], F32).ap(), excl=True) for i in range(5)]
    PSL = [Buf(nc.alloc_psum_tensor("psl%d" % i, [128, 512], F32).ap(), excl=True) for i in range(2)]
    PST = Buf(nc.alloc_psum_tensor("pst", [128, 1024], BF16).ap(), excl=True)
    ps_rr = [0]

    def next_ps():
        b = PSG[ps_rr[0] % 5]
        ps_rr[0] += 1
        return b

    WSRC = {"in": Buf(Win), "out": Buf(Wout), "up": Buf(Wup), "dn": Buf(Wdn)}
    WAP = {"in": Win, "out": Wout, "up": Wup, "dn": Wdn}
    WN = {"in": 4096, "out": 4096, "up": 4096, "dn": 2816}
    cast_sems = {k: P.new_sem("cs_" + k) for k in ("in", "out", "up", "dn", "v", "pw")}

    def cast(dst, src_, key, bufs=()):
        s = cast_sems[key]
        tok = P.emit("pool", lambda E: E.dma_start(out=dst, in_=src_), sem=s, inc=16)
        for b in bufs:
            b.w = tok
        if key in WSRC:
            WSRC[key].w = tok
        return tok

    def pmemset(buf, ap, val):
        P.emit("pool", lambda E: E.memset(ap, val), writes=[buf])

    cast(V.ap, vecs, "v", [V])
    cast(POOLW.ap, pool_w.rearrange("g c d -> c g d"), "pw", [POOLW])
    win_r = w_in.rearrange("(k p) n -> p k n", p=128)
    cast(Win[0], win_r[:, :, 1536:2048], "in")
    for h in range(4):
        for j, base in enumerate((0, 512, 1024, 2048)):
            cast(Win[1 + h][:, :, j * 128:(j + 1) * 128],
                 win_r[:, :, base + h * 128: base + (h + 1) * 128], "in")
    wout_r = w_out.rearrange("(k p) n -> p k n", p=128)
    wup_r = w_up.rearrange("(k p) n -> p k n", p=128)
    wdn_r = w_down.rearrange("(k p) n -> p k n", p=128)
    late_casts = []
    for b in range(2):
        late_casts.append((Wout[b], wout_r[:, :, b * 512:(b + 1) * 512], "out"))
    for b in range(11):
        late_casts.append((Wup[b][:, :, 0:256], wup_r[:, :, b * 256:(b + 1) * 256], "up"))
        late_casts.append((Wup[b][:, :, 256:512], wup_r[:, :, DFF + b * 256: DFF + (b + 1) * 256], "up"))
    for b in range(8):
        late_casts.append((Wdn[b], wdn_r[:, :, b * 128:(b + 1) * 128], "dn"))

    def issue_late_casts(n):
        for _ in range(n):
            if late_casts:
                cast(*late_casts.pop(0))

    pmemset(IDF, IDF.ap[:, 0:128], 0.0)
    P.emit("pool", lambda E: E.affine_select(out=IDF.ap[:, 0:128], in_=IDF.ap[:, 0:128], pattern=[[-1, 128]],
                                             compare_op=ALU.not_equal, fill=1.0, base=0, channel_multiplier=1),
           reads=[IDF], writes=[IDF])
    pmemset(CMASK, CMASK.ap, 1.0)
    P.emit("pool", lambda E: E.affine_select(out=CMASK.ap, in_=CMASK.ap, pattern=[[1, 128]], compare_op=ALU.is_ge,
                                             fill=0.0, base=0, channel_multiplier=-1), reads=[CMASK], writes=[CMASK])
    pmemset(CMASK, CMASK.ap[0:64, 64:128], 0.0)
    pmemset(MSK, MSK.ap, 0.0)
    pmemset(MSK, MSK.ap.rearrange("p (c t) -> p c t", t=64)[:, :, 0:1], 1.0)
    pmemset(ONES_R, ONES_R.ap, 1.0 / 128.0)
    pmemset(ONES_L, ONES_L.ap, 1.0 / 1024.0)
    for g in range(4):
        w = 2 ** (g + 1)
        pmemset(WC, WC.ap[:, g, :], 1.0)
        for t_ in range(w - 1):
            pmemset(WC, WC.ap[:, g, t_:t_ + 1], float(w) / float(t_ + 1))
        pmemset(PHALO[g], PHALO[g].ap, 0.0)
    for h in range(4):
        pmemset(S32[h], S32[h].ap, 0.0)
    pmemset(KHE, KHE.ap, 0.0)
    pmemset(KHO, KHO.ap, 0.0)
    pmemset(PTMP, PTMP.ap, 0.0)
    issue_late_casts(2)

    P.emit("dve", lambda E: E.tensor_copy(out=IDENT.ap, in_=IDF.ap[:, 0:128]), reads=[IDF], writes=[IDENT])
    P.emit("dve", lambda E: E.tensor_tensor(out=LBD.ap, in0=V.ap[:, C_LB0:C_LB0 + 4], in1=V.ap[:, C_LB1:C_LB1 + 4],
                                            op=ALU.subtract), reads=[V], writes=[LBD])
    P.emit("act", lambda E: E.activation(out=LBV.ap, in_=LBD.ap, func=AF.Sigmoid), reads=[LBD], writes=[LBV])
    P.emit("act", lambda E: E.activation(out=OML.ap, in_=LBD.ap, func=AF.Sigmoid, scale=-1.0),
           reads=[LBD], writes=[OML])

    P.emit("dve", lambda E: E.tensor_scalar(out=A1.ap, in0=OML.ap, scalar1=0.5, scalar2=None, op0=ALU.mult),
           reads=[OML], writes=[A1])
    P.emit("dve", lambda E: E.tensor_tensor(out=A2.ap, in0=LBV.ap, in1=A1.ap, op=ALU.add),
           reads=[LBV, A1], writes=[A2])
    P.emit("dve", lambda E: E.tensor_scalar(out=GNH.ap, in0=V.ap[:, C_GN:C_GN + 1], scalar1=0.5, scalar2=None,
                                            op0=ALU.mult), reads=[V], writes=[GNH])

    def psc_op(g):
        P.emit("dve", lambda E: E.tensor_scalar(out=PSC.ap[:, g:g + 1], in0=V.ap[:, C_PS + g:C_PS + g + 1],
                                                scalar1=1.0 / (2 ** (g + 1)), scalar2=None, op0=ALU.mult),
               reads=[V], writes=[PSC])
    for g in range(4):
        psc_op(g)

    ring_state = {"next_load": 0, "next_use": 0}
    slot_of = {}

    def _load_into(slot):
        i = ring_state["next_load"]
        if i >= len(order):
            return
        ring_state["next_load"] += 1
        key, b = order[i]
        assert not any(c[2] == key for c in late_casts), "ring load emitted before all casts of %s" % key
        n = WN[key]
        src = WAP[key][b].rearrange("p k n -> p (k n)")
        slot_of[i] = slot
        P.emit("sp", lambda E: E.dma_start(out=RING[slot].ap[:, 0:n], in_=src),
               reads=[WSRC[key]], writes=[RING[slot]], sem=RSEM[slot], inc=16)

    def ring_init():
        if record:
            return
        for s in range(NR):
            _load_into(s)

    def ring_release(rb):
        if record:
            return
        _load_into(RING.index(rb))

    def ring_take(key, b):
        i = ring_state["next_use"]
        ring_state["next_use"] += 1
        if record:
            take_log.append((key, b))
            return RING[0]
        assert order[i] == (key, b), (i, order[i], key, b)
        return RING[slot_of[i]]

    def x_load(t):
        b = t % 2
        P.emit("sp", lambda E: E.dma_start(out=XF_full[b],
                                           in_=xT[:, t * TT:(t + 1) * TT].rearrange("(k p) n -> p k n", p=128)),
               writes=XFc[b], sem=XSEM[b], inc=16)

    def xb_load(t):
        P.emit("pool", lambda E: E.dma_start(out=XB_full,
                                             in_=xT[:, t * TT:(t + 1) * TT].rearrange("(k p) n -> p k n", p=128)),
               writes=XBc, sem=XBSEM, inc=16)

    def mm_group(out_ap, pairs, reads, writes):
        def fn(E):
            n = len(pairs)
            inst = None
            for i, (l, r) in enumerate(pairs):
                inst = E.matmul(out_ap, lhsT=l, rhs=r, start=(i == 0), stop=(i == n - 1))
            return inst
        return P.emit("pe", fn, reads=reads, writes=writes)

    def c3(ap):
        return ap.rearrange("p (c t) -> p c t", t=64)

    def vcol(c):
        return V.ap[:, c:c + 1]

    def ln_phase1(src_c, dst_c, j, ps):
        P.emit("dve", lambda E: E.scalar_tensor_tensor(
            out=dst_c[j].ap, in0=src_c[j].ap, scalar=ALPHA, in1=ps.ap, op0=ALU.mult, op1=ALU.add),
            reads=[ps, src_c[j]], writes=[dst_c[j]])

    def ln_stat(j, zb, zs):
        P.emit("pe", lambda E: E.matmul(PSL[0].ap, lhsT=ONES_L.ap, rhs=zb.ap, start=(j == 0), stop=(j == 7)),
               reads=[zb, ONES_L], writes=[PSL[0]] if j == 0 else [])
        tok = P.emit("pe", lambda E: E.matmul(PSL[1].ap, lhsT=ONES_L.ap, rhs=zs.ap, start=(j == 0), stop=(j == 7)),
                     reads=[zs, ONES_L], writes=[PSL[1]] if j == 0 else [])
        if j == 7:
            PSL[0].w = tok
            PSL[1].w = tok

    def ln_phase2_chunk(z_c, j):
        zb, zs = ZB[j % 2], ZSQ[j % 2]
        P.emit("act", lambda E: E.activation(out=zb.ap, in_=z_c[j].ap, func=AF.Copy), reads=[z_c[j]], writes=[zb])
        P.emit("act", lambda E: E.activation(out=zs.ap, in_=z_c[j].ap, func=AF.Square), reads=[z_c[j]], writes=[zs])
        ln_stat(j, zb, zs)

    def ln_rstd():
        P.emit("act", lambda E: E.activation(out=M2.ap, in_=PSL[0].ap, func=AF.Square), reads=[PSL[0]], writes=[M2])
        P.emit("dve", lambda E: E.tensor_tensor(out=VAR.ap, in0=PSL[1].ap, in1=M2.ap, op=ALU.subtract),
               reads=[PSL[1], M2], writes=[VAR])
        P.emit("dve", lambda E: E.tensor_scalar(out=VAR.ap, in0=VAR.ap, scalar1=LN_EPS, scalar2=None, op0=ALU.add),
               reads=[VAR], writes=[VAR])
        P.emit("act", lambda E: E.activation(out=VAR.ap, in_=VAR.ap, func=AF.Ln), reads=[VAR], writes=[VAR])
        P.emit("act", lambda E: E.activation(out=PSL[1].ap, in_=VAR.ap, func=AF.Exp, scale=-0.5),
               reads=[VAR], writes=[PSL[1]])

    def ln_out_chunk(z_c, j, gcol, bcol, outs):
        lt = LT[j % 2]
        P.emit("dve", lambda E: E.tensor_tensor(out=lt.ap, in0=z_c[j].ap, in1=PSL[0].ap, op=ALU.subtract),
               reads=[z_c[j], PSL[0]], writes=[lt])
        P.emit("dve", lambda E: E.tensor_tensor(out=lt.ap, in0=lt.ap, in1=PSL[1].ap, op=ALU.mult),
               reads=[lt, PSL[1]], writes=[lt])
        for (ob, oap) in outs:
            P.emit("act", lambda E, oap=oap: E.activation(out=oap, in_=lt.ap, func=AF.Identity,
                                                          scale=vcol(gcol + j), bias=vcol(bcol + j)),
                   reads=[lt, V], writes=[ob])

    def ln_finish_gen(z_c, gcol, bcol, outs_fn, after_chunk=None):
        for j in range(8):
            ln_phase2_chunk(z_c, j)
            yield
        ln_rstd()
        yield
        for j in range(8):
            ln_out_chunk(z_c, j, gcol, bcol, outs_fn(j))
            if after_chunk is not None:
                after_chunk(j)
            yield

    def proj_v_block(tb, w3, rb):
        ps = next_ps()
        mm_group(ps.ap, [(XBc[k].ap[:, tb * 128:(tb + 1) * 128], w3[:, k, :]) for k in range(8)],
                 reads=XBc + [rb], writes=[ps])
        P.emit("act", lambda E: E.activation(out=VT.ap[:, tb, :], in_=ps.ap, func=AF.Copy), reads=[ps], writes=[VT])

    def pool_level(dst, src, lo, sh):
        P.emit("pool", lambda E: E.tensor_tensor(
            out=dst.ap[:, lo:16 + TT], in0=src.ap[:, lo:16 + TT], in1=src.ap[:, lo - sh:16 + TT - sh], op=ALU.add),
            reads=[src], writes=[dst])

    def scan_step(h, c):
        cur, nxt = (S32[h], S32B[h]) if c % 2 == 0 else (S32B[h], S32[h])
        P.emit("pool", lambda E: E.tensor_scalar(out=ST.ap[:, c, :], in0=cur.ap, scalar1=EM.ap[:, c:c + 1],
                                                 scalar2=0.0, op0=ALU.mult, op1=ALU.add),
               reads=[cur, EM], writes=[ST])
        P.emit("dve", lambda E: E.scalar_tensor_tensor(
            out=nxt.ap, in0=cur.ap, scalar=EL.ap[:, c:c + 1],
            in1=PSL[c // 4].ap[:, (c % 4) * 128:(c % 4 + 1) * 128], op0=ALU.mult, op1=ALU.add),
            reads=[cur, EL, PSL[c // 4]], writes=[nxt])

    def proj_group(w3, rb, j):
        ps = next_ps()
        mm_group(ps.ap, [(w3[:, k, j * 128:(j + 1) * 128], XBc[k].ap) for k in range(8)],
                 reads=XBc + [rb], writes=[ps])
        return ps

    def head_gen(t, h):
        rb = ring_take("in", 1 + h)
        w3 = rb.ap[:, 0:4096].rearrange("p (k n) -> p k n", n=512)
        wv = float(2 ** (h + 1))
        g = h
        ps_f = proj_group(w3, rb, 2)
        P.emit("act", lambda E: E.activation(out=SG.ap, in_=ps_f.ap, func=AF.Tanh, scale=0.5), reads=[ps_f], writes=[SG])
        yield
        ps_q = proj_group(w3, rb, 1)
        P.emit("act", lambda E: E.activation(out=QS.ap, in_=ps_q.ap, func=AF.Tanh, scale=0.5), reads=[ps_q], writes=[QS])
        P.emit("dve", lambda E: E.scalar_tensor_tensor(out=QS.ap, in0=QS.ap, scalar=1.0, in1=ps_q.ap,
                                                       op0=ALU.add, op1=ALU.mult), reads=[QS, ps_q], writes=[QS])
        yield
        ps_xp = proj_group(w3, rb, 0)
        P.emit("act", lambda E: E.activation(out=XPG.ap[:, 16:16 + TT], in_=ps_xp.ap, func=AF.Copy),
               reads=[ps_xp], writes=[XPG])
        yield
        ps_g = proj_group(w3, rb, 3)
        ring_release(rb)
        P.emit("act", lambda E: E.activation(out=GS.ap, in_=ps_g.ap, func=AF.Tanh, scale=0.5), reads=[ps_g], writes=[GS])
        P.emit("dve", lambda E: E.scalar_tensor_tensor(out=GS.ap, in0=GS.ap, scalar=1.0, in1=ps_g.ap,
                                                       op0=ALU.add, op1=ALU.mult), reads=[GS, ps_g], writes=[GS])
        yield

        P.emit("dve", lambda E: E.tensor_scalar(out=SG.ap, in0=SG.ap, scalar1=A1.ap[:, h:h + 1],
                                                scalar2=A2.ap[:, h:h + 1], op0=ALU.mult, op1=ALU.add),
               reads=[SG, A1, A2], writes=[SG])
        P.emit("pool", lambda E: E.tensor_scalar(out=KK.ap, in0=SG.ap, scalar1=-1.0, scalar2=1.0,
                                                 op0=ALU.mult, op1=ALU.add), reads=[SG], writes=[KK])
        yield
        P.emit("dve", lambda E: E.tensor_tensor_scan(out=BB.ap, data0=MSK.ap, data1=SG.ap, initial=0.0,
                                                     op0=ALU.max, op1=ALU.mult), reads=[MSK, SG], writes=[BB])
        P.emit("dve", lambda E: E.reciprocal(out=LF.ap, in_=BB.ap), reads=[BB], writes=[LF])
        yield
        P.emit("dve", lambda E: E.tensor_tensor(out=c3(EE.ap), in0=c3(BB.ap),
                                                in1=c3(LF.ap)[:, :, 31:32].to_broadcast([128, 8, 64]),
                                                op=ALU.mult), reads=[BB, LF], writes=[EE])
        P.emit("dve", lambda E: E.tensor_tensor(out=c3(EN.ap), in0=c3(LF.ap),
                                                in1=c3(BB.ap)[:, :, 31:32].to_broadcast([128, 8, 64]),
                                                op=ALU.mult), reads=[BB, LF], writes=[EN])
        yield
        P.emit("pool", lambda E: E.tensor_tensor(out=QT.ap, in0=QS.ap, in1=EE.ap, op=ALU.mult),
               reads=[QS, EE], writes=[QT])
        P.emit("dve", lambda E: E.tensor_tensor(out=KK.ap, in0=KK.ap, in1=EN.ap, op=ALU.mult),
               reads=[KK, EN], writes=[KK])
        P.emit("act", lambda E: E.activation(out=KTB.ap, in_=KK.ap, func=AF.Copy), reads=[KK], writes=[KTB])
        P.emit("dve", lambda E: E.tensor_tensor(out=c3(KH.ap), in0=c3(KK.ap),
                                                in1=c3(EE.ap)[:, :, 63:64].to_broadcast([128, 8, 64]),
                                                op=ALU.mult), reads=[KK, EE], writes=[KH])
        P.emit("pool", lambda E: E.tensor_copy(out=EM.ap, in_=c3(BB.ap)[:, :, 31]), reads=[BB], writes=[EM])
        P.emit("pool", lambda E: E.tensor_copy(out=EL.ap, in_=c3(BB.ap)[:, :, 63]), reads=[BB], writes=[EL])
        yield

        P.emit("pool", lambda E: E.tensor_copy(out=XPG.ap[:, 0:16], in_=PHALO[g].ap), reads=[PHALO[g]], writes=[XPG])
        src = XPG
        bufs = [PA, PB]
        sh = 1
        lo = 0
        for lvl in range(g + 1):
            dst = bufs[lvl % 2]
            lo = lo + sh
            pool_level(dst, src, lo, sh)
            src = dst
            sh *= 2
        ssum = src
        xw = PB if ssum is PA else PA
        P.emit("pool", lambda E: E.tensor_scalar(out=xw.ap[:, 16:16 + TT], in0=XPG.ap[:, 16:16 + TT], scalar1=-wv,
                                                 scalar2=0.0, op0=ALU.mult, op1=ALU.add), reads=[XPG], writes=[xw])
        P.emit("pool", lambda E: E.tensor_tensor(out=DT.ap, in0=ssum.ap[:, 16:16 + TT], in1=xw.ap[:, 16:16 + TT],
                                                 op=ALU.add), reads=[ssum, xw], writes=[DT])
        if t == 0:
            P.emit("pool", lambda E: E.tensor_tensor(out=PTMP.ap, in0=ssum.ap[:, 16:32], in1=WC.ap[:, g, :], op=ALU.mult),
                   reads=[ssum, WC], writes=[PTMP])
            P.emit("pool", lambda E: E.tensor_tensor(out=DT.ap[:, 0:16], in0=PTMP.ap, in1=xw.ap[:, 16:32], op=ALU.add),
                   reads=[PTMP, xw], writes=[DT])
        P.emit("pool", lambda E: E.tensor_copy(out=PHALO[g].ap, in_=XPG.ap[:, TT:TT + 16]), reads=[XPG], writes=[PHALO[g]])
        yield

        psa = next_ps()

        def fn_at(E):
            inst = None
            for tb in range(4):
                inst = E.matmul(psa.ap[:, tb * 128:(tb + 1) * 128], lhsT=KTB.ap[:, tb * 128:(tb + 1) * 128],
                                rhs=QT.ap[:, tb * 128:(tb + 1) * 128], start=True, stop=True)
            return inst
        P.emit("pe", fn_at, reads=[KTB, QT], writes=[psa])
        P.emit("dve", lambda E: E.tensor_tensor(
            out=AT.ap, in0=psa.ap.rearrange("p (b t) -> p b t", t=128),
            in1=CMASK.ap.unsqueeze(1).to_broadcast([128, 4, 128]), op=ALU.mult),
            reads=[psa, CMASK], writes=[AT])

        def fn_tr(E):
            inst = None
            for tb in range(4):
                inst = E.transpose(out=PST.ap[:, tb * 128:(tb + 1) * 128], in_=KH.ap[:, tb * 128:(tb + 1) * 128],
                                   identity=IDENT.ap)
            return inst
        P.emit("pe", fn_tr, reads=[KH, IDENT], writes=[PST])
        P.emit("act", lambda E: E.activation(out=KHE.ap[0:64].rearrange("p b d -> p (b d)"), in_=PST.ap[0:64, 0:512],
                                             func=AF.Copy), reads=[PST], writes=[KHE])
        P.emit("act", lambda E: E.activation(out=KHO.ap[64:128].rearrange("p b d -> p (b d)"),
                                             in_=PST.ap[64:128, 0:512], func=AF.Copy), reads=[PST], writes=[KHO])
        psy = next_ps()
        mm_group(psy.ap, [(POOLW.ap[:, g, :], DT.ap)], reads=[POOLW, DT], writes=[psy])
        P.emit("act", lambda E: E.activation(out=YT[g].ap, in_=psy.ap, func=AF.Identity, scale=PSC.ap[:, g:g + 1]),
               reads=[psy, PSC], writes=[YT[g]])
        yield

        def fn_ds(E):
            inst = None
            for c in range(8):
                tb = c // 2
                kh = KHE if c % 2 == 0 else KHO
                inst = E.matmul(PSL[c // 4].ap[:, (c % 4) * 128:(c % 4 + 1) * 128], lhsT=kh.ap[:, tb, :],
                                rhs=VT.ap[:, tb, h * 128:(h + 1) * 128], start=True, stop=True)
            return inst
        P.emit("pe", fn_ds, reads=[KHE, KHO, VT], writes=[PSL[0], PSL[1]])
        yield
        for c in range(8):
            scan_step(h, c)
            if c % 4 == 3:
                yield
        pso = next_ps()

        def fn_o(E):
            inst = None
            for tb in range(4):
                E.matmul(pso.ap[:, tb * 128:(tb + 1) * 128], lhsT=VT.ap[:, tb, h * 128:(h + 1) * 128],
                         rhs=AT.ap[:, tb, :], start=True, stop=False)
                for c in (2 * tb, 2 * tb + 1):
                    inst = E.matmul(pso.ap[:, c * 64:(c + 1) * 64], lhsT=ST.ap[:, c, :],
                                    rhs=QT.ap[:, c * 64:(c + 1) * 64], start=False, stop=(c == 2 * tb + 1))
            return inst
        P.emit("pe", fn_o, reads=[VT, AT, ST, QT], writes=[pso])
        P.emit("act", lambda E: E.activation(out=O32.ap, in_=pso.ap, func=AF.Copy), reads=[pso], writes=[O32])
        P.emit("act", lambda E: E.activation(out=OSQ.ap, in_=pso.ap, func=AF.Square), reads=[pso], writes=[OSQ])
        yield
        psm = next_ps()
        mm_group(psm.ap, [(ONES_R.ap, OSQ.ap)], reads=[ONES_R, OSQ], writes=[psm])
        P.emit("dve", lambda E: E.tensor_scalar(out=RS.ap, in0=psm.ap, scalar1=4.0 * RMS_EPS, scalar2=None,
                                                op0=ALU.add), reads=[psm], writes=[RS])
        yield
        P.emit("act", lambda E: E.activation(out=RS.ap, in_=RS.ap, func=AF.Ln), reads=[RS], writes=[RS])
        P.emit("act", lambda E: E.activation(out=RS.ap, in_=RS.ap, func=AF.Exp, scale=-0.5), reads=[RS], writes=[RS])
        yield
        P.emit("pool", lambda E: E.tensor_tensor(out=T2.ap, in0=O32.ap, in1=RS.ap, op=ALU.mult),
               reads=[O32, RS], writes=[T2])
        P.emit("dve", lambda E: E.scalar_tensor_tensor(out=YT[4 + h].ap, in0=T2.ap, scalar=GNH.ap[:, 0:1],
                                                       in1=GS.ap, op0=ALU.mult, op1=ALU.mult),
               reads=[T2, GNH, GS], writes=[YT[4 + h]])
        yield

    def x_to_bf16(t):
        xc = XFc[t % 2]
        P.emit("dve", lambda E: E.tensor_copy(out=XB_full[:, 0:4, :], in_=XF_full[t % 2][:, 0:4, :]),
               reads=xc[0:4], writes=XBc[0:4])
        P.emit("act", lambda E: E.activation(out=XB_full[:, 4:8, :], in_=XF_full[t % 2][:, 4:8, :], func=AF.Copy),
               reads=xc[4:8], writes=XBc[4:8])

    def mixer_gen(t):
        x_to_bf16(t)
        yield
        rb = ring_take("in", 0)
        w3 = rb.ap[:, 0:4096].rearrange("p (k n) -> p k n", n=512)
        for tb in range(4):
            proj_v_block(tb, w3, rb)
            if tb == 3:
                ring_release(rb)
            yield
        for h in range(4):
            yield from head_gen(t, h)
            if t == 0 and h < 2:
                issue_late_casts(15)

    def mix_gen(t):
        xc = XFc[t % 2]
        for b in range(2):
            rb = ring_take("out", b)
            w3 = rb.ap[:, 0:4096].rearrange("p (k n) -> p k n", n=512)
            for jj in range(4):
                j = 4 * b + jj
                ps = next_ps()
                mm_group(ps.ap, [(w3[:, k, jj * 128:(jj + 1) * 128], YT[k].ap) for k in range(8)],
                         reads=[rb] + YT, writes=[ps])
                if jj == 3:
                    ring_release(rb)
                ln_phase1(xc, xc, j, ps)
                yield

    def halo_prep(t):
        hs = HS[(t - 1) % 2]
        w1 = V.ap[:, C_CW1:C_CW1 + 2 * NFC]
        w0 = V.ap[:, C_CW0:C_CW0 + 2 * NFC]
        P.emit("dve", lambda E: E.tensor_tensor(out=HW.ap[:, :, 0], in0=hs.ap[:, :, 1], in1=w1, op=ALU.mult),
               reads=[hs, V], writes=[HW])
        P.emit("dve", lambda E: E.tensor_tensor(out=HTMP.ap, in0=hs.ap[:, :, 0], in1=w0, op=ALU.mult),
               reads=[hs, V], writes=[HTMP])
        P.emit("dve", lambda E: E.tensor_tensor(out=HW.ap[:, :, 0], in0=HW.ap[:, :, 0], in1=HTMP.ap, op=ALU.add),
               reads=[HW, HTMP], writes=[HW])
        P.emit("dve", lambda E: E.tensor_tensor(out=HW.ap[:, :, 1], in0=hs.ap[:, :, 1], in1=w0, op=ALU.mult),
               reads=[hs, V], writes=[HW])

    def conv_act(t, ps, acc, ci):
        P.emit("act", lambda E: E.activation(out=acc.ap, in_=ps.ap, func=AF.Identity, scale=vcol(C_CW2 + ci),
                                             bias=vcol(C_CB + ci)), reads=[ps, V], writes=[acc])

    def conv_save(t, ps, ci):
        if t + 1 < NT:
            hs = HS[t % 2]
            P.emit("act", lambda E: E.activation(out=hs.ap[:, ci, :], in_=ps.ap[:, TT - 2:TT], func=AF.Copy),
                   reads=[ps], writes=[hs])

    def conv_dve(t, ps, acc, ci):
        P.emit("dve", lambda E: E.scalar_tensor_tensor(
            out=acc.ap[:, 1:TT], in0=ps.ap[:, 0:TT - 1], scalar=vcol(C_CW1 + ci),
            in1=acc.ap[:, 1:TT], op0=ALU.mult, op1=ALU.add), reads=[ps, acc, V], writes=[acc])
        P.emit("dve", lambda E: E.scalar_tensor_tensor(
            out=acc.ap[:, 2:TT], in0=ps.ap[:, 0:TT - 2], scalar=vcol(C_CW0 + ci),
            in1=acc.ap[:, 2:TT], op0=ALU.mult, op1=ALU.add), reads=[ps, acc, V], writes=[acc])
        if t > 0:
            P.emit("dve", lambda E: E.tensor_tensor(out=acc.ap[:, 0:2], in0=acc.ap[:, 0:2], in1=HW.ap[:, ci, :],
                                                    op=ALU.add), reads=[acc, HW], writes=[acc])

    def gate(j):
        aca, acv, ga = ACA[j % 2], ACV[j % 2], GA[j % 2]
        P.emit("act", lambda E: E.activation(out=ga.ap, in_=aca.ap, func=AF.Gelu_apprx_tanh), reads=[aca], writes=[ga])
        P.emit("pool", lambda E: E.tensor_tensor(out=GT[j].ap, in0=ga.ap, in1=acv.ap, op=ALU.mult),
               reads=[ga, acv], writes=[GT[j]])

    def ffn_pair(t, j, jj, w3, rb, last, prev_gate):
        hb = H1Bc[t % 2]
        psa = next_ps()
        mm_group(psa.ap, [(w3[:, k, jj * 128:(jj + 1) * 128], hb[k].ap) for k in range(8)],
                 reads=hb + [rb], writes=[psa])
        psv = next_ps()
        mm_group(psv.ap, [(w3[:, k, 256 + jj * 128:256 + (jj + 1) * 128], hb[k].ap) for k in range(8)],
                 reads=hb + [rb], writes=[psv])
        if last:
            ring_release(rb)
        aca, acv = ACA[j % 2], ACV[j % 2]
        conv_save(t, psa, j)
        conv_act(t, psa, aca, j)
        conv_save(t, psv, NFC + j)
        conv_act(t, psv, acv, NFC + j)
        if prev_gate is not None:
            gate(prev_gate)
        conv_dve(t, psa, aca, j)
        conv_dve(t, psv, acv, NFC + j)

    def ffn_up_gen(t):
        if t > 0:
            halo_prep(t)
        prev = None
        for b in range(11):
            rb = ring_take("up", b)
            w3 = rb.ap[:, 0:4096].rearrange("p (k n) -> p k n", n=512)
            for jj in range(2):
                j = 2 * b + jj
                ffn_pair(t, j, jj, w3, rb, jj == 1, prev)
                prev = j
                yield
        gate(prev)
        yield

    def ffn_down_gen(t):
        zc = H1Fc[t % 2]
        for j in range(8):
            rb = ring_take("dn", j)
            w3 = rb.ap[:, 0:2816].rearrange("p (k n) -> p k n", n=128)
            ps = next_ps()
            mm_group(ps.ap, [(w3[:, k, :], GT[k].ap) for k in range(NFC)], reads=[rb] + GT, writes=[ps])
            ring_release(rb)
            ln_phase1(zc, zc, j, ps)
            yield

    def ln1_finish_gen(t):
        zc = XFc[t % 2]
        hf, hb = H1Fc[t % 2], H1Bc[t % 2]
        yield from ln_finish_gen(zc, C_L1G, C_L1B, lambda j: [(hf[j], hf[j].ap), (hb[j], hb[j].ap)])
        if DEBUG and t < 2:
            P.emit("sp", lambda E: E.dma_start(out=dbg[t].rearrange("(k p) n -> p k n", p=128), in_=H1F_full[t % 2]),
                   reads=hf, sem=DSEM, inc=16)
        if t + 2 < NT:
            x_load(t + 2)
        yield

    def ln2_finish_gen(t):
        zc = H1Fc[t % 2]
        yield from ln_finish_gen(zc, C_L2G, C_L2B, lambda j: [(zc[j], zc[j].ap)])
        b_ = t % 2
        P.emit("sp", lambda E: E.dma_start(
            out=outT[:, t * TT:(t + 1) * TT].rearrange("(k p) n -> p k n", p=128), in_=H1F_full[b_]),
            reads=zc, sem=OSEM[b_], inc=16)
        yield

    def drain(g):
        for _ in g:
            pass

    def chain(*gens):
        for g in gens:
            yield from g

    def step(g, n=1):
        for _ in range(n):
            try:
                next(g)
            except StopIteration:
                return False
        return True

    def interleave(main, side, k):
        alive = side is not None
        for _ in main:
            if alive:
                alive = step(side, k)
        if alive:
            drain(side)

    ring_init()
    x_load(0)
    x_load(1)

    drain(mixer_gen(0))
    issue_late_casts(100)
    drain(mix_gen(0))
    drain(ln1_finish_gen(0))

    def interleave_keep(main, side, k):
        alive = side is not None
        for _ in main:
            if alive:
                alive = step(side, k)
        return alive

    for t in range(NT):
        sa = []
        if t >= 1:
            sa.append(ln2_finish_gen(t - 1))
        if t + 1 < NT:
            sa += [mixer_gen(t + 1), mix_gen(t + 1)]
        side_a = chain(*sa) if sa else None
        alive = interleave_keep(ffn_up_gen(t), side_a, SIDE_K)
        rest = [side_a] if alive else []
        if t + 1 < NT:
            rest.append(ln1_finish_gen(t + 1))
        interleave(ffn_down_gen(t), chain(*rest) if rest else None, SIDE_K2)
    drain(ln2_finish_gen(NT - 1))

    P.wait_only("sp", [(OSEM[i], OSEM[i].count) for i in range(2)] + ([(DSEM, DSEM.count)] if DEBUG else []))

    if record:
        return take_log

    with nc.Block() as block:
        for ename, attr in ENG_ATTR.items():
            ops = P.ops[ename]

            def body(E, ops=ops):
                for f in ops:
                    f(E)
            getattr(block, attr)(body)
    return nc


def build():
    order = _emit_program(bass.Bass("TRN2", target_bir_lowering=False), None)
    return _emit_program(bass.Bass("TRN2", target_bir_lowering=False), order)


_NC_CACHE = {}


def _pack_vecs(pool_scale, hgrn_lb, hgrn_gnorm, ln1_g, ln1_b, ln2_g, ln2_b, conv_w, conv_b):
    cols = []
    cols.append(np.asarray(pool_scale)[0].reshape(4, 128).T)
    cols.append(np.asarray(hgrn_lb)[0].reshape(4, 128).T)
    cols.append(np.asarray(hgrn_lb)[1].reshape(4, 128).T)
    cols.append(np.asarray(hgrn_gnorm)[0].reshape(1, 128).T)
    for v in (ln1_g, ln1_b, ln2_g, ln2_b):
        cols.append(np.asarray(v)[0].reshape(8, 128).T)
    cw = np.asarray(conv_w)[0]
    for j in range(3):
        cols.append(cw[j].reshape(2 * NFC, 128).T)
    cols.append(np.asarray(conv_b)[0].reshape(2 * NFC, 128).T)
    v = np.ascontiguousarray(np.concatenate(cols, axis=1).astype(np.float32))
    assert v.shape == (128, NV), v.shape
    return v


def kernel(x, w_in, pool_w, pool_scale, hgrn_lb, hgrn_gnorm, w_out, ln1_g, ln1_b,
           w_up, conv_w, conv_b, w_down, ln2_g, ln2_b):
    x = np.asarray(x, dtype=np.float32)
    if "nc" not in _NC_CACHE:
        _NC_CACHE["nc"] = build()
    nc = _NC_CACHE["nc"]
    vecs = _pack_vecs(pool_scale, hgrn_lb, hgrn_gnorm, ln1_g, ln1_b, ln2_g, ln2_b, conv_w, conv_b)
    shared = {
        "w_in": np.ascontiguousarray(np.asarray(w_in, np.float32)[0]),
        "pool_w": np.ascontiguousarray(np.asarray(pool_w, np.float32)[0]),
        "w_out": np.ascontiguousarray(np.asarray(w_out, np.float32)[0]),
        "w_up": np.ascontiguousarray(np.asarray(w_up, np.float32)[0]),
        "w_down": np.ascontiguousarray(np.asarray(w_down, np.float32)[0]),
        "vecs": vecs,
    }
    in_maps = []
    for b in range(8):
        m = dict(shared)
        m["xT"] = np.ascontiguousarray(x[b].T)
        in_maps.append(m)
    res = run_bass_kernel_spmd(nc, in_maps, core_ids=list(range(8)))
    out = np.empty((8, T, D), dtype=np.float32)
    for b in range(8):
        out[b] = res.results[b]["outT"].T
    return out
```

```python
import numpy as np
import concourse.bass as bass
import concourse.mybir as mybir
from concourse.bass_utils import run_bass_kernel_spmd

F32 = mybir.dt.float32
BF16 = mybir.dt.bfloat16
AF = mybir.ActivationFunctionType
ALU = mybir.AluOpType

T = 4096
D = 1024
TT = 512
NT = T // TT
DFF = 2816
NFC = DFF // 128
ALPHA = 2.0 ** 0.25
LN_EPS = 1e-5
RMS_EPS = 1e-6

C_PS = 0
C_LB0 = 4
C_LB1 = 8
C_GN = 12
C_L1G = 13
C_L1B = 21
C_L2G = 29
C_L2B = 37
C_CW0 = 45
C_CW1 = 89
C_CW2 = 133
C_CB = 177
NV = 221

ENG_ATTR = {"pe": "tensor", "act": "scalar", "dve": "vector", "pool": "gpsimd", "sp": "sync"}


class Sem:
    def __init__(self, handle):
        self.handle = handle
        self.count = 0


class Buf:
    def __init__(self, ap, excl=False):
        self.ap = ap
        self.w = None
        self.r = {}
        self.excl = excl

    def __getitem__(self, k):
        return self.ap[k]


class Prog:
    def __init__(self, nc):
        self.nc = nc
        self.ops = {e: [] for e in ENG_ATTR}
        self.sems = {e: Sem(nc.alloc_semaphore("sem_" + e)) for e in ENG_ATTR}
        self.waited = {e: {} for e in ENG_ATTR}

    def new_sem(self, name):
        return Sem(self.nc.alloc_semaphore(name))

    def emit(self, eng, fn, reads=(), writes=(), extra=(), sem=None, inc=1):
        deps = list(extra)
        for b in reads:
            if b.w is not None:
                deps.append(b.w)
            if b.excl:
                deps.extend(tok for s_, tok in b.r.items() if s_ is not self.sems[eng])
        for b in writes:
            if b.w is not None:
                deps.append(b.w)
            deps.extend(b.r.values())
        waits = {}
        wd = self.waited[eng]
        for (s, v) in deps:
            if wd.get(s, 0) < v and waits.get(s, 0) < v:
                waits[s] = v
        for s, v in waits.items():
            wd[s] = v
        if sem is None:
            sem = self.sems[eng]
        sem.count += inc
        tok = (sem, sem.count)
        wl = [(s.handle, v) for s, v in waits.items()]
        h = sem.handle

        def run(E, wl=wl, fn=fn, h=h, inc=inc):
            for (sh, v) in wl:
                E.wait_ge(sh, v)
            fn(E).then_inc(h, inc)

        self.ops[eng].append(run)
        for b in reads:
            old = b.r.get(sem)
            if old is None or old[1] < tok[1]:
                b.r[sem] = tok
        for b in writes:
            b.w = tok
            b.r = {}
        return tok

    def wait_only(self, eng, toks):
        wl = [(s.handle, v) for (s, v) in toks]

        def run(E, wl=wl):
            for (sh, v) in wl:
                E.wait_ge(sh, v)

        self.ops[eng].append(run)


DEBUG = False
SIDE_K = 4
SIDE_K2 = 4


def _emit_program(nc, order):
    P = Prog(nc)
    record = order is None
    take_log = []

    xT = nc.dram_tensor("xT", [D, T], F32, kind="ExternalInput").ap()
    w_in = nc.dram_tensor("w_in", [D, 2560], F32, kind="ExternalInput").ap()
    pool_w = nc.dram_tensor("pool_w", [4, 128, 128], F32, kind="ExternalInput").ap()
    w_out = nc.dram_tensor("w_out", [D, D], F32, kind="ExternalInput").ap()
    w_up = nc.dram_tensor("w_up", [D, 2 * DFF], F32, kind="ExternalInput").ap()
    w_down = nc.dram_tensor("w_down", [DFF, D], F32, kind="ExternalInput").ap()
    vecs = nc.dram_tensor("vecs", [128, NV], F32, kind="ExternalInput").ap()
    outT = nc.dram_tensor("outT", [D, T], F32, kind="ExternalOutput").ap()
    dbg = nc.dram_tensor("dbg", [4, D, TT], F32, kind="ExternalOutput").ap() if DEBUG else None
    DSEM = P.new_sem("dsem") if DEBUG else None

    Win = nc.dram_tensor("Win_s", [5, 128, 8, 512], BF16).ap()
    Wout = nc.dram_tensor("Wout_s", [2, 128, 8, 512], BF16).ap()
    Wup = nc.dram_tensor("Wup_s", [11, 128, 8, 512], BF16).ap()
    Wdn = nc.dram_tensor("Wdn_s", [8, 128, 22, 128], BF16).ap()

    def sb(name, shape, dt=F32):
        return Buf(nc.alloc_sbuf_tensor(name, list(shape), dt).ap())

    def chunks(name, n, dt=F32):
        full = nc.alloc_sbuf_tensor(name, [128, n, TT], dt).ap()
        return full, [Buf(full[:, j, :]) for j in range(n)]

    V = sb("V", [128, NV])
    LBV = sb("LBV", [128, 4])
    OML = sb("OML", [128, 4])
    LBD = sb("LBD", [128, 4])
    PSC = sb("PSC", [128, 4])
    A1 = sb("A1", [128, 4])
    A2 = sb("A2", [128, 4])
    GNH = sb("GNH", [128, 1])
    EPSR = sb("EPSR", [128, 1])
    EPSL = sb("EPSL", [128, 1])
    POOLW = sb("POOLW", [128, 4, 128], BF16)
    IDENT = sb("IDENT", [128, 128], BF16)
    CMASK = sb("CMASK", [128, 128])
    MSK = sb("MSK", [128, TT])
    ONES_R = sb("ONES_R", [128, 128], BF16)
    ONES_L = sb("ONES_L", [128, 128], BF16)
    WC = sb("WC", [128, 4, 16])

    XF_full, XFc = [], []
    for i in range(2):
        f, c = chunks("XF%d" % i, 8)
        XF_full.append(f)
        XFc.append(c)
    XB_full, XBc = chunks("XB", 8, BF16)
    H1B_full, H1Bc, H1F_full, H1Fc = [], [], [], []
    _f, _c = chunks("H1B", 8, BF16)
    for i in range(2):
        H1B_full.append(_f)
        H1Bc.append(_c)
        f, c = chunks("H1F%d" % i, 8)
        H1F_full.append(f)
        H1Fc.append(c)
    RING_E = 4096
    NR = 4
    RING = [sb("RING%d" % i, [128, RING_E], BF16) for i in range(NR)]
    RSEM = [P.new_sem("rsem%d" % i) for i in range(NR)]
    XSEM = [P.new_sem("xsem%d" % i) for i in range(2)]
    XBSEM = P.new_sem("xbsem")
    OSEM = [P.new_sem("osem%d" % i) for i in range(2)]

    XPG = sb("XPG", [128, 16 + TT])
    PA = sb("PA", [128, 16 + TT])
    PB = sb("PB", [128, 16 + TT])
    PTMP = sb("PTMP", [128, 16])
    PHALO = [sb("PHALO%d" % g, [128, 16]) for g in range(4)]
    DT = sb("DT", [128, TT], BF16)
    YT = [sb("YT%d" % k, [128, TT], BF16) for k in range(8)]

    QS = sb("QS", [128, TT])
    SG = sb("SG", [128, TT])
    GS = sb("GS", [128, TT])
    KK = sb("KK", [128, TT])
    LF = sb("LF", [128, TT])
    BB = sb("BB", [128, TT])
    EE = sb("EE", [128, TT])
    EN = sb("EN", [128, TT])
    O32, RS, T2 = LF, BB, EN
    EM = sb("EM", [128, 8])
    EL = sb("EL", [128, 8])
    QT = sb("QT", [128, TT], BF16)
    KTB = sb("KTB", [128, TT], BF16)
    KH = sb("KH", [128, TT], BF16)
    VT = sb("VT", [128, 4, 512], BF16)
    AT = sb("AT", [128, 4, 128], BF16)
    KHE = sb("KHE", [128, 4, 128], BF16)
    KHO = sb("KHO", [128, 4, 128], BF16)
    S32 = [sb("S32_%d" % h, [128, 128]) for h in range(4)]
    S32B = [sb("S32B_%d" % h, [128, 128]) for h in range(4)]
    ST = sb("ST", [128, 8, 128], BF16)
    OSQ = sb("OSQ", [128, TT], BF16)

    ZB = [sb("ZB%d" % i, [128, TT], BF16) for i in range(2)]
    ZSQ = [sb("ZSQ%d" % i, [128, TT], BF16) for i in range(2)]
    LT = [sb("LT%d" % i, [128, TT]) for i in range(2)]
    M2, VAR = LT[0], LT[1]
    IDF = LT[0]

    ACA = [sb("ACA%d" % i, [128, TT]) for i in range(2)]
    ACV = [sb("ACV%d" % i, [128, TT]) for i in range(2)]
    _ga = sb("GA", [128, TT])
    GA = [_ga, _ga]
    GT = [sb("GT%d" % k, [128, TT], BF16) for k in range(NFC)]
    HS = [sb("HS%d" % i, [128, 2 * NFC, 2]) for i in range(2)]
    HW = sb("HW", [128, 2 * NFC, 2])
    HTMP = sb("HTMP", [128, 2 * NFC])

    if record:
        print("sbuf bytes remaining:", nc.sbuf_bytes_remaining)
    PSG = [Buf(nc.alloc_psum_tensor("psg%d" % i, [128, 512], F32).ap(), excl=True) for i in range(5)]
    PSL = [Buf(nc.alloc_psum_tensor("psl%d" % i, [128, 512], F32).ap(), excl=True) for i in range(2)]
    PST = Buf(nc.alloc_psum_tensor("pst", [128, 1024], BF16).ap(), excl=True)
    ps_rr = [0]

    def next_ps():
        b = PSG[ps_rr[0] % 5]
        ps_rr[0] += 1
        return b

    WSRC = {"in": Buf(Win), "out": Buf(Wout), "up": Buf(Wup), "dn": Buf(Wdn)}
    WAP = {"in": Win, "out": Wout, "up": Wup, "dn": Wdn}
    WN = {"in": 4096, "out": 4096, "up": 4096, "dn": 2816}
    cast_sems = {k: P.new_sem("cs_" + k) for k in ("in", "out", "up", "dn", "v", "pw")}

    def cast(dst, src_, key, bufs=()):
        s = cast_sems[key]
        tok = P.emit("pool", lambda E: E.dma_start(out=dst, in_=src_), sem=s, inc=16)
        for b in bufs:
            b.w = tok
        if key in WSRC:
            WSRC[key].w = tok
        return tok

    def pmemset(buf, ap, val):
        P.emit("pool", lambda E: E.memset(ap, val), writes=[buf])

    cast(V.ap, vecs, "v", [V])
    cast(POOLW.ap, pool_w.rearrange("g c d -> c g d"), "pw", [POOLW])
    win_r = w_in.rearrange("(k p) n -> p k n", p=128)
    cast(Win[0], win_r[:, :, 1536:2048], "in")
    for h in range(4):
        for j, base in enumerate((0, 512, 1024, 2048)):
            cast(Win[1 + h][:, :, j * 128:(j + 1) * 128],
                 win_r[:, :, base + h * 128: base + (h + 1) * 128], "in")
    wout_r = w_out.rearrange("(k p) n -> p k n", p=128)
    wup_r = w_up.rearrange("(k p) n -> p k n", p=128)
    wdn_r = w_down.rearrange("(k p) n -> p k n", p=128)
    late_casts = []
    for b in range(2):
        late_casts.append((Wout[b], wout_r[:, :, b * 512:(b + 1) * 512], "out"))
    for b in range(11):
        late_casts.append((Wup[b][:, :, 0:256], wup_r[:, :, b * 256:(b + 1) * 256], "up"))
        late_casts.append((Wup[b][:, :, 256:512], wup_r[:, :, DFF + b * 256: DFF + (b + 1) * 256], "up"))
    for b in range(8):
        late_casts.append((Wdn[b], wdn_r[:, :, b * 128:(b + 1) * 128], "dn"))

    def issue_late_casts(n):
        for _ in range(n):
            if late_casts:
                cast(*late_casts.pop(0))

    pmemset(IDF, IDF.ap[:, 0:128], 0.0)
    P.emit("pool", lambda E: E.affine_select(out=IDF.ap[:, 0:128], in_=IDF.ap[:, 0:128], pattern=[[-1, 128]],
                                             compare_op=ALU.not_equal, fill=1.0, base=0, channel_multiplier=1),
           reads=[IDF], writes=[IDF])
    pmemset(CMASK, CMASK.ap, 1.0)
    P.emit("pool", lambda E: E.affine_select(out=CMASK.ap, in_=CMASK.ap, pattern=[[1, 128]], compare_op=ALU.is_ge,
                                             fill=0.0, base=0, channel_multiplier=-1), reads=[CMASK], writes=[CMASK])
    pmemset(CMASK, CMASK.ap[0:64, 64:128], 0.0)
    pmemset(MSK, MSK.ap, 0.0)
    pmemset(MSK, MSK.ap.rearrange("p (c t) -> p c t", t=64)[:, :, 0:1], 1.0)
    pmemset(ONES_R, ONES_R.ap, 1.0 / 128.0)
    pmemset(ONES_L, ONES_L.ap, 1.0 / 1024.0)
    for g in range(4):
        w = 2 ** (g + 1)
        pmemset(WC, WC.ap[:, g, :], 1.0)
        for t_ in range(w - 1):
            pmemset(WC, WC.ap[:, g, t_:t_ + 1], float(w) / float(t_ + 1))
        pmemset(PHALO[g], PHALO[g].ap, 0.0)
    for h in range(4):
        pmemset(S32[h], S32[h].ap, 0.0)
    pmemset(KHE, KHE.ap, 0.0)
    pmemset(KHO, KHO.ap, 0.0)
    pmemset(PTMP, PTMP.ap, 0.0)
    pmemset(EPSR, EPSR.ap, 4.0 * RMS_EPS)
    pmemset(EPSL, EPSL.ap, LN_EPS)
    issue_late_casts(2)

    P.emit("dve", lambda E: E.tensor_copy(out=IDENT.ap, in_=IDF.ap[:, 0:128]), reads=[IDF], writes=[IDENT])
    P.emit("dve", lambda E: E.tensor_tensor(out=LBD.ap, in0=V.ap[:, C_LB0:C_LB0 + 4], in1=V.ap[:, C_LB1:C_LB1 + 4],
                                            op=ALU.subtract), reads=[V], writes=[LBD])
    P.emit("act", lambda E: E.activation(out=LBV.ap, in_=LBD.ap, func=AF.Sigmoid), reads=[LBD], writes=[LBV])
    P.emit("act", lambda E: E.activation(out=OML.ap, in_=LBD.ap, func=AF.Sigmoid, scale=-1.0),
           reads=[LBD], writes=[OML])

    P.emit("dve", lambda E: E.tensor_scalar(out=A1.ap, in0=OML.ap, scalar1=0.5, scalar2=None, op0=ALU.mult),
           reads=[OML], writes=[A1])
    P.emit("dve", lambda E: E.tensor_tensor(out=A2.ap, in0=LBV.ap, in1=A1.ap, op=ALU.add),
           reads=[LBV, A1], writes=[A2])
    P.emit("dve", lambda E: E.tensor_scalar(out=GNH.ap, in0=V.ap[:, C_GN:C_GN + 1], scalar1=0.5, scalar2=None,
                                            op0=ALU.mult), reads=[V], writes=[GNH])

    def psc_op(g):
        P.emit("dve", lambda E: E.tensor_scalar(out=PSC.ap[:, g:g + 1], in0=V.ap[:, C_PS + g:C_PS + g + 1],
                                                scalar1=1.0 / (2 ** (g + 1)), scalar2=None, op0=ALU.mult),
               reads=[V], writes=[PSC])
    for g in range(4):
        psc_op(g)

    ring_state = {"next_load": 0, "next_use": 0}
    slot_of = {}

    def _load_into(slot):
        i = ring_state["next_load"]
        if i >= len(order):
            return
        ring_state["next_load"] += 1
        key, b = order[i]
        assert not any(c[2] == key for c in late_casts), "ring load emitted before all casts of %s" % key
        n = WN[key]
        src = WAP[key][b].rearrange("p k n -> p (k n)")
        slot_of[i] = slot
        P.emit("sp", lambda E: E.dma_start(out=RING[slot].ap[:, 0:n], in_=src),
               reads=[WSRC[key]], writes=[RING[slot]], sem=RSEM[slot], inc=16)

    def ring_init():
        if record:
            return
        for s in range(NR):
            _load_into(s)

    def ring_release(rb):
        if record:
            return
        _load_into(RING.index(rb))

    def ring_take(key, b):
        i = ring_state["next_use"]
        ring_state["next_use"] += 1
        if record:
            take_log.append((key, b))
            return RING[0]
        assert order[i] == (key, b), (i, order[i], key, b)
        return RING[slot_of[i]]

    def x_load(t):
        b = t % 2
        P.emit("sp", lambda E: E.dma_start(out=XF_full[b],
                                           in_=xT[:, t * TT:(t + 1) * TT].rearrange("(k p) n -> p k n", p=128)),
               writes=XFc[b], sem=XSEM[b], inc=16)

    def xb_load(t):
        P.emit("pool", lambda E: E.dma_start(out=XB_full,
                                             in_=xT[:, t * TT:(t + 1) * TT].rearrange("(k p) n -> p k n", p=128)),
               writes=XBc, sem=XBSEM, inc=16)

    def mm_group(out_ap, pairs, reads, writes):
        def fn(E):
            n = len(pairs)
            inst = None
            for i, (l, r) in enumerate(pairs):
                inst = E.matmul(out_ap, lhsT=l, rhs=r, start=(i == 0), stop=(i == n - 1))
            return inst
        return P.emit("pe", fn, reads=reads, writes=writes)

    def c3(ap):
        return ap.rearrange("p (c t) -> p c t", t=64)

    def vcol(c):
        return V.ap[:, c:c + 1]

    def ln_phase1(src_c, dst_c, j, ps):
        P.emit("dve", lambda E: E.scalar_tensor_tensor(
            out=dst_c[j].ap, in0=src_c[j].ap, scalar=ALPHA, in1=ps.ap, op0=ALU.mult, op1=ALU.add),
            reads=[ps, src_c[j]], writes=[dst_c[j]])

    def ln_stat(j, zb, zs):
        P.emit("pe", lambda E: E.matmul(PSL[0].ap, lhsT=ONES_L.ap, rhs=zb.ap, start=(j == 0), stop=(j == 7)),
               reads=[zb, ONES_L], writes=[PSL[0]] if j == 0 else [])
        tok = P.emit("pe", lambda E: E.matmul(PSL[1].ap, lhsT=ONES_L.ap, rhs=zs.ap, start=(j == 0), stop=(j == 7)),
                     reads=[zs, ONES_L], writes=[PSL[1]] if j == 0 else [])
        if j == 7:
            PSL[0].w = tok
            PSL[1].w = tok

    def ln_phase2_chunk(z_c, j):
        zb, zs = ZB[j % 2], ZSQ[j % 2]
        P.emit("act", lambda E: E.activation(out=zb.ap, in_=z_c[j].ap, func=AF.Copy), reads=[z_c[j]], writes=[zb])
        P.emit("act", lambda E: E.activation(out=zs.ap, in_=z_c[j].ap, func=AF.Square), reads=[z_c[j]], writes=[zs])
        ln_stat(j, zb, zs)

    def ln_rstd():
        P.emit("act", lambda E: E.activation(out=M2.ap, in_=PSL[0].ap, func=AF.Square), reads=[PSL[0]], writes=[M2])
        P.emit("dve", lambda E: E.tensor_tensor(out=VAR.ap, in0=PSL[1].ap, in1=M2.ap, op=ALU.subtract),
               reads=[PSL[1], M2], writes=[VAR])
        P.emit("act", lambda E: E.activation(out=VAR.ap, in_=VAR.ap, func=AF.Ln, bias=EPSL.ap[:, 0:1]),
               reads=[VAR, EPSL], writes=[VAR])
        P.emit("act", lambda E: E.activation(out=PSL[1].ap, in_=VAR.ap, func=AF.Exp, scale=-0.5),
               reads=[VAR], writes=[PSL[1]])

    def ln_out_chunk(z_c, j, gcol, bcol, outs):
        lt = LT[j % 2]
        P.emit("dve", lambda E: E.tensor_tensor(out=lt.ap, in0=z_c[j].ap, in1=PSL[0].ap, op=ALU.subtract),
               reads=[z_c[j], PSL[0]], writes=[lt])
        P.emit("dve", lambda E: E.tensor_tensor(out=lt.ap, in0=lt.ap, in1=PSL[1].ap, op=ALU.mult),
               reads=[lt, PSL[1]], writes=[lt])
        for (ob, oap) in outs:
            P.emit("act", lambda E, oap=oap: E.activation(out=oap, in_=lt.ap, func=AF.Identity,
                                                          scale=vcol(gcol + j), bias=vcol(bcol + j)),
                   reads=[lt, V], writes=[ob])

    def ln_finish_gen(z_c, gcol, bcol, outs_fn, after_chunk=None):
        for j in range(8):
            ln_phase2_chunk(z_c, j)
            yield
        ln_rstd()
        yield
        for j in range(8):
            ln_out_chunk(z_c, j, gcol, bcol, outs_fn(j))
            if after_chunk is not None:
                after_chunk(j)
            yield

    def proj_v_block(tb, w3, rb):
        ps = next_ps()
        mm_group(ps.ap, [(XBc[k].ap[:, tb * 128:(tb + 1) * 128], w3[:, k, :]) for k in range(8)],
                 reads=XBc + [rb], writes=[ps])
        P.emit("act", lambda E: E.activation(out=VT.ap[:, tb, :], in_=ps.ap, func=AF.Copy), reads=[ps], writes=[VT])

    def pool_level(dst, src, lo, sh):
        P.emit("pool", lambda E: E.tensor_tensor(
            out=dst.ap[:, lo:16 + TT], in0=src.ap[:, lo:16 + TT], in1=src.ap[:, lo - sh:16 + TT - sh], op=ALU.add),
            reads=[src], writes=[dst])

    def scan_step(h, c):
        cur, nxt = (S32[h], S32B[h]) if c % 2 == 0 else (S32B[h], S32[h])
        P.emit("pool", lambda E: E.tensor_scalar(out=ST.ap[:, c, :], in0=cur.ap, scalar1=EM.ap[:, c:c + 1],
                                                 scalar2=0.0, op0=ALU.mult, op1=ALU.add),
               reads=[cur, EM], writes=[ST])
        P.emit("dve", lambda E: E.scalar_tensor_tensor(
            out=nxt.ap, in0=cur.ap, scalar=EL.ap[:, c:c + 1],
            in1=PSL[c // 4].ap[:, (c % 4) * 128:(c % 4 + 1) * 128], op0=ALU.mult, op1=ALU.add),
            reads=[cur, EL, PSL[c // 4]], writes=[nxt])

    def proj_group(w3, rb, j):
        ps = next_ps()
        mm_group(ps.ap, [(w3[:, k, j * 128:(j + 1) * 128], XBc[k].ap) for k in range(8)],
                 reads=XBc + [rb], writes=[ps])
        return ps

    def head_gen(t, h):
        rb = ring_take("in", 1 + h)
        w3 = rb.ap[:, 0:4096].rearrange("p (k n) -> p k n", n=512)
        wv = float(2 ** (h + 1))
        g = h
        ps_f = proj_group(w3, rb, 2)
        P.emit("act", lambda E: E.activation(out=SG.ap, in_=ps_f.ap, func=AF.Tanh, scale=0.5), reads=[ps_f], writes=[SG])
        yield
        ps_q = proj_group(w3, rb, 1)
        P.emit("act", lambda E: E.activation(out=QS.ap, in_=ps_q.ap, func=AF.Tanh, scale=0.5), reads=[ps_q], writes=[QS])
        P.emit("dve", lambda E: E.scalar_tensor_tensor(out=QS.ap, in0=QS.ap, scalar=1.0, in1=ps_q.ap,
                                                       op0=ALU.add, op1=ALU.mult), reads=[QS, ps_q], writes=[QS])
        yield
        ps_xp = proj_group(w3, rb, 0)
        P.emit("act", lambda E: E.activation(out=XPG.ap[:, 16:16 + TT], in_=ps_xp.ap, func=AF.Copy),
               reads=[ps_xp], writes=[XPG])
        yield
        ps_g = proj_group(w3, rb, 3)
        ring_release(rb)
        P.emit("act", lambda E: E.activation(out=GS.ap, in_=ps_g.ap, func=AF.Tanh, scale=0.5), reads=[ps_g], writes=[GS])
        P.emit("dve", lambda E: E.scalar_tensor_tensor(out=GS.ap, in0=GS.ap, scalar=1.0, in1=ps_g.ap,
                                                       op0=ALU.add, op1=ALU.mult), reads=[GS, ps_g], writes=[GS])
        yield

        P.emit("dve", lambda E: E.tensor_scalar(out=SG.ap, in0=SG.ap, scalar1=A1.ap[:, h:h + 1],
                                                scalar2=A2.ap[:, h:h + 1], op0=ALU.mult, op1=ALU.add),
               reads=[SG, A1, A2], writes=[SG])
        P.emit("pool", lambda E: E.tensor_scalar(out=KK.ap, in0=SG.ap, scalar1=-1.0, scalar2=1.0,
                                                 op0=ALU.mult, op1=ALU.add), reads=[SG], writes=[KK])
        yield
        P.emit("dve", lambda E: E.tensor_tensor_scan(out=BB.ap, data0=MSK.ap, data1=SG.ap, initial=0.0,
                                                     op0=ALU.max, op1=ALU.mult), reads=[MSK, SG], writes=[BB])
        P.emit("dve", lambda E: E.reciprocal(out=LF.ap, in_=BB.ap), reads=[BB], writes=[LF])
        yield
        P.emit("dve", lambda E: E.tensor_tensor(out=c3(EE.ap), in0=c3(BB.ap),
                                                in1=c3(LF.ap)[:, :, 31:32].to_broadcast([128, 8, 64]),
                                                op=ALU.mult), reads=[BB, LF], writes=[EE])
        P.emit("dve", lambda E: E.tensor_tensor(out=c3(EN.ap), in0=c3(LF.ap),
                                                in1=c3(BB.ap)[:, :, 31:32].to_broadcast([128, 8, 64]),
                                                op=ALU.mult), reads=[BB, LF], writes=[EN])
        yield
        P.emit("pool", lambda E: E.tensor_tensor(out=QT.ap, in0=QS.ap, in1=EE.ap, op=ALU.mult),
               reads=[QS, EE], writes=[QT])
        P.emit("dve", lambda E: E.tensor_tensor(out=KK.ap, in0=KK.ap, in1=EN.ap, op=ALU.mult),
               reads=[KK, EN], writes=[KK])
        P.emit("act", lambda E: E.activation(out=KTB.ap, in_=KK.ap, func=AF.Copy), reads=[KK], writes=[KTB])
        P.emit("dve", lambda E: E.tensor_tensor(out=c3(KH.ap), in0=c3(KK.ap),
                                                in1=c3(EE.ap)[:, :, 63:64].to_broadcast([128, 8, 64]),
                                                op=ALU.mult), reads=[KK, EE], writes=[KH])
        P.emit("pool", lambda E: E.tensor_copy(out=EM.ap, in_=c3(BB.ap)[:, :, 31]), reads=[BB], writes=[EM])
        P.emit("pool", lambda E: E.tensor_copy(out=EL.ap, in_=c3(BB.ap)[:, :, 63]), reads=[BB], writes=[EL])
        yield

        P.emit("pool", lambda E: E.tensor_copy(out=XPG.ap[:, 0:16], in_=PHALO[g].ap), reads=[PHALO[g]], writes=[XPG])
        src = XPG
        bufs = [PA, PB]
        sh = 1
        lo = 0
        for lvl in range(g + 1):
            dst = bufs[lvl % 2]
            lo = lo + sh
            pool_level(dst, src, lo, sh)
            src = dst
            sh *= 2
        ssum = src
        xw = PB if ssum is PA else PA
        P.emit("pool", lambda E: E.tensor_scalar(out=xw.ap[:, 16:16 + TT], in0=XPG.ap[:, 16:16 + TT], scalar1=-wv,
                                                 scalar2=0.0, op0=ALU.mult, op1=ALU.add), reads=[XPG], writes=[xw])
        P.emit("pool", lambda E: E.tensor_tensor(out=DT.ap, in0=ssum.ap[:, 16:16 + TT], in1=xw.ap[:, 16:16 + TT],
                                                 op=ALU.add), reads=[ssum, xw], writes=[DT])
        if t == 0:
            P.emit("pool", lambda E: E.tensor_tensor(out=PTMP.ap, in0=ssum.ap[:, 16:32], in1=WC.ap[:, g, :], op=ALU.mult),
                   reads=[ssum, WC], writes=[PTMP])
            P.emit("pool", lambda E: E.tensor_tensor(out=DT.ap[:, 0:16], in0=PTMP.ap, in1=xw.ap[:, 16:32], op=ALU.add),
                   reads=[PTMP, xw], writes=[DT])
        P.emit("pool", lambda E: E.tensor_copy(out=PHALO[g].ap, in_=XPG.ap[:, TT:TT + 16]), reads=[XPG], writes=[PHALO[g]])
        yield

        psa = next_ps()

        def fn_at(E):
            inst = None
            for tb in range(4):
                inst = E.matmul(psa.ap[:, tb * 128:(tb + 1) * 128], lhsT=KTB.ap[:, tb * 128:(tb + 1) * 128],
                                rhs=QT.ap[:, tb * 128:(tb + 1) * 128], start=True, stop=True)
            return inst
        P.emit("pe", fn_at, reads=[KTB, QT], writes=[psa])
        P.emit("dve", lambda E: E.tensor_tensor(
            out=AT.ap, in0=psa.ap.rearrange("p (b t) -> p b t", t=128),
            in1=CMASK.ap.unsqueeze(1).to_broadcast([128, 4, 128]), op=ALU.mult),
            reads=[psa, CMASK], writes=[AT])

        def fn_tr(E):
            inst = None
            for tb in range(4):
                inst = E.transpose(out=PST.ap[:, tb * 128:(tb + 1) * 128], in_=KH.ap[:, tb * 128:(tb + 1) * 128],
                                   identity=IDENT.ap)
            return inst
        P.emit("pe", fn_tr, reads=[KH, IDENT], writes=[PST])
        P.emit("act", lambda E: E.activation(out=KHE.ap[0:64].rearrange("p b d -> p (b d)"), in_=PST.ap[0:64, 0:512],
                                             func=AF.Copy), reads=[PST], writes=[KHE])
        P.emit("act", lambda E: E.activation(out=KHO.ap[64:128].rearrange("p b d -> p (b d)"),
                                             in_=PST.ap[64:128, 0:512], func=AF.Copy), reads=[PST], writes=[KHO])
        psy = next_ps()
        mm_group(psy.ap, [(POOLW.ap[:, g, :], DT.ap)], reads=[POOLW, DT], writes=[psy])
        P.emit("act", lambda E: E.activation(out=YT[g].ap, in_=psy.ap, func=AF.Identity, scale=PSC.ap[:, g:g + 1]),
               reads=[psy, PSC], writes=[YT[g]])
        yield

        def fn_ds(E):
            inst = None
            for c in range(8):
                tb = c // 2
                kh = KHE if c % 2 == 0 else KHO
                inst = E.matmul(PSL[c // 4].ap[:, (c % 4) * 128:(c % 4 + 1) * 128], lhsT=kh.ap[:, tb, :],
                                rhs=VT.ap[:, tb, h * 128:(h + 1) * 128], start=True, stop=True)
            return inst
        P.emit("pe", fn_ds, reads=[KHE, KHO, VT], writes=[PSL[0], PSL[1]])
        yield
        for c in range(8):
            scan_step(h, c)
            if c % 4 == 3:
                yield
        pso = next_ps()

        def fn_o(E):
            inst = None
            for tb in range(4):
                E.matmul(pso.ap[:, tb * 128:(tb + 1) * 128], lhsT=VT.ap[:, tb, h * 128:(h + 1) * 128],
                         rhs=AT.ap[:, tb, :], start=True, stop=False)
                for c in (2 * tb, 2 * tb + 1):
                    inst = E.matmul(pso.ap[:, c * 64:(c + 1) * 64], lhsT=ST.ap[:, c, :],
                                    rhs=QT.ap[:, c * 64:(c + 1) * 64], start=False, stop=(c == 2 * tb + 1))
            return inst
        P.emit("pe", fn_o, reads=[VT, AT, ST, QT], writes=[pso])
        P.emit("act", lambda E: E.activation(out=OSQ.ap, in_=pso.ap, func=AF.Square), reads=[pso], writes=[OSQ])
        P.emit("act", lambda E: E.activation(out=O32.ap, in_=pso.ap, func=AF.Copy), reads=[pso], writes=[O32])
        yield
        psm = next_ps()
        mm_group(psm.ap, [(ONES_R.ap, OSQ.ap)], reads=[ONES_R, OSQ], writes=[psm])
        P.emit("act", lambda E: E.activation(out=RS.ap, in_=psm.ap, func=AF.Ln, bias=EPSR.ap[:, 0:1]),
               reads=[psm, EPSR], writes=[RS])
        yield
        P.emit("act", lambda E: E.activation(out=RS.ap, in_=RS.ap, func=AF.Exp, scale=-0.5), reads=[RS], writes=[RS])
        yield
        P.emit("pool", lambda E: E.tensor_tensor(out=T2.ap, in0=O32.ap, in1=RS.ap, op=ALU.mult),
               reads=[O32, RS], writes=[T2])
        P.emit("dve", lambda E: E.scalar_tensor_tensor(out=YT[4 + h].ap, in0=T2.ap, scalar=GNH.ap[:, 0:1],
                                                       in1=GS.ap, op0=ALU.mult, op1=ALU.mult),
               reads=[T2, GNH, GS], writes=[YT[4 + h]])
        yield

    def x_to_bf16(t):
        xc = XFc[t % 2]
        P.emit("dve", lambda E: E.tensor_copy(out=XB_full[:, 0:4, :], in_=XF_full[t % 2][:, 0:4, :]),
               reads=xc[0:4], writes=XBc[0:4])
        P.emit("act", lambda E: E.activation(out=XB_full[:, 4:8, :], in_=XF_full[t % 2][:, 4:8, :], func=AF.Copy),
               reads=xc[4:8], writes=XBc[4:8])

    def mixer_gen(t):
        x_to_bf16(t)
        yield
        rb = ring_take("in", 0)
        w3 = rb.ap[:, 0:4096].rearrange("p (k n) -> p k n", n=512)
        for tb in range(4):
            proj_v_block(tb, w3, rb)
            if tb == 3:
                ring_release(rb)
            yield
        for h in range(4):
            yield from head_gen(t, h)
            if t == 0 and h < 2:
                issue_late_casts(15)

    def mix_gen(t):
        xc = XFc[t % 2]
        for b in range(2):
            rb = ring_take("out", b)
            w3 = rb.ap[:, 0:4096].rearrange("p (k n) -> p k n", n=512)
            for jj in range(4):
                j = 4 * b + jj
                ps = next_ps()
                mm_group(ps.ap, [(w3[:, k, jj * 128:(jj + 1) * 128], YT[k].ap) for k in range(8)],
                         reads=[rb] + YT, writes=[ps])
                if jj == 3:
                    ring_release(rb)
                ln_phase1(xc, xc, j, ps)
                yield

    def halo_prep(t):
        hs = HS[(t - 1) % 2]
        w1 = V.ap[:, C_CW1:C_CW1 + 2 * NFC]
        w0 = V.ap[:, C_CW0:C_CW0 + 2 * NFC]
        P.emit("dve", lambda E: E.tensor_tensor(out=HW.ap[:, :, 0], in0=hs.ap[:, :, 1], in1=w1, op=ALU.mult),
               reads=[hs, V], writes=[HW])
        P.emit("dve", lambda E: E.tensor_tensor(out=HTMP.ap, in0=hs.ap[:, :, 0], in1=w0, op=ALU.mult),
               reads=[hs, V], writes=[HTMP])
        P.emit("dve", lambda E: E.tensor_tensor(out=HW.ap[:, :, 0], in0=HW.ap[:, :, 0], in1=HTMP.ap, op=ALU.add),
               reads=[HW, HTMP], writes=[HW])
        P.emit("dve", lambda E: E.tensor_tensor(out=HW.ap[:, :, 1], in0=hs.ap[:, :, 1], in1=w0, op=ALU.mult),
               reads=[hs, V], writes=[HW])

    def conv_act(t, ps, acc, ci):
        P.emit("act", lambda E: E.activation(out=acc.ap, in_=ps.ap, func=AF.Identity, scale=vcol(C_CW2 + ci),
                                             bias=vcol(C_CB + ci)), reads=[ps, V], writes=[acc])

    def conv_save(t, ps, ci):
        if t + 1 < NT:
            hs = HS[t % 2]
            P.emit("act", lambda E: E.activation(out=hs.ap[:, ci, :], in_=ps.ap[:, TT - 2:TT], func=AF.Copy),
                   reads=[ps], writes=[hs])

    def conv_dve(t, ps, acc, ci):
        P.emit("dve", lambda E: E.scalar_tensor_tensor(
            out=acc.ap[:, 1:TT], in0=ps.ap[:, 0:TT - 1], scalar=vcol(C_CW1 + ci),
            in1=acc.ap[:, 1:TT], op0=ALU.mult, op1=ALU.add), reads=[ps, acc, V], writes=[acc])
        P.emit("dve", lambda E: E.scalar_tensor_tensor(
            out=acc.ap[:, 2:TT], in0=ps.ap[:, 0:TT - 2], scalar=vcol(C_CW0 + ci),
            in1=acc.ap[:, 2:TT], op0=ALU.mult, op1=ALU.add), reads=[ps, acc, V], writes=[acc])
        if t > 0:
            P.emit("dve", lambda E: E.tensor_tensor(out=acc.ap[:, 0:2], in0=acc.ap[:, 0:2], in1=HW.ap[:, ci, :],
                                                    op=ALU.add), reads=[acc, HW], writes=[acc])

    def gate(j):
        aca, acv, ga = ACA[j % 2], ACV[j % 2], GA[j % 2]
        P.emit("act", lambda E: E.activation(out=ga.ap, in_=aca.ap, func=AF.Gelu_apprx_tanh), reads=[aca], writes=[ga])
        P.emit("pool", lambda E: E.tensor_tensor(out=GT[j].ap, in0=ga.ap, in1=acv.ap, op=ALU.mult),
               reads=[ga, acv], writes=[GT[j]])

    def ffn_pair(t, j, jj, w3, rb, last, prev_gate):
        hb = H1Bc[t % 2]
        psa = next_ps()
        mm_group(psa.ap, [(w3[:, k, jj * 128:(jj + 1) * 128], hb[k].ap) for k in range(8)],
                 reads=hb + [rb], writes=[psa])
        psv = next_ps()
        mm_group(psv.ap, [(w3[:, k, 256 + jj * 128:256 + (jj + 1) * 128], hb[k].ap) for k in range(8)],
                 reads=hb + [rb], writes=[psv])
        if last:
            ring_release(rb)
        aca, acv = ACA[j % 2], ACV[j % 2]
        conv_save(t, psa, j)
        conv_act(t, psa, aca, j)
        conv_save(t, psv, NFC + j)
        conv_act(t, psv, acv, NFC + j)
        if prev_gate is not None:
            gate(prev_gate)
        conv_dve(t, psa, aca, j)
        conv_dve(t, psv, acv, NFC + j)

    def ffn_up_gen(t):
        if t > 0:
            halo_prep(t)
        prev = None
        for b in range(11):
            rb = ring_take("up", b)
            w3 = rb.ap[:, 0:4096].rearrange("p (k n) -> p k n", n=512)
            for jj in range(2):
                j = 2 * b + jj
                ffn_pair(t, j, jj, w3, rb, jj == 1, prev)
                prev = j
                yield
        gate(prev)
        yield

    def ffn_down_gen(t):
        zc = H1Fc[t % 2]
        for j in range(8):
            rb = ring_take("dn", j)
            w3 = rb.ap[:, 0:2816].rearrange("p (k n) -> p k n", n=128)
            ps = next_ps()
            mm_group(ps.ap, [(w3[:, k, :], GT[k].ap) for k in range(NFC)], reads=[rb] + GT, writes=[ps])
            ring_release(rb)
            ln_phase1(zc, zc, j, ps)
            yield

    def ln1_finish_gen(t):
        zc = XFc[t % 2]
        hf, hb = H1Fc[t % 2], H1Bc[t % 2]
        yield from ln_finish_gen(zc, C_L1G, C_L1B, lambda j: [(hf[j], hf[j].ap), (hb[j], hb[j].ap)])
        if DEBUG and t < 2:
            P.emit("sp", lambda E: E.dma_start(out=dbg[t].rearrange("(k p) n -> p k n", p=128), in_=H1F_full[t % 2]),
                   reads=hf, sem=DSEM, inc=16)
        if t + 2 < NT:
            x_load(t + 2)
        yield

    def ln2_finish_gen(t):
        zc = H1Fc[t % 2]
        yield from ln_finish_gen(zc, C_L2G, C_L2B, lambda j: [(zc[j], zc[j].ap)])
        b_ = t % 2
        P.emit("sp", lambda E: E.dma_start(
            out=outT[:, t * TT:(t + 1) * TT].rearrange("(k p) n -> p k n", p=128), in_=H1F_full[b_]),
            reads=zc, sem=OSEM[b_], inc=16)
        yield

    def drain(g):
        for _ in g:
            pass

    def chain(*gens):
        for g in gens:
            yield from g

    def step(g, n=1):
        for _ in range(n):
            try:
                next(g)
            except StopIteration:
                return False
        return True

    def interleave(main, side, k):
        alive = side is not None
        for _ in main:
            if alive:
                alive = step(side, k)
        if alive:
            drain(side)

    ring_init()
    x_load(0)
    x_load(1)

    drain(mixer_gen(0))
    issue_late_casts(100)
    drain(mix_gen(0))
    drain(ln1_finish_gen(0))

    def interleave_keep(main, side, k):
        alive = side is not None
        for _ in main:
            if alive:
                alive = step(side, k)
        return alive

    for t in range(NT):
        sa = []
        if t >= 1:
            sa.append(ln2_finish_gen(t - 1))
        if t + 1 < NT:
            sa += [mixer_gen(t + 1), mix_gen(t + 1)]
        side_a = chain(*sa) if sa else None
        alive = interleave_keep(ffn_up_gen(t), side_a, SIDE_K)
        rest = [side_a] if alive else []
        if t + 1 < NT:
            rest.append(ln1_finish_gen(t + 1))
        interleave(ffn_down_gen(t), chain(*rest) if rest else None, SIDE_K2)
    drain(ln2_finish_gen(NT - 1))

    P.wait_only("sp", [(OSEM[i], OSEM[i].count) for i in range(2)] + ([(DSEM, DSEM.count)] if DEBUG else []))

    if record:
        return take_log

    with nc.Block() as block:
        for ename, attr in ENG_ATTR.items():
            ops = P.ops[ename]

            def body(E, ops=ops):
                for f in ops:
                    f(E)
            getattr(block, attr)(body)
    return nc


def build():
    order = _emit_program(bass.Bass("TRN2", target_bir_lowering=False), None)
    return _emit_program(bass.Bass("TRN2", target_bir_lowering=False), order)


_NC_CACHE = {}


def _pack_vecs(pool_scale, hgrn_lb, hgrn_gnorm, ln1_g, ln1_b, ln2_g, ln2_b, conv_w, conv_b):
    cols = []
    cols.append(np.asarray(pool_scale)[0].reshape(4, 128).T)
    cols.append(np.asarray(hgrn_lb)[0].reshape(4, 128).T)
    cols.append(np.asarray(hgrn_lb)[1].reshape(4, 128).T)
    cols.append(np.asarray(hgrn_gnorm)[0].reshape(1, 128).T)
    for v in (ln1_g, ln1_b, ln2_g, ln2_b):
        cols.append(np.asarray(v)[0].reshape(8, 128).T)
    cw = np.asarray(conv_w)[0]
    for j in range(3):
        cols.append(cw[j].reshape(2 * NFC, 128).T)
    cols.append(np.asarray(conv_b)[0].reshape(2 * NFC, 128).T)
    v = np.ascontiguousarray(np.concatenate(cols, axis=1).astype(np.float32))
    assert v.shape == (128, NV), v.shape
    return v


def kernel(x, w_in, pool_w, pool_scale, hgrn_lb, hgrn_gnorm, w_out, ln1_g, ln1_b,
           w_up, conv_w, conv_b, w_down, ln2_g, ln2_b):
    x = np.asarray(x, dtype=np.float32)
    if "nc" not in _NC_CACHE:
        _NC_CACHE["nc"] = build()
    nc = _NC_CACHE["nc"]
    vecs = _pack_vecs(pool_scale, hgrn_lb, hgrn_gnorm, ln1_g, ln1_b, ln2_g, ln2_b, conv_w, conv_b)
    shared = {
        "w_in": np.ascontiguousarray(np.asarray(w_in, np.float32)[0]),
        "pool_w": np.ascontiguousarray(np.asarray(pool_w, np.float32)[0]),
        "w_out": np.ascontiguousarray(np.asarray(w_out, np.float32)[0]),
        "w_up": np.ascontiguousarray(np.asarray(w_up, np.float32)[0]),
        "w_down": np.ascontiguousarray(np.asarray(w_down, np.float32)[0]),
        "vecs": vecs,
    }
    in_maps = []
    for b in range(8):
        m = dict(shared)
        m["xT"] = np.ascontiguousarray(x[b].T)
        in_maps.append(m)
    res = run_bass_kernel_spmd(nc, in_maps, core_ids=list(range(8)))
    out = np.empty((8, T, D), dtype=np.float32)
    for b in range(8):
        out[b] = res.results[b]["outT"].T
    return out
```

```python
import numpy as np
import concourse.bass as bass
import concourse.mybir as mybir
from concourse.bass_utils import run_bass_kernel_spmd

F32 = mybir.dt.float32
BF16 = mybir.dt.bfloat16
AF = mybir.ActivationFunctionType
ALU = mybir.AluOpType

T = 4096
D = 1024
TT = 512
NT = T // TT
DFF = 2816
NFC = DFF // 128
ALPHA = 2.0 ** 0.25
LN_EPS = 1e-5
RMS_EPS = 1e-6

C_PS = 0
C_LB0 = 4
C_LB1 = 8
C_GN = 12
C_L1G = 13
C_L1B = 21
C_L2G = 29
C_L2B = 37
C_CW0 = 45
C_CW1 = 89
C_CW2 = 133
C_CB = 177
NV = 221

ENG_ATTR = {"pe": "tensor", "act": "scalar", "dve": "vector", "pool": "gpsimd", "sp": "sync"}


class Sem:
    def __init__(self, handle):
        self.handle = handle
        self.count = 0


class Buf:
    def __init__(self, ap, excl=False):
        self.ap = ap
        self.w = None
        self.r = {}
        self.excl = excl

    def __getitem__(self, k):
        return self.ap[k]


class Prog:
    def __init__(self, nc):
        self.nc = nc
        self.ops = {e: [] for e in ENG_ATTR}
        self.sems = {e: Sem(nc.alloc_semaphore("sem_" + e)) for e in ENG_ATTR}
        self.waited = {e: {} for e in ENG_ATTR}

    def new_sem(self, name):
        return Sem(self.nc.alloc_semaphore(name))

    def emit(self, eng, fn, reads=(), writes=(), extra=(), sem=None, inc=1):
        deps = list(extra)
        for b in reads:
            if b.w is not None:
                deps.append(b.w)
            if b.excl:
                deps.extend(tok for s_, tok in b.r.items() if s_ is not self.sems[eng])
        for b in writes:
            if b.w is not None:
                deps.append(b.w)
            deps.extend(b.r.values())
        waits = {}
        wd = self.waited[eng]
        for (s, v) in deps:
            if wd.get(s, 0) < v and waits.get(s, 0) < v:
                waits[s] = v
        for s, v in waits.items():
            wd[s] = v
        if sem is None:
            sem = self.sems[eng]
        sem.count += inc
        tok = (sem, sem.count)
        wl = [(s.handle, v) for s, v in waits.items()]
        h = sem.handle

        def run(E, wl=wl, fn=fn, h=h, inc=inc):
            for (sh, v) in wl:
                E.wait_ge(sh, v)
            fn(E).then_inc(h, inc)

        self.ops[eng].append(run)
        for b in reads:
            old = b.r.get(sem)
            if old is None or old[1] < tok[1]:
                b.r[sem] = tok
        for b in writes:
            b.w = tok
            b.r = {}
        return tok

    def wait_only(self, eng, toks):
        wl = [(s.handle, v) for (s, v) in toks]

        def run(E, wl=wl):
            for (sh, v) in wl:
                E.wait_ge(sh, v)

        self.ops[eng].append(run)


DEBUG = False
SIDE_K = 4
SIDE_K2 = 4


def _emit_program(nc, order):
    P = Prog(nc)
    record = order is None
    take_log = []

    xT = nc.dram_tensor("xT", [D, T], F32, kind="ExternalInput").ap()
    w_in = nc.dram_tensor("w_in", [D, 2560], F32, kind="ExternalInput").ap()
    pool_w = nc.dram_tensor("pool_w", [4, 128, 128], F32, kind="ExternalInput").ap()
    w_out = nc.dram_tensor("w_out", [D, D], F32, kind="ExternalInput").ap()
    w_up = nc.dram_tensor("w_up", [D, 2 * DFF], F32, kind="ExternalInput").ap()
    w_down = nc.dram_tensor("w_down", [DFF, D], F32, kind="ExternalInput").ap()
    vecs = nc.dram_tensor("vecs", [128, NV], F32, kind="ExternalInput").ap()
    outT = nc.dram_tensor("outT", [D, T], F32, kind="ExternalOutput").ap()
    dbg = nc.dram_tensor("dbg", [4, D, TT], F32, kind="ExternalOutput").ap() if DEBUG else None
    DSEM = P.new_sem("dsem") if DEBUG else None

    Win = nc.dram_tensor("Win_s", [5, 128, 8, 512], BF16).ap()
    Wout = nc.dram_tensor("Wout_s", [2, 128, 8, 512], BF16).ap()
    Wup = nc.dram_tensor("Wup_s", [11, 128, 8, 512], BF16).ap()
    Wdn = nc.dram_tensor("Wdn_s", [8, 128, 22, 128], BF16).ap()

    def sb(name, shape, dt=F32):
        return Buf(nc.alloc_sbuf_tensor(name, list(shape), dt).ap())

    def chunks(name, n, dt=F32):
        full = nc.alloc_sbuf_tensor(name, [128, n, TT], dt).ap()
        return full, [Buf(full[:, j, :]) for j in range(n)]

    V = sb("V", [128, NV])
    LBV = sb("LBV", [128, 4])
    OML = sb("OML", [128, 4])
    LBD = sb("LBD", [128, 4])
    PSC = sb("PSC", [128, 4])
    A1 = sb("A1", [128, 4])
    A2 = sb("A2", [128, 4])
    GNH = sb("GNH", [128, 1])
    EPSR = sb("EPSR", [128, 1])
    EPSL = sb("EPSL", [128, 1])
    POOLW = sb("POOLW", [128, 4, 128], BF16)
    IDENT = sb("IDENT", [128, 128], BF16)
    CMASK = sb("CMASK", [128, 128])
    MSK = sb("MSK", [128, TT])
    ONES_R = sb("ONES_R", [128, 128], BF16)
    ONES_L = sb("ONES_L", [128, 128], BF16)
    WC = sb("WC", [128, 4, 16])

    XF_full, XFc = [], []
    for i in range(2):
        f, c = chunks("XF%d" % i, 8)
        XF_full.append(f)
        XFc.append(c)
    XB_full, XBc = chunks("XB", 8, BF16)
    H1B_full, H1Bc, H1F_full, H1Fc = [], [], [], []
    _f, _c = chunks("H1B", 8, BF16)
    for i in range(2):
        H1B_full.append(_f)
        H1Bc.append(_c)
        f, c = chunks("H1F%d" % i, 8)
        H1F_full.append(f)
        H1Fc.append(c)
    RING_E = 4096
    NR = 4
    RING = [sb("RING%d" % i, [128, RING_E], BF16) for i in range(NR)]
    RSEM = [P.new_sem("rsem%d" % i) for i in range(NR)]
    XSEM = [P.new_sem("xsem%d" % i) for i in range(2)]
    XBSEM = P.new_sem("xbsem")
    OSEM = [P.new_sem("osem%d" % i) for i in range(2)]

    XPG = sb("XPG", [128, 16 + TT])
    PA = sb("PA", [128, 16 + TT])
    PB = sb("PB", [128, 16 + TT])
    PTMP = sb("PTMP", [128, 16])
    PHALO = [sb("PHALO%d" % g, [128, 16]) for g in range(4)]
    DT = sb("DT", [128, TT], BF16)
    YT = [sb("YT%d" % k, [128, TT], BF16) for k in range(8)]

    QS = sb("QS", [128, TT])
    SG = sb("SG", [128, TT])
    GS = sb("GS", [128, TT])
    KK = sb("KK", [128, TT])
    LF = sb("LF", [128, TT])
    BB = sb("BB", [128, TT])
    EE = sb("EE", [128, TT])
    EN = sb("EN", [128, TT])
    O32, RS, T2 = LF, BB, EN
    EM = sb("EM", [128, 8])
    EL = sb("EL", [128, 8])
    QT = sb("QT", [128, TT], BF16)
    KTB = sb("KTB", [128, TT], BF16)
    KH = sb("KH", [128, TT], BF16)
    VT = sb("VT", [128, 4, 512], BF16)
    AT = sb("AT", [128, 4, 128], BF16)
    KHE = sb("KHE", [128, 4, 128], BF16)
    KHO = sb("KHO", [128, 4, 128], BF16)
    S32 = [sb("S32_%d" % h, [128, 128]) for h in range(4)]
    S32B = [sb("S32B_%d" % h, [128, 128]) for h in range(4)]
    ST = sb("ST", [128, 8, 128], BF16)
    OSQ = sb("OSQ", [128, TT], BF16)

    ZB = [sb("ZB%d" % i, [128, TT], BF16) for i in range(2)]
    ZSQ = [sb("ZSQ%d" % i, [128, TT], BF16) for i in range(2)]
    LT = [sb("LT%d" % i, [128, TT]) for i in range(2)]
    M2, VAR = LT[0], LT[1]
    IDF = LT[0]

    ACCF = [nc.alloc_sbuf_tensor("ACC%d" % i, [128, 2, TT], F32).ap() for i in range(2)]
    ACA = [Buf(ACCF[i][:, 0, :]) for i in range(2)]
    ACV = [Buf(ACCF[i][:, 1, :]) for i in range(2)]
    _ga = sb("GA", [128, TT])
    GA = [_ga, _ga]
    GT = [sb("GT%d" % k, [128, TT], BF16) for k in range(NFC)]
    HS = [sb("HS%d" % i, [128, 2 * NFC, 2]) for i in range(2)]
    HW = sb("HW", [128, 2 * NFC, 2])
    HTMP = sb("HTMP", [128, 2 * NFC])

    if record:
        print("sbuf bytes remaining:", nc.sbuf_bytes_remaining)
    PSG = [Buf(nc.alloc_psum_tensor("psg%d" % i, [128, 512], F32).ap(), excl=True) for i in range(5)]
    PSL = [Buf(nc.alloc_psum_tensor("psl%d" % i, [128, 512], F32).ap(), excl=True) for i in range(2)]
    PST = Buf(nc.alloc_psum_tensor("pst", [128, 1024], BF16).ap(), excl=True)
    ps_rr = [0]

    def next_ps():
        b = PSG[ps_rr[0] % 5]
        ps_rr[0] += 1
        return b

    WSRC = {"in": Buf(Win), "out": Buf(Wout), "up": Buf(Wup), "dn": Buf(Wdn)}
    WAP = {"in": Win, "out": Wout, "up": Wup, "dn": Wdn}
    WN = {"in": 4096, "out": 4096, "up": 4096, "dn": 2816}
    cast_sems = {k: P.new_sem("cs_" + k) for k in ("in", "out", "up", "dn", "v", "pw")}

    def cast(dst, src_, key, bufs=()):
        s = cast_sems[key]
        tok = P.emit("pool", lambda E: E.dma_start(out=dst, in_=src_), sem=s, inc=16)
        for b in bufs:
            b.w = tok
        if key in WSRC:
            WSRC[key].w = tok
        return tok

    def pmemset(buf, ap, val):
        P.emit("pool", lambda E: E.memset(ap, val), writes=[buf])

    cast(V.ap, vecs, "v", [V])
    cast(POOLW.ap, pool_w.rearrange("g c d -> c g d"), "pw", [POOLW])
    win_r = w_in.rearrange("(k p) n -> p k n", p=128)
    cast(Win[0], win_r[:, :, 1536:2048], "in")
    for h in range(4):
        for j, base in enumerate((0, 512, 1024, 2048)):
            cast(Win[1 + h][:, :, j * 128:(j + 1) * 128],
                 win_r[:, :, base + h * 128: base + (h + 1) * 128], "in")
    wout_r = w_out.rearrange("(k p) n -> p k n", p=128)
    wup_r = w_up.rearrange("(k p) n -> p k n", p=128)
    wdn_r = w_down.rearrange("(k p) n -> p k n", p=128)
    late_casts = []
    for b in range(2):
        late_casts.append((Wout[b], wout_r[:, :, b * 512:(b + 1) * 512], "out"))
    for b in range(11):
        late_casts.append((Wup[b][:, :, 0:256], wup_r[:, :, b * 256:(b + 1) * 256], "up"))
        late_casts.append((Wup[b][:, :, 256:512], wup_r[:, :, DFF + b * 256: DFF + (b + 1) * 256], "up"))
    for b in range(8):
        late_casts.append((Wdn[b], wdn_r[:, :, b * 128:(b + 1) * 128], "dn"))

    def issue_late_casts(n):
        for _ in range(n):
            if late_casts:
                cast(*late_casts.pop(0))

    pmemset(IDF, IDF.ap[:, 0:128], 0.0)
    P.emit("pool", lambda E: E.affine_select(out=IDF.ap[:, 0:128], in_=IDF.ap[:, 0:128], pattern=[[-1, 128]],
                                             compare_op=ALU.not_equal, fill=1.0, base=0, channel_multiplier=1),
           reads=[IDF], writes=[IDF])
    pmemset(CMASK, CMASK.ap, 1.0)
    P.emit("pool", lambda E: E.affine_select(out=CMASK.ap, in_=CMASK.ap, pattern=[[1, 128]], compare_op=ALU.is_ge,
                                             fill=0.0, base=0, channel_multiplier=-1), reads=[CMASK], writes=[CMASK])
    pmemset(CMASK, CMASK.ap[0:64, 64:128], 0.0)
    pmemset(MSK, MSK.ap, 0.0)
    pmemset(MSK, MSK.ap.rearrange("p (c t) -> p c t", t=64)[:, :, 0:1], 1.0)
    pmemset(ONES_R, ONES_R.ap, 1.0 / 128.0)
    pmemset(ONES_L, ONES_L.ap, 1.0 / 1024.0)
    for g in range(4):
        w = 2 ** (g + 1)
        pmemset(WC, WC.ap[:, g, :], 1.0)
        for t_ in range(w - 1):
            pmemset(WC, WC.ap[:, g, t_:t_ + 1], float(w) / float(t_ + 1))
        pmemset(PHALO[g], PHALO[g].ap, 0.0)
    for h in range(4):
        pmemset(S32[h], S32[h].ap, 0.0)
    pmemset(KHE, KHE.ap, 0.0)
    pmemset(KHO, KHO.ap, 0.0)
    pmemset(PTMP, PTMP.ap, 0.0)
    pmemset(EPSR, EPSR.ap, 4.0 * RMS_EPS)
    pmemset(EPSL, EPSL.ap, LN_EPS)
    issue_late_casts(2)

    P.emit("dve", lambda E: E.tensor_copy(out=IDENT.ap, in_=IDF.ap[:, 0:128]), reads=[IDF], writes=[IDENT])
    P.emit("dve", lambda E: E.tensor_tensor(out=LBD.ap, in0=V.ap[:, C_LB0:C_LB0 + 4], in1=V.ap[:, C_LB1:C_LB1 + 4],
                                            op=ALU.subtract), reads=[V], writes=[LBD])
    P.emit("act", lambda E: E.activation(out=LBV.ap, in_=LBD.ap, func=AF.Sigmoid), reads=[LBD], writes=[LBV])
    P.emit("act", lambda E: E.activation(out=OML.ap, in_=LBD.ap, func=AF.Sigmoid, scale=-1.0),
           reads=[LBD], writes=[OML])

    P.emit("dve", lambda E: E.tensor_scalar(out=A1.ap, in0=OML.ap, scalar1=0.5, scalar2=None, op0=ALU.mult),
           reads=[OML], writes=[A1])
    P.emit("dve", lambda E: E.tensor_tensor(out=A2.ap, in0=LBV.ap, in1=A1.ap, op=ALU.add),
           reads=[LBV, A1], writes=[A2])
    P.emit("dve", lambda E: E.tensor_scalar(out=GNH.ap, in0=V.ap[:, C_GN:C_GN + 1], scalar1=0.5, scalar2=None,
                                            op0=ALU.mult), reads=[V], writes=[GNH])

    def psc_op(g):
        P.emit("dve", lambda E: E.tensor_scalar(out=PSC.ap[:, g:g + 1], in0=V.ap[:, C_PS + g:C_PS + g + 1],
                                                scalar1=1.0 / (2 ** (g + 1)), scalar2=None, op0=ALU.mult),
               reads=[V], writes=[PSC])
    for g in range(4):
        psc_op(g)

    ring_state = {"next_load": 0, "next_use": 0}
    slot_of = {}

    def _load_into(slot):
        i = ring_state["next_load"]
        if i >= len(order):
            return
        ring_state["next_load"] += 1
        key, b = order[i]
        assert not any(c[2] == key for c in late_casts), "ring load emitted before all casts of %s" % key
        n = WN[key]
        src = WAP[key][b].rearrange("p k n -> p (k n)")
        slot_of[i] = slot
        P.emit("sp", lambda E: E.dma_start(out=RING[slot].ap[:, 0:n], in_=src),
               reads=[WSRC[key]], writes=[RING[slot]], sem=RSEM[slot], inc=16)

    def ring_init():
        if record:
            return
        for s in range(NR):
            _load_into(s)

    def ring_release(rb):
        if record:
            return
        _load_into(RING.index(rb))

    def ring_take(key, b):
        i = ring_state["next_use"]
        ring_state["next_use"] += 1
        if record:
            take_log.append((key, b))
            return RING[0]
        assert order[i] == (key, b), (i, order[i], key, b)
        return RING[slot_of[i]]

    def x_load(t):
        b = t % 2
        P.emit("sp", lambda E: E.dma_start(out=XF_full[b],
                                           in_=xT[:, t * TT:(t + 1) * TT].rearrange("(k p) n -> p k n", p=128)),
               writes=XFc[b], sem=XSEM[b], inc=16)

    def xb_load(t):
        P.emit("pool", lambda E: E.dma_start(out=XB_full,
                                             in_=xT[:, t * TT:(t + 1) * TT].rearrange("(k p) n -> p k n", p=128)),
               writes=XBc, sem=XBSEM, inc=16)

    def mm_group(out_ap, pairs, reads, writes):
        def fn(E):
            n = len(pairs)
            inst = None
            for i, (l, r) in enumerate(pairs):
                inst = E.matmul(out_ap, lhsT=l, rhs=r, start=(i == 0), stop=(i == n - 1))
            return inst
        return P.emit("pe", fn, reads=reads, writes=writes)

    def c3(ap):
        return ap.rearrange("p (c t) -> p c t", t=64)

    def vcol(c):
        return V.ap[:, c:c + 1]

    def ln_phase1(src_c, dst_c, j, ps):
        P.emit("dve", lambda E: E.scalar_tensor_tensor(
            out=dst_c[j].ap, in0=src_c[j].ap, scalar=ALPHA, in1=ps.ap, op0=ALU.mult, op1=ALU.add),
            reads=[ps, src_c[j]], writes=[dst_c[j]])

    def ln_stat(j, zb, zs):
        P.emit("pe", lambda E: E.matmul(PSL[0].ap, lhsT=ONES_L.ap, rhs=zb.ap, start=(j == 0), stop=(j == 7)),
               reads=[zb, ONES_L], writes=[PSL[0]] if j == 0 else [])
        tok = P.emit("pe", lambda E: E.matmul(PSL[1].ap, lhsT=ONES_L.ap, rhs=zs.ap, start=(j == 0), stop=(j == 7)),
                     reads=[zs, ONES_L], writes=[PSL[1]] if j == 0 else [])
        if j == 7:
            PSL[0].w = tok
            PSL[1].w = tok

    def ln_phase2_chunk(z_c, j):
        zb, zs = ZB[j % 2], ZSQ[j % 2]
        P.emit("act", lambda E: E.activation(out=zb.ap, in_=z_c[j].ap, func=AF.Copy), reads=[z_c[j]], writes=[zb])
        P.emit("act", lambda E: E.activation(out=zs.ap, in_=z_c[j].ap, func=AF.Square), reads=[z_c[j]], writes=[zs])
        ln_stat(j, zb, zs)

    def ln_rstd():
        P.emit("act", lambda E: E.activation(out=M2.ap, in_=PSL[0].ap, func=AF.Square), reads=[PSL[0]], writes=[M2])
        P.emit("dve", lambda E: E.tensor_tensor(out=VAR.ap, in0=PSL[1].ap, in1=M2.ap, op=ALU.subtract),
               reads=[PSL[1], M2], writes=[VAR])
        P.emit("act", lambda E: E.activation(out=VAR.ap, in_=VAR.ap, func=AF.Ln, bias=EPSL.ap[:, 0:1]),
               reads=[VAR, EPSL], writes=[VAR])
        P.emit("act", lambda E: E.activation(out=PSL[1].ap, in_=VAR.ap, func=AF.Exp, scale=-0.5),
               reads=[VAR], writes=[PSL[1]])

    def ln_out_chunk(z_c, j, gcol, bcol, outs):
        lt = LT[j % 2]
        P.emit("dve", lambda E: E.tensor_tensor(out=lt.ap, in0=z_c[j].ap, in1=PSL[0].ap, op=ALU.subtract),
               reads=[z_c[j], PSL[0]], writes=[lt])
        P.emit("dve", lambda E: E.tensor_tensor(out=lt.ap, in0=lt.ap, in1=PSL[1].ap, op=ALU.mult),
               reads=[lt, PSL[1]], writes=[lt])
        for (ob, oap) in outs:
            P.emit("act", lambda E, oap=oap: E.activation(out=oap, in_=lt.ap, func=AF.Identity,
                                                          scale=vcol(gcol + j), bias=vcol(bcol + j)),
                   reads=[lt, V], writes=[ob])

    def ln_finish_gen(z_c, gcol, bcol, outs_fn, after_chunk=None):
        for j in range(8):
            ln_phase2_chunk(z_c, j)
            yield
        ln_rstd()
        yield
        for j in range(8):
            ln_out_chunk(z_c, j, gcol, bcol, outs_fn(j))
            if after_chunk is not None:
                after_chunk(j)
            yield

    def proj_v_block(tb, w3, rb):
        ps = next_ps()
        mm_group(ps.ap, [(XBc[k].ap[:, tb * 128:(tb + 1) * 128], w3[:, k, :]) for k in range(8)],
                 reads=XBc + [rb], writes=[ps])
        P.emit("act", lambda E: E.activation(out=VT.ap[:, tb, :], in_=ps.ap, func=AF.Copy), reads=[ps], writes=[VT])

    def pool_level(dst, src, lo, sh):
        P.emit("pool", lambda E: E.tensor_tensor(
            out=dst.ap[:, lo:16 + TT], in0=src.ap[:, lo:16 + TT], in1=src.ap[:, lo - sh:16 + TT - sh], op=ALU.add),
            reads=[src], writes=[dst])

    def scan_step(h, c):
        cur, nxt = (S32[h], S32B[h]) if c % 2 == 0 else (S32B[h], S32[h])
        P.emit("pool", lambda E: E.tensor_scalar(out=ST.ap[:, c, :], in0=cur.ap, scalar1=EM.ap[:, c:c + 1],
                                                 scalar2=0.0, op0=ALU.mult, op1=ALU.add),
               reads=[cur, EM], writes=[ST])
        P.emit("dve", lambda E: E.scalar_tensor_tensor(
            out=nxt.ap, in0=cur.ap, scalar=EL.ap[:, c:c + 1],
            in1=PSL[c // 4].ap[:, (c % 4) * 128:(c % 4 + 1) * 128], op0=ALU.mult, op1=ALU.add),
            reads=[cur, EL, PSL[c // 4]], writes=[nxt])

    def proj_group(w3, rb, j):
        ps = next_ps()
        mm_group(ps.ap, [(w3[:, k, j * 128:(j + 1) * 128], XBc[k].ap) for k in range(8)],
                 reads=XBc + [rb], writes=[ps])
        return ps

    def head_gen(t, h):
        rb = ring_take("in", 1 + h)
        w3 = rb.ap[:, 0:4096].rearrange("p (k n) -> p k n", n=512)
        wv = float(2 ** (h + 1))
        g = h
        ps_f = proj_group(w3, rb, 2)
        P.emit("act", lambda E: E.activation(out=SG.ap, in_=ps_f.ap, func=AF.Tanh, scale=0.5), reads=[ps_f], writes=[SG])
        yield
        ps_q = proj_group(w3, rb, 1)
        P.emit("act", lambda E: E.activation(out=QS.ap, in_=ps_q.ap, func=AF.Tanh, scale=0.5), reads=[ps_q], writes=[QS])
        P.emit("dve", lambda E: E.scalar_tensor_tensor(out=QS.ap, in0=QS.ap, scalar=1.0, in1=ps_q.ap,
                                                       op0=ALU.add, op1=ALU.mult), reads=[QS, ps_q], writes=[QS])
        yield
        ps_xp = proj_group(w3, rb, 0)
        P.emit("act", lambda E: E.activation(out=XPG.ap[:, 16:16 + TT], in_=ps_xp.ap, func=AF.Copy),
               reads=[ps_xp], writes=[XPG])
        yield
        ps_g = proj_group(w3, rb, 3)
        ring_release(rb)
        P.emit("act", lambda E: E.activation(out=GS.ap, in_=ps_g.ap, func=AF.Tanh, scale=0.5), reads=[ps_g], writes=[GS])
        P.emit("dve", lambda E: E.scalar_tensor_tensor(out=GS.ap, in0=GS.ap, scalar=1.0, in1=ps_g.ap,
                                                       op0=ALU.add, op1=ALU.mult), reads=[GS, ps_g], writes=[GS])
        yield

        P.emit("dve", lambda E: E.tensor_scalar(out=SG.ap, in0=SG.ap, scalar1=A1.ap[:, h:h + 1],
                                                scalar2=A2.ap[:, h:h + 1], op0=ALU.mult, op1=ALU.add),
               reads=[SG, A1, A2], writes=[SG])
        P.emit("pool", lambda E: E.tensor_scalar(out=KK.ap, in0=SG.ap, scalar1=-1.0, scalar2=1.0,
                                                 op0=ALU.mult, op1=ALU.add), reads=[SG], writes=[KK])
        yield
        P.emit("dve", lambda E: E.tensor_tensor_scan(out=BB.ap, data0=MSK.ap, data1=SG.ap, initial=0.0,
                                                     op0=ALU.max, op1=ALU.mult), reads=[MSK, SG], writes=[BB])
        P.emit("dve", lambda E: E.reciprocal(out=LF.ap, in_=BB.ap), reads=[BB], writes=[LF])
        yield
        P.emit("dve", lambda E: E.tensor_tensor(out=c3(EE.ap), in0=c3(BB.ap),
                                                in1=c3(LF.ap)[:, :, 31:32].to_broadcast([128, 8, 64]),
                                                op=ALU.mult), reads=[BB, LF], writes=[EE])
        P.emit("dve", lambda E: E.tensor_tensor(out=c3(EN.ap), in0=c3(LF.ap),
                                                in1=c3(BB.ap)[:, :, 31:32].to_broadcast([128, 8, 64]),
                                                op=ALU.mult), reads=[BB, LF], writes=[EN])
        yield
        P.emit("pool", lambda E: E.tensor_tensor(out=QT.ap, in0=QS.ap, in1=EE.ap, op=ALU.mult),
               reads=[QS, EE], writes=[QT])
        P.emit("dve", lambda E: E.tensor_tensor(out=KK.ap, in0=KK.ap, in1=EN.ap, op=ALU.mult),
               reads=[KK, EN], writes=[KK])
        P.emit("act", lambda E: E.activation(out=KTB.ap, in_=KK.ap, func=AF.Copy), reads=[KK], writes=[KTB])
        P.emit("dve", lambda E: E.tensor_tensor(out=c3(KH.ap), in0=c3(KK.ap),
                                                in1=c3(EE.ap)[:, :, 63:64].to_broadcast([128, 8, 64]),
                                                op=ALU.mult), reads=[KK, EE], writes=[KH])
        P.emit("pool", lambda E: E.tensor_copy(out=EM.ap, in_=c3(BB.ap)[:, :, 31]), reads=[BB], writes=[EM])
        P.emit("pool", lambda E: E.tensor_copy(out=EL.ap, in_=c3(BB.ap)[:, :, 63]), reads=[BB], writes=[EL])
        yield

        P.emit("pool", lambda E: E.tensor_copy(out=XPG.ap[:, 0:16], in_=PHALO[g].ap), reads=[PHALO[g]], writes=[XPG])
        src = XPG
        bufs = [PA, PB]
        sh = 1
        lo = 0
        for lvl in range(g + 1):
            dst = bufs[lvl % 2]
            lo = lo + sh
            pool_level(dst, src, lo, sh)
            src = dst
            sh *= 2
        ssum = src
        xw = PB if ssum is PA else PA
        P.emit("pool", lambda E: E.tensor_scalar(out=xw.ap[:, 16:16 + TT], in0=XPG.ap[:, 16:16 + TT], scalar1=-wv,
                                                 scalar2=0.0, op0=ALU.mult, op1=ALU.add), reads=[XPG], writes=[xw])
        P.emit("pool", lambda E: E.tensor_tensor(out=DT.ap, in0=ssum.ap[:, 16:16 + TT], in1=xw.ap[:, 16:16 + TT],
                                                 op=ALU.add), reads=[ssum, xw], writes=[DT])
        if t == 0:
            P.emit("pool", lambda E: E.tensor_tensor(out=PTMP.ap, in0=ssum.ap[:, 16:32], in1=WC.ap[:, g, :], op=ALU.mult),
                   reads=[ssum, WC], writes=[PTMP])
            P.emit("pool", lambda E: E.tensor_tensor(out=DT.ap[:, 0:16], in0=PTMP.ap, in1=xw.ap[:, 16:32], op=ALU.add),
                   reads=[PTMP, xw], writes=[DT])
        P.emit("pool", lambda E: E.tensor_copy(out=PHALO[g].ap, in_=XPG.ap[:, TT:TT + 16]), reads=[XPG], writes=[PHALO[g]])
        yield

        psa = next_ps()

        def fn_at(E):
            inst = None
            for tb in range(4):
                inst = E.matmul(psa.ap[:, tb * 128:(tb + 1) * 128], lhsT=KTB.ap[:, tb * 128:(tb + 1) * 128],
                                rhs=QT.ap[:, tb * 128:(tb + 1) * 128], start=True, stop=True)
            return inst
        P.emit("pe", fn_at, reads=[KTB, QT], writes=[psa])
        P.emit("dve", lambda E: E.tensor_tensor(
            out=AT.ap, in0=psa.ap.rearrange("p (b t) -> p b t", t=128),
            in1=CMASK.ap.unsqueeze(1).to_broadcast([128, 4, 128]), op=ALU.mult),
            reads=[psa, CMASK], writes=[AT])

        def fn_tr(E):
            inst = None
            for tb in range(4):
                inst = E.transpose(out=PST.ap[:, tb * 128:(tb + 1) * 128], in_=KH.ap[:, tb * 128:(tb + 1) * 128],
                                   identity=IDENT.ap)
            return inst
        P.emit("pe", fn_tr, reads=[KH, IDENT], writes=[PST])
        P.emit("act", lambda E: E.activation(out=KHE.ap[0:64].rearrange("p b d -> p (b d)"), in_=PST.ap[0:64, 0:512],
                                             func=AF.Copy), reads=[PST], writes=[KHE])
        P.emit("act", lambda E: E.activation(out=KHO.ap[64:128].rearrange("p b d -> p (b d)"),
                                             in_=PST.ap[64:128, 0:512], func=AF.Copy), reads=[PST], writes=[KHO])
        psy = next_ps()
        mm_group(psy.ap, [(POOLW.ap[:, g, :], DT.ap)], reads=[POOLW, DT], writes=[psy])
        P.emit("act", lambda E: E.activation(out=YT[g].ap, in_=psy.ap, func=AF.Identity, scale=PSC.ap[:, g:g + 1]),
               reads=[psy, PSC], writes=[YT[g]])
        yield

        def fn_ds(E):
            inst = None
            for c in range(8):
                tb = c // 2
                kh = KHE if c % 2 == 0 else KHO
                inst = E.matmul(PSL[c // 4].ap[:, (c % 4) * 128:(c % 4 + 1) * 128], lhsT=kh.ap[:, tb, :],
                                rhs=VT.ap[:, tb, h * 128:(h + 1) * 128], start=True, stop=True)
            return inst
        P.emit("pe", fn_ds, reads=[KHE, KHO, VT], writes=[PSL[0], PSL[1]])
        yield
        for c in range(8):
            scan_step(h, c)
            if c % 4 == 3:
                yield
        pso = next_ps()

        def fn_o(E):
            inst = None
            for tb in range(4):
                E.matmul(pso.ap[:, tb * 128:(tb + 1) * 128], lhsT=VT.ap[:, tb, h * 128:(h + 1) * 128],
                         rhs=AT.ap[:, tb, :], start=True, stop=False)
                for c in (2 * tb, 2 * tb + 1):
                    inst = E.matmul(pso.ap[:, c * 64:(c + 1) * 64], lhsT=ST.ap[:, c, :],
                                    rhs=QT.ap[:, c * 64:(c + 1) * 64], start=False, stop=(c == 2 * tb + 1))
            return inst
        P.emit("pe", fn_o, reads=[VT, AT, ST, QT], writes=[pso])
        P.emit("act", lambda E: E.activation(out=OSQ.ap, in_=pso.ap, func=AF.Square), reads=[pso], writes=[OSQ])
        P.emit("act", lambda E: E.activation(out=O32.ap, in_=pso.ap, func=AF.Copy), reads=[pso], writes=[O32])
        yield
        psm = next_ps()
        mm_group(psm.ap, [(ONES_R.ap, OSQ.ap)], reads=[ONES_R, OSQ], writes=[psm])
        P.emit("act", lambda E: E.activation(out=RS.ap, in_=psm.ap, func=AF.Ln, bias=EPSR.ap[:, 0:1]),
               reads=[psm, EPSR], writes=[RS])
        yield
        P.emit("act", lambda E: E.activation(out=RS.ap, in_=RS.ap, func=AF.Exp, scale=-0.5), reads=[RS], writes=[RS])
        yield
        P.emit("pool", lambda E: E.tensor_tensor(out=T2.ap, in0=O32.ap, in1=RS.ap, op=ALU.mult),
               reads=[O32, RS], writes=[T2])
        P.emit("dve", lambda E: E.scalar_tensor_tensor(out=YT[4 + h].ap, in0=T2.ap, scalar=GNH.ap[:, 0:1],
                                                       in1=GS.ap, op0=ALU.mult, op1=ALU.mult),
               reads=[T2, GNH, GS], writes=[YT[4 + h]])
        yield

    def x_to_bf16(t):
        xc = XFc[t % 2]
        P.emit("dve", lambda E: E.tensor_copy(out=XB_full[:, 0:4, :], in_=XF_full[t % 2][:, 0:4, :]),
               reads=xc[0:4], writes=XBc[0:4])
        P.emit("act", lambda E: E.activation(out=XB_full[:, 4:8, :], in_=XF_full[t % 2][:, 4:8, :], func=AF.Copy),
               reads=xc[4:8], writes=XBc[4:8])

    def mixer_gen(t):
        x_to_bf16(t)
        yield
        rb = ring_take("in", 0)
        w3 = rb.ap[:, 0:4096].rearrange("p (k n) -> p k n", n=512)
        for tb in range(4):
            proj_v_block(tb, w3, rb)
            if tb == 3:
                ring_release(rb)
            yield
        for h in range(4):
            yield from head_gen(t, h)
            if t == 0 and h < 2:
                issue_late_casts(15)

    def mix_gen(t):
        xc = XFc[t % 2]
        for b in range(2):
            rb = ring_take("out", b)
            w3 = rb.ap[:, 0:4096].rearrange("p (k n) -> p k n", n=512)
            for jj in range(4):
                j = 4 * b + jj
                ps = next_ps()
                mm_group(ps.ap, [(w3[:, k, jj * 128:(jj + 1) * 128], YT[k].ap) for k in range(8)],
                         reads=[rb] + YT, writes=[ps])
                if jj == 3:
                    ring_release(rb)
                ln_phase1(xc, xc, j, ps)
                yield

    def halo_prep(t):
        hs = HS[(t - 1) % 2]
        w1 = V.ap[:, C_CW1:C_CW1 + 2 * NFC]
        w0 = V.ap[:, C_CW0:C_CW0 + 2 * NFC]
        P.emit("dve", lambda E: E.tensor_tensor(out=HW.ap[:, :, 0], in0=hs.ap[:, :, 1], in1=w1, op=ALU.mult),
               reads=[hs, V], writes=[HW])
        P.emit("dve", lambda E: E.tensor_tensor(out=HTMP.ap, in0=hs.ap[:, :, 0], in1=w0, op=ALU.mult),
               reads=[hs, V], writes=[HTMP])
        P.emit("dve", lambda E: E.tensor_tensor(out=HW.ap[:, :, 0], in0=HW.ap[:, :, 0], in1=HTMP.ap, op=ALU.add),
               reads=[HW, HTMP], writes=[HW])
        P.emit("dve", lambda E: E.tensor_tensor(out=HW.ap[:, :, 1], in0=hs.ap[:, :, 1], in1=w0, op=ALU.mult),
               reads=[hs, V], writes=[HW])

    def conv_act(t, ps, acc, ci):
        P.emit("act", lambda E: E.activation(out=acc.ap, in_=ps.ap, func=AF.Identity, scale=vcol(C_CW2 + ci),
                                             bias=vcol(C_CB + ci)), reads=[ps, V], writes=[acc])

    def conv_save(t, ps, ci):
        if t + 1 < NT:
            hs = HS[t % 2]
            P.emit("act", lambda E: E.activation(out=hs.ap[:, ci, :], in_=ps.ap[:, TT - 2:TT], func=AF.Copy),
                   reads=[ps], writes=[hs])

    def conv_dve(t, ps, acc, ci):
        P.emit("dve", lambda E: E.scalar_tensor_tensor(
            out=acc.ap[:, 1:TT], in0=ps.ap[:, 0:TT - 1], scalar=vcol(C_CW1 + ci),
            in1=acc.ap[:, 1:TT], op0=ALU.mult, op1=ALU.add), reads=[ps, acc, V], writes=[acc])
        P.emit("dve", lambda E: E.scalar_tensor_tensor(
            out=acc.ap[:, 2:TT], in0=ps.ap[:, 0:TT - 2], scalar=vcol(C_CW0 + ci),
            in1=acc.ap[:, 2:TT], op0=ALU.mult, op1=ALU.add), reads=[ps, acc, V], writes=[acc])

    def gate(j):
        aca, acv, ga = ACA[j % 2], ACV[j % 2], GA[j % 2]
        P.emit("act", lambda E: E.activation(out=ga.ap, in_=aca.ap, func=AF.Gelu_apprx_tanh), reads=[aca], writes=[ga])
        P.emit("pool", lambda E: E.tensor_tensor(out=GT[j].ap, in0=ga.ap, in1=acv.ap, op=ALU.mult),
               reads=[ga, acv], writes=[GT[j]])

    def ffn_pair(t, j, jj, w3, rb, last, prev_gate):
        hb = H1Bc[t % 2]
        psa = next_ps()
        mm_group(psa.ap, [(w3[:, k, jj * 128:(jj + 1) * 128], hb[k].ap) for k in range(8)],
                 reads=hb + [rb], writes=[psa])
        psv = next_ps()
        mm_group(psv.ap, [(w3[:, k, 256 + jj * 128:256 + (jj + 1) * 128], hb[k].ap) for k in range(8)],
                 reads=hb + [rb], writes=[psv])
        if last:
            ring_release(rb)
        aca, acv = ACA[j % 2], ACV[j % 2]
        conv_save(t, psa, j)
        conv_act(t, psa, aca, j)
        conv_save(t, psv, NFC + j)
        conv_act(t, psv, acv, NFC + j)
        if prev_gate is not None:
            gate(prev_gate)
        conv_dve(t, psa, aca, j)
        conv_dve(t, psv, acv, NFC + j)
        if t > 0:
            accf = ACCF[j % 2]
            hw2 = HW.ap.rearrange("p (a n) c -> p a n c", a=2)[:, :, j, :]
            P.emit("dve", lambda E: E.tensor_tensor(out=accf[:, :, 0:2], in0=accf[:, :, 0:2], in1=hw2, op=ALU.add),
                   reads=[aca, acv, HW], writes=[aca, acv])

    def ffn_up_gen(t):
        if t > 0:
            halo_prep(t)
        prev = None
        for b in range(11):
            rb = ring_take("up", b)
            w3 = rb.ap[:, 0:4096].rearrange("p (k n) -> p k n", n=512)
            for jj in range(2):
                j = 2 * b + jj
                ffn_pair(t, j, jj, w3, rb, jj == 1, prev)
                prev = j
                yield
        gate(prev)
        yield

    def ffn_down_gen(t):
        zc = H1Fc[t % 2]
        for j in range(8):
            rb = ring_take("dn", j)
            w3 = rb.ap[:, 0:2816].rearrange("p (k n) -> p k n", n=128)
            ps = next_ps()
            mm_group(ps.ap, [(w3[:, k, :], GT[k].ap) for k in range(NFC)], reads=[rb] + GT, writes=[ps])
            ring_release(rb)
            ln_phase1(zc, zc, j, ps)
            yield

    def ln1_finish_gen(t):
        zc = XFc[t % 2]
        hf, hb = H1Fc[t % 2], H1Bc[t % 2]
        yield from ln_finish_gen(zc, C_L1G, C_L1B, lambda j: [(hf[j], hf[j].ap), (hb[j], hb[j].ap)])
        if DEBUG and t < 2:
            P.emit("sp", lambda E: E.dma_start(out=dbg[t].rearrange("(k p) n -> p k n", p=128), in_=H1F_full[t % 2]),
                   reads=hf, sem=DSEM, inc=16)
        if t + 2 < NT:
            x_load(t + 2)
        yield

    def ln2_finish_gen(t):
        zc = H1Fc[t % 2]
        yield from ln_finish_gen(zc, C_L2G, C_L2B, lambda j: [(zc[j], zc[j].ap)])
        b_ = t % 2
        P.emit("sp", lambda E: E.dma_start(
            out=outT[:, t * TT:(t + 1) * TT].rearrange("(k p) n -> p k n", p=128), in_=H1F_full[b_]),
            reads=zc, sem=OSEM[b_], inc=16)
        yield

    def drain(g):
        for _ in g:
            pass

    def chain(*gens):
        for g in gens:
            yield from g

    def step(g, n=1):
        for _ in range(n):
            try:
                next(g)
            except StopIteration:
                return False
        return True

    def interleave(main, side, k):
        alive = side is not None
        for _ in main:
            if alive:
                alive = step(side, k)
        if alive:
            drain(side)

    ring_init()
    x_load(0)
    x_load(1)

    drain(mixer_gen(0))
    issue_late_casts(100)
    drain(mix_gen(0))
    drain(ln1_finish_gen(0))

    def interleave_keep(main, side, k):
        alive = side is not None
        for _ in main:
            if alive:
                alive = step(side, k)
        return alive

    for t in range(NT):
        sa = []
        if t >= 1:
            sa.append(ln2_finish_gen(t - 1))
        if t + 1 < NT:
            sa += [mixer_gen(t + 1), mix_gen(t + 1)]
        side_a = chain(*sa) if sa else None
        alive = interleave_keep(ffn_up_gen(t), side_a, SIDE_K)
        rest = [side_a] if alive else []
        if t + 1 < NT:
            rest.append(ln1_finish_gen(t + 1))
        interleave(ffn_down_gen(t), chain(*rest) if rest else None, SIDE_K2)
    drain(ln2_finish_gen(NT - 1))

    P.wait_only("sp", [(OSEM[i], OSEM[i].count) for i in range(2)] + ([(DSEM, DSEM.count)] if DEBUG else []))

    if record:
        return take_log

    with nc.Block() as block:
        for ename, attr in ENG_ATTR.items():
            ops = P.ops[ename]

            def body(E, ops=ops):
                for f in ops:
                    f(E)
            getattr(block, attr)(body)
    return nc


def build():
    order = _emit_program(bass.Bass("TRN2", target_bir_lowering=False), None)
    return _emit_program(bass.Bass("TRN2", target_bir_lowering=False), order)


_NC_CACHE = {}


def _pack_vecs(pool_scale, hgrn_lb, hgrn_gnorm, ln1_g, ln1_b, ln2_g, ln2_b, conv_w, conv_b):
    cols = []
    cols.append(np.asarray(pool_scale)[0].reshape(4, 128).T)
    cols.append(np.asarray(hgrn_lb)[0].reshape(4, 128).T)
    cols.append(np.asarray(hgrn_lb)[1].reshape(4, 128).T)
    cols.append(np.asarray(hgrn_gnorm)[0].reshape(1, 128).T)
    for v in (ln1_g, ln1_b, ln2_g, ln2_b):
        cols.append(np.asarray(v)[0].reshape(8, 128).T)
    cw = np.asarray(conv_w)[0]
    for j in range(3):
        cols.append(cw[j].reshape(2 * NFC, 128).T)
    cols.append(np.asarray(conv_b)[0].reshape(2 * NFC, 128).T)
    v = np.ascontiguousarray(np.concatenate(cols, axis=1).astype(np.float32))
    assert v.shape == (128, NV), v.shape
    return v


def kernel(x, w_in, pool_w, pool_scale, hgrn_lb, hgrn_gnorm, w_out, ln1_g, ln1_b,
           w_up, conv_w, conv_b, w_down, ln2_g, ln2_b):
    x = np.asarray(x, dtype=np.float32)
    if "nc" not in _NC_CACHE:
        _NC_CACHE["nc"] = build()
    nc = _NC_CACHE["nc"]
    vecs = _pack_vecs(pool_scale, hgrn_lb, hgrn_gnorm, ln1_g, ln1_b, ln2_g, ln2_b, conv_w, conv_b)
    shared = {
        "w_in": np.ascontiguousarray(np.asarray(w_in, np.float32)[0]),
        "pool_w": np.ascontiguousarray(np.asarray(pool_w, np.float32)[0]),
        "w_out": np.ascontiguousarray(np.asarray(w_out, np.float32)[0]),
        "w_up": np.ascontiguousarray(np.asarray(w_up, np.float32)[0]),
        "w_down": np.ascontiguousarray(np.asarray(w_down, np.float32)[0]),
        "vecs": vecs,
    }
    in_maps = []
    for b in range(8):
        m = dict(shared)
        m["xT"] = np.ascontiguousarray(x[b].T)
        in_maps.append(m)
    res = run_bass_kernel_spmd(nc, in_maps, core_ids=list(range(8)))
    out = np.empty((8, T, D), dtype=np.float32)
    for b in range(8):
        out[b] = res.results[b]["outT"].T
    return out
```
